# Optimizing a Trainium2 kernel written in Bass

```python
import jax, jax.numpy as jnp
from jax import lax
import numpy as np

D_MODEL = 2048
BATCH = 4
SEQ = 8192
DEPTH = 1
DEC_BATCH = 16
DEC_SEQ = 32
PAST_LEN = 2048

CHUNK = 64
CONV_WIDTH = 3
D_CONV = D_MODEL // 2
D_GLA = D_MODEL - D_CONV
GLA_HEADS = 4
GLA_DK = D_GLA // 2
HEAD_K = GLA_DK // GLA_HEADS
HEAD_V = D_GLA // GLA_HEADS
GATE_RANK = 16
GATE_TAU = 16.0
D_FF = -(-8 * D_MODEL // (3 * 256)) * 256
SPLIT_SIZES = (D_CONV, D_CONV, D_CONV, GLA_DK, GLA_DK, D_GLA, D_GLA, GATE_RANK)
D_IN = sum(SPLIT_SIZES)
SPLIT_POINTS = tuple(int(s) for s in np.cumsum(SPLIT_SIZES)[:-1])
ALPHA = (2.0 * DEPTH) ** 0.25
BETA = (8.0 * DEPTH) ** -0.25
LN_EPS = 1e-5
RMS_EPS = 1e-6

kernel_name = 'hybrid_conv_gla_stream'


def _layernorm(x, g, b):
    xf = x.astype(jnp.float32)
    mu = jnp.mean(xf, axis=-1, keepdims=True)
    var = jnp.mean(jnp.square(xf - mu), axis=-1, keepdims=True)
    return ((xf - mu) * lax.rsqrt(var + LN_EPS) * g + b).astype(x.dtype)


def _to_blocks(a, n, L):
    Bn, _, H, d = a.shape
    return a.reshape(Bn, n, L, H, d).transpose(1, 0, 3, 2, 4)


def _gla(q, k, v, logf, s0, L):
    Bn, T, H, _ = q.shape
    n = T // L
    mask = jnp.tril(jnp.ones((L, L), dtype=bool))[None, None, :, :, None]

    def step(S, inp):
        qc, kc, vc, fc = inp
        b = jnp.cumsum(fc, axis=2)
        diff = b[:, :, :, None, :] - b[:, :, None, :, :]
        decay = jnp.exp(jnp.where(mask, diff, -jnp.inf))
        A = jnp.einsum('bhid,bhjd,bhijd->bhij', qc, kc, decay)
        o = (jnp.einsum('bhij,bhjv->bhiv', A, vc)
             + jnp.einsum('bhid,bhdv->bhiv', qc * jnp.exp(b), S))
        bL = b[:, :, -1:, :]
        S = (jnp.exp(bL[:, :, 0, :])[..., None] * S
             + jnp.einsum('bhjd,bhjv->bhdv', kc * jnp.exp(bL - b), vc))
        return S, o

    f32 = jnp.float32
    xs = (_to_blocks(q.astype(f32), n, L), _to_blocks(k.astype(f32), n, L),
          _to_blocks(v.astype(f32), n, L), _to_blocks(logf, n, L))
    S, o = lax.scan(step, s0.astype(f32), xs)
    o = o.transpose(1, 0, 3, 2, 4).reshape(Bn, T, H, v.shape[-1])
    return o, S


def _layer(x, c, conv_state, gla_state, L, w_ada, b_ada, w_in, w_conv, w_f, b_f,
           gla_gain, w_o, ln1_g, ln1_b, w_gate, w_up, w_down, ln2_g, ln2_b):
    Bn, T, _ = x.shape
    mod = jax.nn.silu(c) @ w_ada + b_ada
    sh1, sc1, g1, sh2, sc2, g2 = jnp.split(mod[:, None, :], 6, axis=-1)
    h = x * (1 + sc1) + sh1
    p = h @ w_in
    hb, hc, hin, q, k, v, go, flr = jnp.split(p, SPLIT_POINTS, axis=-1)

    u = hc * hin
    upad = jnp.concatenate([conv_state.astype(u.dtype), u], axis=1)
    conv = upad[:, 0:T] * w_conv[0]
    for i in range(1, CONV_WIDTH):
        conv = conv + upad[:, i:i + T] * w_conv[i]
    y_conv = hb * conv
    new_conv = upad[:, -(CONV_WIDTH - 1):]

    logf = jax.nn.log_sigmoid((flr @ w_f + b_f).astype(jnp.float32)) / GATE_TAU
    qh = q.reshape(Bn, T, GLA_HEADS, HEAD_K) * (HEAD_K ** -0.5)
    kh = k.reshape(Bn, T, GLA_HEADS, HEAD_K)
    vh = v.reshape(Bn, T, GLA_HEADS, HEAD_V)
    fh = logf.reshape(Bn, T, GLA_HEADS, HEAD_K)
    o, s_new = _gla(qh, kh, vh, fh, gla_state, L)
    o = o * lax.rsqrt(jnp.mean(jnp.square(o), axis=-1, keepdims=True) + RMS_EPS) * gla_gain
    o = o.reshape(Bn, T, D_GLA).astype(x.dtype) * jax.nn.silu(go)

    mix = jnp.concatenate([y_conv, o], axis=-1) @ w_o
    x1 = _layernorm(ALPHA * x + g1 * mix, ln1_g, ln1_b)

    h2 = x1 * (1 + sc2) + sh2
    f = (jax.nn.silu(h2 @ w_gate) * (h2 @ w_up)) @ w_down
    x2 = _layernorm(ALPHA * x1 + g2 * f, ln2_g, ln2_b)
    return x2, new_conv, s_new


def setup_inputs(seed: int = 0) -> dict:
    key = jax.random.key(seed)
    ks = iter(jax.random.split(key, 40))
    f32 = jnp.float32

    def nrm(shape, scale):
        return jax.random.normal(next(ks), shape, f32) * scale

    Ds = D_MODEL ** -0.5
    in_scales = (Ds, Ds, Ds * BETA, Ds, Ds, Ds * BETA, Ds, Ds)
    w_in = jnp.concatenate([nrm((DEPTH, D_MODEL, s), sc) for s, sc in zip(SPLIT_SIZES, in_scales)], axis=-1)
    return {
        'x_prompt': nrm((BATCH, SEQ, D_MODEL), 1.0),
        'x_sample': nrm((DEC_BATCH, DEC_SEQ, D_MODEL), 1.0),
        'c_prompt': nrm((BATCH, D_MODEL), 1.0),
        'c_sample': nrm((DEC_BATCH, D_MODEL), 1.0),
        'cache_conv': nrm((DEPTH, DEC_BATCH, CONV_WIDTH - 1, D_CONV), 0.5),
        'state_gla': nrm((DEPTH, DEC_BATCH, GLA_HEADS, HEAD_K, HEAD_V), HEAD_K ** -0.5),
        'w_ada': nrm((DEPTH, D_MODEL, 6 * D_MODEL), Ds),
        'b_ada': nrm((DEPTH, 6 * D_MODEL), 0.01),
        'w_in': w_in,
        'w_conv': nrm((DEPTH, CONV_WIDTH, D_CONV), CONV_WIDTH ** -0.5),
        'w_f': nrm((DEPTH, GATE_RANK, GLA_DK), GATE_RANK ** -0.5),
        'b_f': nrm((DEPTH, GLA_DK), 0.1),
        'gla_gain': 1.0 + nrm((DEPTH, HEAD_V), 0.01),
        'w_o': nrm((DEPTH, D_MODEL, D_MODEL), Ds * BETA),
        'ln1_g': 1.0 + nrm((DEPTH, D_MODEL), 0.01),
        'ln1_b': nrm((DEPTH, D_MODEL), 0.01),
        'w_gate': nrm((DEPTH, D_MODEL, D_FF), Ds * BETA),
        'w_up': nrm((DEPTH, D_MODEL, D_FF), Ds * BETA),
        'w_down': nrm((DEPTH, D_FF, D_MODEL), D_FF ** -0.5 * BETA),
        'ln2_g': 1.0 + nrm((DEPTH, D_MODEL), 0.01),
        'ln2_b': nrm((DEPTH, D_MODEL), 0.01),
    }


def reference(x_prompt, x_sample, c_prompt, c_sample, cache_conv, state_gla,
              w_ada, b_ada, w_in, w_conv, w_f, b_f, gla_gain, w_o,
              ln1_g, ln1_b, w_gate, w_up, w_down, ln2_g, ln2_b):
    yp, ys = x_prompt, x_sample
    bp = x_prompt.shape[0]
    conv_p, gla_p, conv_s, gla_s = [], [], [], []
    for l in range(DEPTH):
        wl = (w_ada[l], b_ada[l], w_in[l], w_conv[l], w_f[l], b_f[l], gla_gain[l], w_o[l],
              ln1_g[l], ln1_b[l], w_gate[l], w_up[l], w_down[l], ln2_g[l], ln2_b[l])
        conv0 = jnp.zeros((bp, CONV_WIDTH - 1, D_CONV), yp.dtype)
        gla0 = jnp.zeros((bp, GLA_HEADS, HEAD_K, HEAD_V), jnp.float32)
        yp, cp, sp = _layer(yp, c_prompt, conv0, gla0, CHUNK, *wl)
        ys, cs, ss = _layer(ys, c_sample, cache_conv[l], state_gla[l], ys.shape[1], *wl)
        conv_p.append(cp)
        gla_p.append(sp)
        conv_s.append(cs)
        gla_s.append(ss)
    return (yp, ys,
            jnp.stack(conv_p).astype(cache_conv.dtype),
            jnp.stack(gla_p).astype(state_gla.dtype),
            jnp.stack(conv_s).astype(cache_conv.dtype),
            jnp.stack(gla_s).astype(state_gla.dtype))
```

```python
from contextlib import ExitStack
import numpy as np
import concourse.bass as bass
import concourse.mybir as mybir
from concourse.bass_utils import run_bass_kernel_spmd

F32 = mybir.dt.float32
BF16 = mybir.dt.bfloat16
AF = mybir.ActivationFunctionType
ALU = mybir.AluOpType

D = 2048
KC = 16
DFF = 5632
FC = 44
ALPHA = 2.0 ** 0.25
LN_EPS = 1e-5
RMS_EPS = 1e-6
QSCALE_LN = float(np.log(128.0 ** -0.5))
C_HB, C_HC, C_HIN, C_Q, C_K, C_V, C_GO, C_FLR = 0, 1024, 2048, 3072, 3584, 4096, 5120, 6144
NWB = 4
WB_ELEMS = 4096

PV_BADA = 0
PV_LN1G = 96
PV_LN1B = 112
PV_LN2G = 128
PV_LN2B = 144
PV_WC = 160
PV_GAIN = 184
PV_FLAG = 186
PV_N = 187
CS_ID, CS_MAP, CS_MCP, CS_MAS, CS_MCS, CS_N = 0, 128, 256, 384, 448, 512

ENGS = ("pe", "act", "dve", "pool", "sp")
HP_ORDER = (0, 1)
FULL_BARRIER = False
EXP_SKIP = False
FORCE_SIGNAL = False
USE_WCACHE = True
N_PRECONV = 40


class Buf:
    __slots__ = ("t", "lw", "rd", "name", "psum")

    def __init__(self, t, name="", psum=False):
        self.psum = psum
        self.t = t
        self.lw = None
        self.rd = {}
        self.name = name


class Sched:
    def __init__(self, nc, es):
        self.nc = nc
        self.es = es
        self.q = {e: [] for e in ENGS}
        self.cnt = {e: 0 for e in ENGS}
        self.pending = {e: False for e in ENGS}
        self.sems = {}
        self.waited = {e: {} for e in ENGS}
        self.dma_cnt = {}
        self.out_tokens = []

    def sem(self, key):
        if key not in self.sems:
            self.sems[key] = self.es.enter_context(self.nc.semaphore(key))
        return self.sems[key]

    def _deps(self, eng, reads, writes, extra):
        deps = {}

        def add(tok):
            if tok is None:
                return
            k, v = tok
            if deps.get(k, 0) < v:
                deps[k] = v

        for b in reads:
            add(b.lw)
            if b.psum:
                for k, v in b.rd.items():
                    if k != "e_" + eng:
                        add((k, v))
        for b in writes:
            add(b.lw)
            for k, v in b.rd.items():
                add((k, v))
        for t in extra:
            add(t)
        waits = []
        for k, v in deps.items():
            if k == "e_pe" and eng == "pe":
                continue
            if self.waited[eng].get(k, 0) >= v:
                continue
            self.waited[eng][k] = v
            waits.append((k, v))
        return waits

    def _register(self, tok, reads, writes):
        k, v = tok
        for b in reads:
            if b.rd.get(k, 0) < v:
                b.rd[k] = v
        for b in writes:
            b.lw = tok
            b.rd = {}

    def op(self, eng, fn, reads=(), writes=(), signal=True, extra=()):
        if FORCE_SIGNAL:
            signal = True
        waits = self._deps(eng, reads, writes, extra)
        k = "e_" + eng
        if signal:
            self.cnt[eng] += 1
            self.pending[eng] = False
            tok = (k, self.cnt[eng])
        else:
            self.pending[eng] = True
            tok = (k, self.cnt[eng] + 1)
        self.q[eng].append((waits, fn, (k, 1) if signal else None))
        self._register(tok, reads, writes)
        return tok

    def dma(self, eng, semkey, fn, reads=(), writes=(), extra=()):
        waits = self._deps(eng, reads, writes, extra)
        n = self.dma_cnt.get(semkey, 0) + 1
        self.dma_cnt[semkey] = n
        tok = (semkey, 16 * n)
        self.q[eng].append((waits, fn, (semkey, 16)))
        self._register(tok, reads, writes)
        return tok

    def wait_all(self, eng, toks):
        waits = self._deps(eng, (), (), toks)
        self.q[eng].append((waits, None, None))

    def replay(self, eng, h):
        for waits, fn, inc in self.q[eng]:
            for k, v in waits:
                h.wait_ge(self.sem(k), v)
            if fn is None:
                continue
            ins = fn(h)
            if inc is not None:
                ins.then_inc(self.sem(inc[0]), inc[1])


class TileDesc:
    pass


def build(n_main=8, n_pre=8, do_sample=True, dbg=False):
    TWP = 512
    nc = bass.Bass("TRN2", target_bir_lowering=False)
    NM = n_main * TWP
    NP = n_pre * TWP

    def din(name, shape, dt=F32):
        return nc.dram_tensor(name, list(shape), dt, kind="ExternalInput").ap()

    def dout(name, shape):
        return nc.dram_tensor(name, list(shape), F32, kind="ExternalOutput").ap()

    xm = din("xm", [max(NM, 1), D])
    xp = din("xp", [max(NP, 1), D])
    xs = din("xs", [64, D])
    cvec = din("cvec", [128, 48])
    pvec = din("pvec", [128, PV_N])
    consts = din("consts", [128, CS_N])
    wf_d = din("wf", [16, 512])
    bf_d = din("bf", [1, 512])
    cc_d = din("cc", [128, 32])
    sg_d = din("sgla", [2, 4, 128, 256])
    w_ada = din("w_ada", [D, 6 * D])
    w_in = din("w_in", [D, 6160])
    w_o = din("w_o", [D, D])
    w_gate = din("w_gate", [D, DFF])
    w_up = din("w_up", [D, DFF])
    w_down = din("w_down", [DFF, D])

    yp = dout("yp", [max(NM, 1), D])
    ys = dout("ys", [64, D])
    cvp = dout("cvp", [128, 16])
    glp = dout("glp", [4, 128, 256])
    cvs = dout("cvs", [128, 32])
    gls = dout("gls", [2, 4, 128, 256])
    if dbg:
        dbgmix = nc.dram_tensor("dbgmix", [16, 128, 512], BF16, kind="ExternalOutput").ap()
        dbgh = nc.dram_tensor("dbgh", [16, 128, 512], BF16, kind="ExternalOutput").ap()

    with ExitStack() as es:
        S = Sched(nc, es)

        def sbt(name, shape, dt):
            return es.enter_context(nc.sbuf_tensor(name, list(shape), dt))

        def sb(name, shape, dt):
            t = sbt(name, shape, dt)
            return Buf(t[tuple(slice(None) for _ in shape)], name)

        WB = [sb(f"wb{i}", [128, WB_ELEMS], BF16) for i in range(NWB)]
        XIO = [sb(f"xio{i}", [128, D], F32) for i in range(2)]
        XL = XIO
        OS = XIO
        XPF = sb("xpf", [128, D], F32)
        xpref = {}
        H = [sb(f"h{k}", [128, 512], BF16) for k in range(KC)]
        R = [sb(f"r{k}", [128, 512], F32) for k in range(KC)]
        MIX = [sb(f"mix{k}", [128, 512], BF16) for k in range(KC)]
        ARENA = sbt("arena", [128, FC * 512], BF16)

        def carve(name, off, n, dt=BF16, pat=None, **kw):
            ap = ARENA[:, off:off + n]
            if dt == F32:
                ap = ap.bitcast(F32)
            if pat is not None:
                ap = ap.rearrange(pat, **kw)
            return Buf(ap, name)

        ACTT = [carve(f"actt{j}", j * 512, 512) for j in range(FC)]
        EQ = carve("eq", 0, 4096, F32, "p (h n) -> p h n", h=4)
        EK = carve("ek", 4096, 4096, F32, "p (h n) -> p h n", h=4)
        QT = carve("qt", 8192, 2048, BF16, "p (h n) -> p h n", h=4)
        KT = carve("kt", 10240, 2048, BF16, "p (h n) -> p h n", h=4)
        VT = carve("vt", 12288, 4096, BF16, "p (s n) -> p s n", s=4)
        SG = carve("sg", 16384, 4096, BF16, "p (m n) -> p m n", m=8)
        FLRB = carve("flrb", 20480, 512)
        AM = [carve(f"am{i}", 20992 + i * 256, 256, BF16, "p (h n) -> p h n", h=2) for i in range(2)]
        KTOK = [carve(f"ktok{i}", 21504 + i * 256, 256) for i in range(2)]
        GLA_TMPS = [EQ, EK, QT, KT, VT, SG, FLRB] + AM + KTOK
        TMP32 = [sb(f"tmp32_{i}", [128, 512], F32) for i in range(4)]
        TMP16 = [sb(f"tmp16_{i}", [128, 512], BF16) for i in range(4)]
        E1 = LL = TN = SGT = HCS = CV = TMP32
        SQ = SQB = XB = TMP16
        DEC = sb("dec", [128, 8, 4], F32)
        RINV = [sb(f"rinv{i}", [128, 256], F32) for i in range(2)]
        UB = [sb(f"ub{i}", [128, 520], F32) for i in range(2)]
        RSTD = sb("rstd", [128, 512], F32)
        NMR = sb("nmr", [128, 512], F32)
        MSQ = NMR
        S32 = [sb(f"s32_{i}", [128, 1024], F32) for i in range(2)]
        SBF = [[sb(f"sbf_{i}_{hp}", [128, 512], BF16) for hp in range(2)] for i in range(2)]
        UH = [sb(f"uh{i}", [128, 8, 2], F32) for i in range(2)]
        CST = sb("cst", [128, CS_N], F32)
        IDB = sb("idb", [128, 128], BF16)
        ONESB = sb("onesb", [128, 128], BF16)
        PV = sb("pv", [128, PV_N], F32)
        CVEC = sb("cvecs", [128, 48], F32)
        SCB = sb("scb", [128, 48], BF16)
        WFB = sb("wfb", [16, 512], BF16)
        BFB = sb("bfb", [1, 512], BF16)
        MOD = sb("mod", [128, 3, 96], F32)
        SC1P = sb("sc1p", [128, 3, 16], F32)
        SC2P = sb("sc2p", [128, 3, 16], F32)
        GA2 = sb("ga2", [128, 3, 16], F32)
        BA2 = sb("ba2", [128, 3, 16], F32)
        A1 = sb("a1", [128, 16], F32)
        B1 = sb("b1", [128, 16], F32)

        def alias_barrier(src, dst):
            toks = {}
            for b in src:
                for tk in ([b.lw] if b.lw else []) + list(b.rd.items()):
                    if toks.get(tk[0], 0) < tk[1]:
                        toks[tk[0]] = tk[1]
            for b in dst:
                for k, v in toks.items():
                    if b.rd.get(k, 0) < v:
                        b.rd[k] = v

        def ps(name):
            t = es.enter_context(nc.psum_tensor(name, [128, 512], F32))
            return Buf(t[:, :], name, psum=True)

        P = [ps(f"p{i}") for i in range(8)]
        rr = {"PR": 0, "PG": 0, "ALL": 0}

        def bank(pool):
            live = modst["live"]
            if pool == "PR":
                b = P[rr["PR"] % 4]
            elif pool == "PG":
                b = P[4 + rr["PG"] % (3 if live else 4)]
            else:
                b = P[rr["ALL"] % (7 if live else 8)]
            rr[pool] += 1
            return b

        ident = lambda n: CST.t[:n, CS_ID:CS_ID + n]

        S.dma("sp", "d_cst", lambda e: e.dma_start(out=CST.t[:, :], in_=consts[:, :]), writes=[CST])
        S.dma("sp", "d_pv", lambda e: e.dma_start(out=PV.t[:, :], in_=pvec[:, :]), writes=[PV])
        S.dma("sp", "d_cv", lambda e: e.dma_start(out=CVEC.t[:, :], in_=cvec[:, :]), writes=[CVEC])
        S.dma("pool", "d_wf", lambda e: e.dma_start(out=WFB.t[:, :], in_=wf_d[:, :]), writes=[WFB])
        S.dma("pool", "d_bf", lambda e: e.dma_start(out=BFB.t[:, :], in_=bf_d[:, :]), writes=[BFB])
        S.op("act", lambda e: e.activation(out=IDB.t[:, :], in_=CST.t[:, CS_ID:CS_ID + 128], func=AF.Copy),
             reads=[CST], writes=[IDB])
        S.op("dve", lambda e: e.memset(ONESB.t[:, :], 1.0), writes=[ONESB])
        S.op("act", lambda e: e.activation(out=SCB.t[:, :], in_=CVEC.t[:, :], func=AF.Silu), reads=[CVEC], writes=[SCB])

        wstate = {"i": 0}
        NCACHE = 128
        wcache = nc.dram_tensor("wcache", [NCACHE, 128, WB_ELEMS], BF16, kind="Internal").ap()
        cache_ix = {}
        WNAMES = {id(w_in): "in", id(w_o): "o", id(w_gate): "g", id(w_up): "u", id(w_down): "d", id(w_ada): "ada"}

        def wblock(W, k0, nk, c0, ncols):
            si = wstate["i"] % NWB
            slot = WB[si]
            key = f"d_wb{si}"
            wstate["i"] += 1
            assert nk * ncols <= WB_ELEMS
            n = nk * ncols
            ck = (WNAMES[id(W)], k0, nk, c0, ncols)
            if USE_WCACHE and ck in cache_ix:
                ci, cbuf = cache_ix[ck]
                S.dma("pool", key, lambda e: e.dma_start(out=slot.t[:, 0:n], in_=wcache[ci, :, 0:n]),
                      reads=[cbuf], writes=[slot])
                return slot
            src = W[k0 * 128:(k0 + nk) * 128, c0:c0 + ncols].rearrange("(k p) c -> p k c", p=128)
            dst = slot.t[:, 0:n].rearrange("p (k c) -> p k c", k=nk)
            S.dma("pool", key, lambda e: e.dma_start(out=dst, in_=src), writes=[slot])
            wstate["new"] = wstate.get("new", 0) + 1
            if (USE_WCACHE and WNAMES[id(W)] != "ada" and len(cache_ix) < NCACHE
                    and wstate["new"] % wstate.get("every", 1) == 0):
                ci = len(cache_ix)
                cbuf = Buf(None, f"wc{ci}")
                cache_ix[ck] = (ci, cbuf)
                S.dma("sp", f"d_ws{si}", lambda e: e.dma_start(out=wcache[ci, :, 0:n], in_=slot.t[:, 0:n]),
                      reads=[slot], writes=[cbuf])
            return slot

        pc_state = {"n": 0}

        def preconvert(W, k0, nk, c0, ncols):
            ck = (WNAMES[id(W)], k0, nk, c0, ncols)
            if (not USE_WCACHE) or ck in cache_ix or len(cache_ix) >= NCACHE:
                return
            n = nk * ncols
            ci = len(cache_ix)
            cbuf = Buf(None, f"wc{ci}")
            cache_ix[ck] = (ci, cbuf)
            src = W[k0 * 128:(k0 + nk) * 128, c0:c0 + ncols].rearrange("(k p) c -> p k c", p=128)
            dst = wcache[ci, :, 0:n].rearrange("p (k c) -> p k c", k=nk)
            key = f"d_pc{pc_state['n']}"
            pc_state["n"] += 1
            S.dma("pool", key, lambda e: e.dma_start(out=dst, in_=src), writes=[cbuf])

        pc_list = []
        for b in range(22):
            pc_list.append((w_gate, 0, KC, b * 256, 256))
            pc_list.append((w_up, 0, KC, b * 256, 256))

        def preconvert_some(n):
            for _ in range(n):
                if pc_list and pc_state["n"] < N_PRECONV:
                    preconvert(*pc_list.pop(0))

        def mm_group(out_ap, outbuf, pairs, reads, tail_signal=True):
            n = len(pairs)
            tok = None
            for i, (l, r) in enumerate(pairs):
                tok = S.op("pe", (lambda l=l, r=r, i=i: (lambda e: e.matmul(out_ap, lhsT=l, rhs=r, start=(i == 0),
                                                                           stop=(i == n - 1))))(),
                           reads=reads if i == 0 else (), writes=[outbuf] if i == 0 else (),
                           signal=(i == n - 1) and tail_signal)
            S._register(tok, reads, [outbuf])
            return tok

        PM = P[7]
        modst = {"live": True}

        def mod_block(blk):
            slot = wblock(w_ada, 0, KC, blk * 256, 256)
            for mmi in range(2):
                m = blk * 2 + mmi
                pairs = [(slot.t[:, kc * 256 + mmi * 128: kc * 256 + (mmi + 1) * 128], SCB.t[:, kc * 3:kc * 3 + 3])
                         for kc in range(KC)]
                mm_group(PM.t[:, m * 3:m * 3 + 3], PM, pairs, reads=[slot, SCB])

        def mod_finish(lo, hi):
            for s in range(3):
                S.op("dve", (lambda s=s: (lambda e: e.tensor_tensor(
                    out=MOD.t[:, s, lo:hi], in0=PM.t[:, 0:288].rearrange("p (m s) -> p m s", s=3)[:, lo:hi, s],
                    in1=PV.t[:, PV_BADA + lo:PV_BADA + hi], op=ALU.add)))(), reads=[PM, PV], writes=[MOD])

        def mod_derive():
            S.op("dve", lambda e: e.tensor_scalar(out=SC2P.t[:, :, :], in0=MOD.t[:, :, 64:80], scalar1=1.0, scalar2=None,
                                                  op0=ALU.add), reads=[MOD], writes=[SC2P])
            for s in range(3):
                S.op("dve", (lambda s=s: (lambda e: e.tensor_tensor(out=GA2.t[:, s, :], in0=SC2P.t[:, s, :],
                                                                   in1=PV.t[:, PV_LN1G:PV_LN1G + 16], op=ALU.mult)))(),
                     reads=[SC2P, PV], writes=[GA2])
                S.op("dve", (lambda s=s: (lambda e: e.tensor_tensor(out=BA2.t[:, s, :], in0=SC2P.t[:, s, :],
                                                                   in1=PV.t[:, PV_LN1B:PV_LN1B + 16], op=ALU.mult)))(),
                     reads=[SC2P, PV], writes=[BA2])
                S.op("dve", (lambda s=s: (lambda e: e.tensor_tensor(out=BA2.t[:, s, :], in0=BA2.t[:, s, :],
                                                                   in1=MOD.t[:, s, 48:64], op=ALU.add)))(),
                     reads=[BA2, MOD], writes=[BA2])
            S.op("dve", lambda e: e.tensor_scalar(out=A1.t[:, :], in0=PV.t[:, PV_LN1G:PV_LN1G + 16], scalar1=ALPHA,
                                                  scalar2=None, op0=ALU.mult), reads=[PV], writes=[A1])
            S.op("dve", lambda e: e.tensor_scalar(out=B1.t[:, :], in0=PV.t[:, PV_LN1B:PV_LN1B + 16], scalar1=ALPHA,
                                                  scalar2=None, op0=ALU.mult), reads=[PV], writes=[B1])

        for blk in range(16):
            mod_block(blk)
        mod_finish(0, 32)
        S.op("dve", lambda e: e.tensor_scalar(out=SC1P.t[:, :, :], in0=MOD.t[:, :, 16:32], scalar1=1.0, scalar2=None,
                                              op0=ALU.add), reads=[MOD], writes=[SC1P])
        mod_next = {"blk": 16}

        def mod_some(n):
            for _ in range(n):
                if mod_next["blk"] < 48:
                    mod_block(mod_next["blk"])
                    mod_next["blk"] += 1
            if mod_next["blk"] == 48 and modst["live"]:
                mod_finish(32, 96)
                mod_derive()
                modst["live"] = False

        SH1 = lambda s, kc: MOD.t[:, s, kc:kc + 1]
        G1 = lambda s, m: MOD.t[:, s, 32 + m:33 + m]
        G2M = lambda s, m: MOD.t[:, s, 80 + m:81 + m]

        for i in range(2):
            S.op("dve", (lambda i=i: (lambda e: e.memset(S32[i].t[:, :], 0.0)))(), writes=[S32[i]])
            S.op("dve", (lambda i=i: (lambda e: e.memset(UH[i].t[:, :, :], 0.0)))(), writes=[UH[i]])
            for hp in range(2):
                S.op("dve", (lambda i=i, hp=hp: (lambda e: e.memset(SBF[i][hp].t[:, :], 0.0)))(), writes=[SBF[i][hp]])

        xl_i = {"i": 0}
        os_i = {"i": 0}
        rot = {}

        def nxt(lst, key):
            key = id(lst)
            i = rot.get(key, 0)
            rot[key] = i + 1
            return lst[i % len(lst)]

        def prefetch_x(T):
            SW = T.SW
            src = T.x[T.r0:T.r0 + SW, :]
            S.dma("sp", "d_xpf", lambda e: e.dma_start(out=XPF.t[:SW, :], in_=src), writes=[XPF])
            xpref[T.tid] = True

        def stage0_groups(T):
            TW, SW = T.TW, T.SW
            out = []
            for s in range(T.NSUB):
                holder = {}
                for g in range(4):
                    def grp(s=s, g=g, holder=holder):
                        if g == 0:
                            if s == 0 and xpref.get(T.tid):
                                holder["xl"] = XPF
                            else:
                                xl = XL[xl_i["i"] % 2]
                                key = f"d_xl{xl_i['i'] % 2}"
                                xl_i["i"] += 1
                                src = T.x[T.r0 + s * SW:T.r0 + (s + 1) * SW, :]
                                S.dma("sp", key, lambda e: e.dma_start(out=xl.t[:SW, :], in_=src), writes=[xl])
                                holder["xl"] = xl
                        xl = holder["xl"]
                        bk = bank("ALL")
                        for j in range(4):
                            kc = g * 4 + j
                            S.op("pe", (lambda j=j, kc=kc: (lambda e: e.transpose(
                                out=bk.t[:, j * SW:(j + 1) * SW], in_=xl.t[:SW, kc * 128:(kc + 1) * 128],
                                identity=ident(SW))))(), reads=[xl, CST], writes=[bk], signal=(j == 3))
                        for j in range(4):
                            kc = g * 4 + j
                            for (c0, n, sidx) in T.segs:
                                lo, hi = max(c0, s * SW), min(c0 + n, (s + 1) * SW)
                                if lo >= hi:
                                    continue
                                src_ap = bk.t[:, j * SW + lo - s * SW: j * SW + hi - s * SW]
                                if g % 2 == 0:
                                    S.op("act", (lambda kc=kc, lo=lo, hi=hi, sidx=sidx, src_ap=src_ap: (
                                        lambda e: e.activation(
                                            out=H[kc].t[:, lo:hi], in_=src_ap, func=AF.Identity,
                                            bias=SH1(sidx, kc), scale=SC1P.t[:, sidx, kc:kc + 1])))(),
                                         reads=[bk, MOD, SC1P], writes=[H[kc]])
                                    if T.kind != "pre":
                                        S.op("act", (lambda kc=kc, lo=lo, hi=hi, src_ap=src_ap: (
                                            lambda e: e.activation(out=R[kc].t[:, lo:hi], in_=src_ap, func=AF.Copy,
                                                                   scale=ALPHA)))(), reads=[bk], writes=[R[kc]])
                                else:
                                    S.op("dve", (lambda kc=kc, lo=lo, hi=hi, sidx=sidx, src_ap=src_ap: (
                                        lambda e: e.tensor_scalar(
                                            out=H[kc].t[:, lo:hi], in0=src_ap, scalar1=SC1P.t[:, sidx, kc:kc + 1],
                                            scalar2=SH1(sidx, kc), op0=ALU.mult, op1=ALU.add)))(),
                                         reads=[bk, MOD, SC1P], writes=[H[kc]])
                                    if T.kind != "pre":
                                        S.op("dve", (lambda kc=kc, lo=lo, hi=hi, src_ap=src_ap: (
                                            lambda e: e.tensor_scalar(out=R[kc].t[:, lo:hi], in0=src_ap, scalar1=ALPHA,
                                                                      scalar2=None, op0=ALU.mult)))(),
                                             reads=[bk], writes=[R[kc]])
                    out.append(grp)
            return out

        def stage0(T):
            if getattr(T, "s0done", False):
                return
            for grp in stage0_groups(T):
                grp()
            T.s0done = True

        def proj_fm(W, c0, ncols, T, consume, mrows_last=None):
            TW = T.TW
            nb = (ncols + 255) // 256
            mi = 0
            for b in range(nb):
                bc = min(256, ncols - b * 256)
                slot = wblock(W, 0, KC, c0 + b * 256, bc)
                for mmi in range((bc + 127) // 128):
                    mr = min(128, bc - mmi * 128)
                    bk = bank("PR")
                    pairs = [(slot.t[:, kc * bc + mmi * 128: kc * bc + mmi * 128 + mr], T.rhs(kc)) for kc in range(KC)]
                    mm_group(bk.t[:mr, :TW], bk, pairs, reads=[slot] + T.rhs_bufs)
                    consume(mi, bk)
                    mi += 1

        def proj_groups(W, c0, ncols, T, consume):
            TW = T.TW
            out = []
            nb = (ncols + 255) // 256
            mi = 0
            for b in range(nb):
                bc = min(256, ncols - b * 256)
                holder = {}
                for mmi in range((bc + 127) // 128):
                    mr = min(128, bc - mmi * 128)

                    def g(b=b, bc=bc, mmi=mmi, mr=mr, mi=mi, holder=holder):
                        if mmi == 0:
                            holder["slot"] = wblock(W, 0, KC, c0 + b * 256, bc)
                        slot = holder["slot"]
                        bk = bank("PR")
                        pairs = [(slot.t[:, kc * bc + mmi * 128: kc * bc + mmi * 128 + mr], T.rhs(kc))
                                 for kc in range(KC)]
                        mm_group(bk.t[:mr, :TW], bk, pairs, reads=[slot] + T.rhs_bufs)
                        consume(mi, bk)
                    out.append(g)
                    mi += 1
            return out

        def stage1(T, extra=None):
            TW, SW = T.TW, T.SW
            main = T.kind != "pre"
            nch = len(T.chunks)
            NSUB = T.NSUB
            alias_barrier(ACTT, GLA_TMPS)
            st = {}

            def c_flr(mi, bk):
                S.op("act", lambda e: e.activation(out=FLRB.t[:16, :TW], in_=bk.t[:16, :TW], func=AF.Copy),
                     reads=[bk], writes=[FLRB])
            p_flr = proj_groups(w_in, C_FLR, 16, T, c_flr)[0]

            def p_z(s):
                pz = bank("PG")
                S.op("pe", lambda e: e.matmul(pz.t[:SW, :512], lhsT=FLRB.t[:16, s * SW:(s + 1) * SW],
                                              rhs=WFB.t[:16, :512], start=True, stop=False),
                     reads=[FLRB, WFB], writes=[pz], signal=False)
                S.op("pe", lambda e: e.matmul(pz.t[:SW, :512], lhsT=ONESB.t[0:1, :SW],
                                              rhs=BFB.t[0:1, :512], start=False, stop=True),
                     reads=[ONESB, BFB], writes=[pz])
                e1 = nxt(E1, "e1")
                ll = nxt(LL, "ll")
                S.op("act", lambda e: e.activation(out=e1.t[:SW, :], in_=pz.t[:SW, :512], func=AF.Exp, scale=-1.0),
                     reads=[pz], writes=[e1])
                S.op("act", lambda e: e.activation(out=ll.t[:SW, :], in_=e1.t[:SW, :], func=AF.Ln, bias=1.0),
                     reads=[e1], writes=[ll])
                st[("ll", s)] = ll

            def p_b(s):
                ll = st[("ll", s)]
                pb = bank("PG")
                for h in range(4):
                    S.op("pe", (lambda h=h: (lambda e: e.matmul(
                        pb.t[:, h * SW:(h + 1) * SW], lhsT=ll.t[:SW, h * 128:(h + 1) * 128],
                        rhs=T.maskC, start=True, stop=True)))(), reads=[ll, CST], writes=[pb], signal=(h == 3))
                pb3 = pb.t[:, 0:4 * SW].rearrange("p (h n) -> p h n", h=4)
                if main:
                    S.op("act", lambda e: e.activation(out=EQ.t[:, :, s * SW:(s + 1) * SW], in_=pb3, func=AF.Exp,
                                                       bias=QSCALE_LN), reads=[pb], writes=[EQ])
                S.op("act", lambda e: e.activation(out=EK.t[:, :, s * SW:(s + 1) * SW], in_=pb3, func=AF.Exp,
                                                   scale=-1.0), reads=[pb], writes=[EK])
                for ic, (c0, CL, sid) in enumerate(T.chunks):
                    ci = s * nch + ic
                    S.op("act", (lambda c0=c0, CL=CL, ci=ci: (lambda e: e.activation(
                        out=DEC.t[:, ci, :], in_=pb3[:, :, c0 + CL - 1], func=AF.Exp)))(), reads=[pb], writes=[DEC])

            def c_q(mi, bk):
                S.op("dve", lambda e: e.tensor_tensor(out=QT.t[:, mi, :TW], in0=bk.t[:, :TW], in1=EQ.t[:, mi, :TW],
                                                      op=ALU.mult), reads=[bk, EQ], writes=[QT])

            def c_k(mi, bk):
                S.op("dve", lambda e: e.tensor_tensor(out=KT.t[:, mi, :TW], in0=bk.t[:, :TW], in1=EK.t[:, mi, :TW],
                                                      op=ALU.mult), reads=[bk, EK], writes=[KT])

            def c_go(mi, bk):
                st[("go", mi)] = True
                S.op("act", lambda e: e.activation(out=SG.t[:, mi, :TW], in_=bk.t[:, :TW], func=AF.Silu),
                     reads=[bk], writes=[SG])

            f_k = proj_groups(w_in, C_K, 512, T, c_k)
            f_q = proj_groups(w_in, C_Q, 512, T, c_q) if main else []
            f_go = proj_groups(w_in, C_GO, 1024, T, c_go) if main else []

            f_v = []
            for b in range(4):
                holder = {}
                for s in range(NSUB):
                    def g(b=b, s=s, holder=holder):
                        if s == 0:
                            holder["slot"] = wblock(w_in, 0, KC, C_V + b * 256, 256)
                        slot = holder["slot"]
                        bk = bank("PR")
                        pairs = [(H[kc].t[:, s * SW:(s + 1) * SW], slot.t[:, kc * 256:(kc + 1) * 256])
                                 for kc in range(KC)]
                        mm_group(bk.t[:SW, :256], bk, pairs, reads=[slot] + H)
                        if s % 2 == 0:
                            S.op("act", lambda e: e.activation(out=VT.t[:SW, s, b * 256:(b + 1) * 256],
                                                               in_=bk.t[:SW, :256], func=AF.Copy),
                                 reads=[bk], writes=[VT])
                        else:
                            S.op("dve", lambda e: e.tensor_copy(out=VT.t[:SW, s, b * 256:(b + 1) * 256],
                                                                in_=bk.t[:SW, :256]), reads=[bk], writes=[VT])
                    f_v.append(g)

            f_conv = []
            if main:
                for c in range(8):
                    cst = {}
                    for nm, cb in (("hc", C_HC), ("hin", C_HIN), ("hb", C_HB)):
                        def g(c=c, nm=nm, cb=cb, cst=cst):
                            slot = wblock(w_in, 0, KC, cb + c * 128, 128)
                            bk = bank("PR")
                            pairs = [(slot.t[:, kc * 128:(kc + 1) * 128], T.rhs(kc)) for kc in range(KC)]
                            mm_group(bk.t[:, :TW], bk, pairs, reads=[slot] + T.rhs_bufs)
                            cst[nm] = bk
                            if nm == "hc":
                                hcs = nxt(HCS, "hcs")
                                cst["hcs"] = hcs
                                S.op("act", lambda e: e.activation(out=hcs.t[:, :TW], in_=bk.t[:, :TW], func=AF.Copy),
                                     reads=[bk], writes=[hcs])
                            elif nm == "hin":
                                hcs = cst["hcs"]
                                ub = nxt(UB, "ub")
                                cst["ub"] = ub
                                off = 0
                                for (c0, n, sidx) in T.segs:
                                    uv = ub.t[:, off:off + n + 2]
                                    off += n + 2
                                    uh = UH[T.slot[sidx]]
                                    S.op("dve", (lambda uv=uv, uh=uh: (lambda e: e.tensor_copy(
                                        out=uv[:, 0:2], in_=uh.t[:, c, :])))(), reads=[uh], writes=[ub])
                                    S.op("dve", (lambda uv=uv, c0=c0, n=n: (lambda e: e.tensor_tensor(
                                        out=uv[:, 2:2 + n], in0=bk.t[:, c0:c0 + n], in1=hcs.t[:, c0:c0 + n],
                                        op=ALU.mult)))(), reads=[bk, hcs], writes=[ub])
                                    S.op("dve", (lambda uv=uv, n=n, uh=uh: (lambda e: e.tensor_copy(
                                        out=uh.t[:, c, :], in_=uv[:, n:n + 2])))(), reads=[ub], writes=[uh])
                            else:
                                ub = cst["ub"]
                                cv = nxt(CV, "cv")
                                off = 0
                                for (c0, n, sidx) in T.segs:
                                    uv = ub.t[:, off:off + n + 2]
                                    off += n + 2
                                    S.op("act", (lambda uv=uv, c0=c0, n=n: (lambda e: e.activation(
                                        out=cv.t[:, c0:c0 + n], in_=uv[:, 0:n], func=AF.Identity,
                                        scale=PV.t[:, PV_WC + c * 3:PV_WC + c * 3 + 1])))(), reads=[ub, PV], writes=[cv])
                                    for i in (1, 2):
                                        S.op("dve", (lambda uv=uv, c0=c0, n=n, i=i: (lambda e: e.scalar_tensor_tensor(
                                            out=cv.t[:, c0:c0 + n], in0=uv[:, i:i + n],
                                            scalar=PV.t[:, PV_WC + c * 3 + i:PV_WC + c * 3 + i + 1],
                                            in1=cv.t[:, c0:c0 + n], op0=ALU.mult, op1=ALU.add)))(),
                                             reads=[ub, PV, cv], writes=[cv])
                                    S.op("dve", (lambda c0=c0, n=n: (lambda e: e.tensor_tensor(
                                        out=MIX[c].t[:, c0:c0 + n], in0=bk.t[:, c0:c0 + n], in1=cv.t[:, c0:c0 + n],
                                        op=ALU.mult)))(), reads=[bk, cv], writes=[MIX[c]])
                        f_conv.append(g)

            def s1(s, hp):
                g0 = P[4]
                g0bf = g0.t[:, :].bitcast(BF16)
                if main:
                    for hh in range(2):
                        h = 2 * hp + hh
                        S.op("pe", (lambda hh=hh, h=h: (lambda e: e.matmul(
                            g0.t[:SW, hh * SW:(hh + 1) * SW], lhsT=KT.t[:, h, s * SW:(s + 1) * SW],
                            rhs=QT.t[:, h, s * SW:(s + 1) * SW], start=True, stop=True)))(),
                             reads=[KT, QT], writes=[g0], signal=False)
                for hh in range(2):
                    h = 2 * hp + hh
                    S.op("pe", (lambda hh=hh, h=h: (lambda e: e.transpose(
                        out=g0bf[:SW, 512 + hh * 128: 512 + (hh + 1) * 128], in_=KT.t[:, h, s * SW:(s + 1) * SW],
                        identity=IDB.t[:, :])))(), reads=[KT, IDB], writes=[g0], signal=(hh == 1))
                ktok = nxt(KTOK, "ktok")
                st[("ktok", s, hp)] = ktok
                S.op("dve", lambda e: e.tensor_copy(out=ktok.t[:SW, :], in_=g0bf[:SW, 512:768]),
                     reads=[g0], writes=[ktok])
                if main:
                    am = nxt(AM, "am")
                    st[("am", s, hp)] = am
                    for hh in range(2):
                        S.op("dve", (lambda hh=hh: (lambda e: e.tensor_tensor(
                            out=am.t[:SW, hh, :SW], in0=g0.t[:SW, hh * SW:(hh + 1) * SW], in1=T.maskA,
                            op=ALU.mult)))(), reads=[g0, CST], writes=[am])

            def s2(s, hp):
                ktok = st[("ktok", s, hp)]
                if main:
                    am = st[("am", s, hp)]
                    g1 = P[5]
                    st[("g1", s, hp)] = g1
                    for hh in range(2):
                        h = 2 * hp + hh
                        for ee in range(2):
                            blk = (hh * 2 + ee) * SW
                            S.op("pe", (lambda blk=blk, h=h, ee=ee, hh=hh: (lambda e: e.matmul(
                                g1.t[:, blk:blk + SW], lhsT=VT.t[:SW, s, h * 256 + ee * 128: h * 256 + (ee + 1) * 128],
                                rhs=am.t[:SW, hh, :SW], start=True, stop=False, skip_group_check=True)))(),
                                 reads=[VT, am], writes=[g1], signal=False)
                            for ic, (c0, CL, sid) in enumerate(T.chunks):
                                last = (ic == nch - 1) and hh == 1 and ee == 1
                                S.op("pe", (lambda blk=blk, c0=c0, CL=CL, sid=sid, h=h, hh=hh, ee=ee, ic=ic:
                                            (lambda e: e.matmul(
                                                g1.t[:, blk + c0: blk + c0 + CL],
                                                lhsT=SBF[sid][hp].t[:, hh * 256 + ee * 128: hh * 256 + (ee + 1) * 128],
                                                rhs=QT.t[:, h, s * SW + c0: s * SW + c0 + CL], start=False,
                                                stop=(ic == nch - 1), skip_group_check=True)))(),
                                     reads=[SBF[sid][hp], QT], writes=[g1], signal=last)
                for ic, (c0, CL, sid) in enumerate(T.chunks):
                    ci = s * nch + ic
                    g2 = P[6]
                    for hh in range(2):
                        h = 2 * hp + hh
                        S.op("pe", (lambda hh=hh, h=h, c0=c0, CL=CL, g2=g2: (lambda e: e.matmul(
                            g2.t[:, hh * 256:(hh + 1) * 256], lhsT=ktok.t[c0:c0 + CL, hh * 128:(hh + 1) * 128],
                            rhs=VT.t[c0:c0 + CL, s, h * 256:(h + 1) * 256], start=True, stop=True)))(),
                             reads=[ktok, VT], writes=[g2], signal=(hh == 1))
                    for hh in range(2):
                        h = 2 * hp + hh
                        sl = S32[sid].t[:, h * 256:(h + 1) * 256]
                        S.op("dve", (lambda sl=sl, ci=ci, h=h: (lambda e: e.tensor_scalar(
                            out=sl, in0=sl, scalar1=DEC.t[:, ci, h:h + 1], scalar2=None, op0=ALU.mult)))(),
                             reads=[S32[sid], DEC], writes=[S32[sid]])
                        S.op("dve", (lambda sl=sl, ci=ci, h=h, g2=g2, hh=hh: (lambda e: e.scalar_tensor_tensor(
                            out=sl, in0=g2.t[:, hh * 256:(hh + 1) * 256], scalar=DEC.t[:, ci, h:h + 1], in1=sl,
                            op0=ALU.mult, op1=ALU.add)))(), reads=[g2, S32[sid], DEC], writes=[S32[sid]])
                    if main:
                        S.op("act", (lambda sid=sid: (lambda e: e.activation(
                            out=SBF[sid][hp].t[:, :], in_=S32[sid].t[:, hp * 512:(hp + 1) * 512], func=AF.Copy)))(),
                             reads=[S32[sid]], writes=[SBF[sid][hp]])
                if main:
                    sq = nxt(SQ, "sq")
                    st[("sq", s, hp)] = sq
                    S.op("act", lambda e: e.activation(out=sq.t[:, :4 * SW], in_=g1.t[:, :4 * SW], func=AF.Square),
                         reads=[g1], writes=[sq])

            def s3(s, hp):
                if not main:
                    return
                while not all(st.get(("go", (2 * hp + hh) * 2 + ee)) for hh in range(2) for ee in range(2)):
                    F(1)
                g1 = st[("g1", s, hp)]
                sq = st[("sq", s, hp)]
                g3 = P[7]
                for hh in range(2):
                    for ee in range(2):
                        S.op("pe", (lambda hh=hh, ee=ee: (lambda e: e.matmul(
                            g3.t[:, hh * SW:(hh + 1) * SW], lhsT=ONESB.t[:, :],
                            rhs=sq.t[:, (hh * 2 + ee) * SW:(hh * 2 + ee + 1) * SW], start=(ee == 0),
                            stop=(ee == 1))))(), reads=[sq, ONESB], writes=[g3], signal=(hh == 1 and ee == 1))
                rinv = nxt(RINV, "rinv")
                S.op("act", lambda e: e.activation(out=rinv.t[:, :2 * SW], in_=g3.t[:, :2 * SW], func=AF.Ln,
                                                   scale=1.0 / 256.0, bias=RMS_EPS), reads=[g3], writes=[rinv])
                S.op("act", lambda e: e.activation(out=rinv.t[:, :2 * SW], in_=rinv.t[:, :2 * SW], func=AF.Exp,
                                                   scale=-0.5), reads=[rinv], writes=[rinv])
                tn = nxt(TN, "tn")
                for hh in range(2):
                    h = 2 * hp + hh
                    for ee in range(2):
                        blk = (hh * 2 + ee) * SW
                        S.op("dve", (lambda blk=blk, ee=ee, hh=hh: (lambda e: e.scalar_tensor_tensor(
                            out=tn.t[:, blk:blk + SW], in0=g1.t[:, blk:blk + SW],
                            scalar=PV.t[:, PV_GAIN + ee:PV_GAIN + ee + 1],
                            in1=rinv.t[:, hh * SW:(hh + 1) * SW], op0=ALU.mult, op1=ALU.mult)))(),
                             reads=[g1, PV, rinv], writes=[tn])
                        mchunk = 8 + h * 2 + ee
                        S.op("dve", (lambda blk=blk, mchunk=mchunk, h=h, ee=ee: (lambda e: e.tensor_tensor(
                            out=MIX[mchunk].t[:, s * SW:(s + 1) * SW], in0=tn.t[:, blk:blk + SW],
                            in1=SG.t[:, h * 2 + ee, s * SW:(s + 1) * SW], op=ALU.mult)))(),
                             reads=[tn, SG], writes=[MIX[mchunk]])

            fill = []

            def F(n):
                for _ in range(n):
                    if fill:
                        fill.pop(0)()

            p_flr()
            fill.extend(f_v)
            F(2)
            p_z(0)
            F(2)
            for s in range(NSUB):
                if s + 1 < NSUB:
                    p_z(s + 1)
                p_b(s)
                F(2)
            F(len(fill))
            for g in f_k:
                g()
            for g in f_q:
                g()
            for g in f_go[:4]:
                g()
            fill.extend(f_go[4:])
            fill.extend(f_conv)
            if extra:
                fill.extend(extra)
            its = [(s, hp) for s in range(NSUB) for hp in HP_ORDER]
            nf = max(1, len(fill) // (2 * len(its) + 1)) if fill else 0
            F(nf)
            s1(*its[0])
            for i, it in enumerate(its):
                F(nf)
                s2(*it)
                if i + 1 < len(its):
                    s1(*its[i + 1])
                F(nf)
                s3(*it)
            F(len(fill))

        def ln_stats_accum(m, TW, pst1, pst2):
            sqb = nxt(SQB, "sqb")
            xb = nxt(XB, "xb")
            S.op("act", lambda e: e.activation(out=sqb.t[:, :TW], in_=R[m].t[:, :TW], func=AF.Square),
                 reads=[R[m]], writes=[sqb])
            S.op("act", lambda e: e.activation(out=xb.t[:, :TW], in_=R[m].t[:, :TW], func=AF.Copy),
                 reads=[R[m]], writes=[xb])

            def pe_part():
                S.op("pe", lambda e: e.matmul(pst1.t[:, :TW], lhsT=ONESB.t[:, :], rhs=xb.t[:, :TW],
                                              start=(m == 0), stop=(m == KC - 1)),
                     reads=[xb, ONESB], writes=[pst1], signal=False)
                S.op("pe", lambda e: e.matmul(pst2.t[:, :TW], lhsT=ONESB.t[:, :], rhs=sqb.t[:, :TW],
                                              start=(m == 0), stop=(m == KC - 1)),
                     reads=[sqb, ONESB], writes=[pst2])
            return pe_part

        def ln_finish(TW, pst1, pst2):
            S.op("act", lambda e: e.activation(out=MSQ.t[:, :TW], in_=pst1.t[:, :TW], func=AF.Square, scale=1.0 / D),
                 reads=[pst1], writes=[MSQ])
            S.op("dve", lambda e: e.scalar_tensor_tensor(out=RSTD.t[:, :TW], in0=pst2.t[:, :TW], scalar=1.0 / D,
                                                         in1=MSQ.t[:, :TW], op0=ALU.mult, op1=ALU.subtract),
                 reads=[pst2, MSQ], writes=[RSTD])
            S.op("act", lambda e: e.activation(out=RSTD.t[:, :TW], in_=RSTD.t[:, :TW], func=AF.Ln, bias=LN_EPS),
                 reads=[RSTD], writes=[RSTD])
            S.op("act", lambda e: e.activation(out=RSTD.t[:, :TW], in_=RSTD.t[:, :TW], func=AF.Exp, scale=-0.5),
                 reads=[RSTD], writes=[RSTD])
            S.op("dve", lambda e: e.scalar_tensor_tensor(out=NMR.t[:, :TW], in0=pst1.t[:, :TW], scalar=-1.0 / D,
                                                         in1=RSTD.t[:, :TW], op0=ALU.mult, op1=ALU.mult),
                 reads=[pst1, RSTD], writes=[NMR])

        def normalize(m, TW):
            S.op("dve", (lambda m=m: (lambda e: e.tensor_tensor(out=R[m].t[:, :TW], in0=R[m].t[:, :TW],
                                                               in1=RSTD.t[:, :TW], op=ALU.mult)))(),
                 reads=[R[m], RSTD], writes=[R[m]])
            S.op("dve", (lambda m=m: (lambda e: e.tensor_tensor(out=R[m].t[:, :TW], in0=R[m].t[:, :TW],
                                                               in1=NMR.t[:, :TW], op=ALU.add)))(),
                 reads=[R[m], NMR], writes=[R[m]])

        def stage2(T):
            TW = T.TW
            pst1, pst2 = P[4], P[5]
            pend = None
            for b in range(8):
                slot = wblock(w_o, 0, KC, b * 256, 256)
                for mmi in range(2):
                    m = b * 2 + mmi
                    bk = bank("PR")
                    pairs = [(slot.t[:, kc * 256 + mmi * 128: kc * 256 + (mmi + 1) * 128], MIX[kc].t[:, :TW])
                             for kc in range(KC)]
                    mm_group(bk.t[:, :TW], bk, pairs, reads=[slot] + MIX)
                    if pend is not None:
                        pend()
                        pend = None
                    for (c0, n, sidx) in T.segs:
                        S.op("dve", (lambda bk=bk, c0=c0, n=n, sidx=sidx, m=m: (lambda e: e.scalar_tensor_tensor(
                            out=R[m].t[:, c0:c0 + n], in0=bk.t[:, c0:c0 + n], scalar=G1(sidx, m),
                            in1=R[m].t[:, c0:c0 + n], op0=ALU.mult, op1=ALU.add)))(),
                             reads=[bk, MOD, R[m]], writes=[R[m]])
                    if pend is not None:
                        pend()
                    pend = ln_stats_accum(m, TW, pst1, pst2)
            pend()
            ln_finish(TW, pst1, pst2)
            for m in range(KC):
                normalize(m, TW)
                for (c0, n, sidx) in T.segs:
                    S.op("act", (lambda m=m, c0=c0, n=n, sidx=sidx: (lambda e: e.activation(
                        out=MIX[m].t[:, c0:c0 + n], in_=R[m].t[:, c0:c0 + n], func=AF.Identity,
                        bias=BA2.t[:, sidx, m:m + 1], scale=GA2.t[:, sidx, m:m + 1])))(),
                         reads=[R[m], BA2, GA2], writes=[MIX[m]])
                S.op("act", (lambda m=m: (lambda e: e.activation(
                    out=R[m].t[:, :TW], in_=R[m].t[:, :TW], func=AF.Identity, scale=A1.t[:, m:m + 1],
                    bias=B1.t[:, m:m + 1])))(), reads=[R[m], A1, B1], writes=[R[m]])

        def stage3(T):
            TW, SW = T.TW, T.SW
            alias_barrier(GLA_TMPS, ACTT)
            pst1, pst2 = P[4], P[5]
            for b in range(22):
                sg_ = wblock(w_gate, 0, KC, b * 256, 256)
                su_ = wblock(w_up, 0, KC, b * 256, 256)
                for mmi in range(2):
                    j = b * 2 + mmi
                    bg = bank("PR")
                    bu = bank("PR")
                    mm_group(bg.t[:, :TW], bg, [(sg_.t[:, kc * 256 + mmi * 128: kc * 256 + (mmi + 1) * 128],
                                                 MIX[kc].t[:, :TW]) for kc in range(KC)], reads=[sg_] + MIX)
                    mm_group(bu.t[:, :TW], bu, [(su_.t[:, kc * 256 + mmi * 128: kc * 256 + (mmi + 1) * 128],
                                                 MIX[kc].t[:, :TW]) for kc in range(KC)], reads=[su_] + MIX)
                    sgt = nxt(SGT, "sgt")
                    S.op("act", (lambda bg=bg, sgt=sgt: (lambda e: e.activation(out=sgt.t[:, :TW], in_=bg.t[:, :TW],
                                                                                func=AF.Silu)))(),
                         reads=[bg], writes=[sgt])
                    S.op("dve", (lambda bu=bu, sgt=sgt, j=j: (lambda e: e.tensor_tensor(
                        out=ACTT[j].t[:, :TW], in0=bu.t[:, :TW], in1=sgt.t[:, :TW], op=ALU.mult)))(),
                         reads=[bu, sgt], writes=[ACTT[j]])
            pend = None
            for m in range(KC):
                bk = bank("PR")
                pairs = []
                slots = []
                for half in range(2):
                    slot = wblock(w_down, half * 22, 22, m * 128, 128)
                    slots.append(slot)
                    pairs += [(slot.t[:, kk * 128:(kk + 1) * 128], ACTT[half * 22 + kk].t[:, :TW]) for kk in range(22)]
                mm_group(bk.t[:, :TW], bk, pairs, reads=slots + ACTT)
                if pend is not None:
                    pend()
                    pend = None
                for (c0, n, sidx) in T.segs:
                    S.op("dve", (lambda bk=bk, c0=c0, n=n, sidx=sidx, m=m: (lambda e: e.scalar_tensor_tensor(
                        out=R[m].t[:, c0:c0 + n], in0=bk.t[:, c0:c0 + n], scalar=G2M(sidx, m),
                        in1=R[m].t[:, c0:c0 + n], op0=ALU.mult, op1=ALU.add)))(),
                         reads=[bk, MOD, R[m]], writes=[R[m]])
                pend = ln_stats_accum(m, TW, pst1, pst2)
            pend()
            ln_finish(TW, pst1, pst2)
            for m in range(KC):
                normalize(m, TW)
                S.op("act", (lambda m=m: (lambda e: e.activation(
                    out=R[m].t[:, :TW], in_=R[m].t[:, :TW], func=AF.Identity,
                    bias=PV.t[:, PV_LN2B + m:PV_LN2B + m + 1], scale=PV.t[:, PV_LN2G + m:PV_LN2G + m + 1])))(),
                     reads=[R[m], PV], writes=[R[m]])
            for s in range(T.NSUB):
                osb = OS[os_i["i"] % 2]
                key = f"d_os{os_i['i'] % 2}"
                os_i["i"] += 1
                for g in range(4):
                    bk = bank("ALL")
                    for j in range(4):
                        kc = g * 4 + j
                        S.op("pe", (lambda bk=bk, j=j, kc=kc, s=s: (lambda e: e.transpose(
                            out=bk.t[:SW, j * 128:(j + 1) * 128], in_=R[kc].t[:, s * SW:(s + 1) * SW],
                            identity=ident(128))))(), reads=[R[kc], CST], writes=[bk], signal=(j == 3))
                    if g % 2 == 0:
                        S.op("act", (lambda bk=bk, g=g, osb=osb: (lambda e: e.activation(
                            out=osb.t[:SW, g * 512:(g + 1) * 512], in_=bk.t[:SW, :512], func=AF.Copy)))(),
                             reads=[bk], writes=[osb])
                    else:
                        S.op("dve", (lambda bk=bk, g=g, osb=osb: (lambda e: e.tensor_copy(
                            out=osb.t[:SW, g * 512:(g + 1) * 512], in_=bk.t[:SW, :512])))(),
                             reads=[bk], writes=[osb])
                dst = T.y[T.r0 + s * SW:T.r0 + (s + 1) * SW, :]
                tok = S.dma("sp", key, (lambda osb=osb, dst=dst: (lambda e: e.dma_start(out=dst, in_=osb.t[:SW, :])))(),
                            reads=[osb])
                S.out_tokens.append(tok)

        def make_tile(kind, x, y, r0, TW, SW, segs, chunks, maskA, maskC):
            T = TileDesc()
            T.tid = (kind, r0)
            T.kind, T.x, T.y, T.r0, T.TW, T.SW = kind, x, y, r0, TW, SW
            T.NSUB = TW // SW
            T.segs, T.chunks, T.maskA, T.maskC = segs, chunks, maskA, maskC
            T.slot = {0: 0, 1: 0, 2: 1}
            T.rhs = lambda kc: H[kc].t[:, :TW]
            T.rhs_bufs = list(H)
            return T

        mAp = CST.t[:, CS_MAP:CS_MAP + 128]
        mCp = CST.t[:, CS_MCP:CS_MCP + 128]
        mAs = CST.t[:64, CS_MAS:CS_MAS + 64]
        mCs = CST.t[:64, CS_MCS:CS_MCS + 64]

        main_tiles = [make_tile("main", xm, yp, t * TWP, TWP, 128, [(0, TWP, 0)], [(0, 128, 0)], mAp, mCp)
                      for t in range(n_main)]
        pre_tiles = [make_tile("pre", xp, None, t * TWP, TWP, 128, [(0, TWP, 0)], [(0, 128, 0)], mAp, mCp)
                     for t in range(n_pre)]
        for t in range(n_pre):
            T = pre_tiles[t]
            stage0(T)
            mod_some(4)
            if dbg == 0:
                preconvert_some(5)
            Tn = pre_tiles[t + 1] if t + 1 < n_pre else (main_tiles[0] if n_main > 0 else None)
            if t == n_pre - 1:
                for c in range(8):
                    bks = {}
                    for nm, cb in (("hc", C_HC), ("hin", C_HIN)):
                        slot = wblock(w_in, 0, KC, cb + c * 128, 128)
                        bk = bank("PR")
                        pairs = [(slot.t[:, kc * 128:(kc + 1) * 128], H[kc].t[:, TWP - 2:TWP]) for kc in range(KC)]
                        mm_group(bk.t[:, :2], bk, pairs, reads=[slot] + H)
                        bks[nm] = bk
                    hcs = nxt(HCS, "hcs")
                    S.op("act", (lambda hcs=hcs, bk=bks["hc"]: (lambda e: e.activation(
                        out=hcs.t[:, :2], in_=bk.t[:, :2], func=AF.Copy)))(), reads=[bks["hc"]], writes=[hcs])
                    S.op("dve", (lambda hcs=hcs, bk=bks["hin"], c=c: (lambda e: e.tensor_tensor(
                        out=UH[0].t[:, c, :], in0=bk.t[:, :2], in1=hcs.t[:, :2], op=ALU.mult)))(),
                         reads=[bks["hin"], hcs], writes=[UH[0]])
            if Tn is not None and dbg == 0:
                stage1(T, extra=stage0_groups(Tn))
                Tn.s0done = True
            else:
                stage1(T)
        mod_some(48)
        if n_pre > 0:
            fl = PV.t[:, PV_FLAG:PV_FLAG + 1]
            S.op("dve", lambda e: e.tensor_scalar(out=S32[0].t[:, :], in0=S32[0].t[:, :], scalar1=fl, scalar2=None,
                                                  op0=ALU.mult), reads=[S32[0], PV], writes=[S32[0]])
            S.op("dve", lambda e: e.tensor_scalar(out=UH[0].t[:, :, :], in0=UH[0].t[:, :, :], scalar1=fl, scalar2=None,
                                                  op0=ALU.mult), reads=[UH[0], PV], writes=[UH[0]])
            for hp in range(2):
                S.op("act", (lambda hp=hp: (lambda e: e.activation(
                    out=SBF[0][hp].t[:, :], in_=S32[0].t[:, hp * 512:(hp + 1) * 512], func=AF.Copy)))(),
                     reads=[S32[0]], writes=[SBF[0][hp]])

        samp_tile = make_tile("samp", xs, ys, 0, 64, 64, [(0, 32, 1), (32, 32, 2)], [(0, 32, 0), (32, 32, 1)],
                              mAs, mCs) if (do_sample and dbg != 2) else None
        for t in range(n_main):
            wstate["every"] = {0: 4, 1: 3, 2: 2}.get(t, 1) if n_main >= 4 else 1
            T = main_tiles[t]
            T.next = main_tiles[t + 1] if t + 1 < n_main else samp_tile
            stage0(T)
            if dbg == 5:
                continue
            stage1(T)
            if dbg == 1 and t == n_main - 1:
                for k in range(16):
                    S.out_tokens.append(S.dma("sp", f"d_dbg{k}", (lambda k=k: (lambda e: e.dma_start(
                        out=dbgmix[k], in_=MIX[k].t[:, :])))(), reads=[MIX[k]]))
                    S.out_tokens.append(S.dma("sp", f"d_dbh{k}", (lambda k=k: (lambda e: e.dma_start(
                        out=dbgh[k], in_=H[k].t[:, :])))(), reads=[H[k]]))
            if dbg == 2:
                continue
            stage2(T)
            if T.next is not None and dbg == 0:
                prefetch_x(T.next)
            stage3(T)
        tok = S.dma("sp", "d_o1", lambda e: e.dma_start(out=glp.rearrange("h d v -> d h v"),
                                                       in_=S32[0].t[:, :].rearrange("p (h v) -> p h v", h=4)),
                    reads=[S32[0]])
        S.out_tokens.append(tok)
        tok = S.dma("sp", "d_o2", lambda e: e.dma_start(out=cvp[:, :], in_=UH[0].t[:, :, :].rearrange("p c r -> p (c r)")),
                    reads=[UH[0]])
        S.out_tokens.append(tok)

        if do_sample and dbg != 2:
            for i in range(2):
                S.dma("sp", f"d_s{i}", (lambda i=i: (lambda e: e.dma_start(
                    out=S32[i].t[:, :].rearrange("p (h v) -> p h v", h=4),
                    in_=sg_d[i].rearrange("h d v -> d h v"))))(), writes=[S32[i]])
                S.dma("sp", f"d_u{i}", (lambda i=i: (lambda e: e.dma_start(
                    out=UH[i].t[:, :, :].rearrange("p c r -> p (c r)"), in_=cc_d[:, i * 16:(i + 1) * 16])))(),
                      writes=[UH[i]])
                for hp in range(2):
                    S.op("act", (lambda i=i, hp=hp: (lambda e: e.activation(
                        out=SBF[i][hp].t[:, :], in_=S32[i].t[:, hp * 512:(hp + 1) * 512], func=AF.Copy)))(),
                         reads=[S32[i]], writes=[SBF[i][hp]])
            T = samp_tile
            stage0(T)
            stage1(T)
            stage2(T)
            stage3(T)
            for i in range(2):
                tok = S.dma("sp", f"d_o3{i}", (lambda i=i: (lambda e: e.dma_start(
                    out=gls[i].rearrange("h d v -> d h v"),
                    in_=S32[i].t[:, :].rearrange("p (h v) -> p h v", h=4))))(), reads=[S32[i]])
                S.out_tokens.append(tok)
                tok = S.dma("sp", f"d_o4{i}", (lambda i=i: (lambda e: e.dma_start(
                    out=cvs[:, i * 16:(i + 1) * 16], in_=UH[i].t[:, :, :].rearrange("p c r -> p (c r)"))))(),
                            reads=[UH[i]])
                S.out_tokens.append(tok)

        S.wait_all("sp", S.out_tokens)

        for e in ENGS:
            for waits, fn, inc in S.q[e]:
                for k, v in waits:
                    S.sem(k)
                if inc is not None:
                    S.sem(inc[0])

        with nc.Block() as block:
            @block.tensor
            def _(h):
                S.replay("pe", h)

            @block.scalar
            def _(h):
                S.replay("act", h)

            @block.vector
            def _(h):
                S.replay("dve", h)

            @block.gpsimd
            def _(h):
                S.replay("pool", h)

            @block.sync
            def _(h):
                S.replay("sp", h)
    return nc


def _consts():
    c = np.zeros((128, CS_N), np.float32)
    c[:, CS_ID:CS_ID + 128] = np.eye(128, dtype=np.float32)
    j = np.arange(128)[:, None]
    i = np.arange(128)[None, :]
    ma = (j <= i).astype(np.float32)
    c[:, CS_MAP:CS_MAP + 128] = ma
    c[:, CS_MCP:CS_MCP + 128] = -ma / 16.0
    js = np.arange(64)[:, None]
    is_ = np.arange(64)[None, :]
    ms = ((js <= is_) & ((js // 32) == (is_ // 32))).astype(np.float32)
    c[:64, CS_MAS:CS_MAS + 64] = ms
    c[:64, CS_MCS:CS_MCS + 64] = -ms / 16.0
    return c


def _col(v):
    return np.ascontiguousarray(np.asarray(v, np.float32).reshape(-1, 128).T)


_NC_CACHE = {}


def run(inputs, n_main=8, n_pre=8, do_sample=True, seq_len=8192, trace=False):
    f = lambda k: np.asarray(inputs[k], np.float32)
    x_prompt, x_sample = f("x_prompt"), f("x_sample")
    c_prompt, c_sample = f("c_prompt"), f("c_sample")
    cache_conv, state_gla = f("cache_conv")[0], f("state_gla")[0]
    key = (n_main, n_pre, do_sample)
    if key not in _NC_CACHE:
        _NC_CACHE[key] = build(n_main, n_pre, do_sample)
    nc = _NC_CACHE[key]
    half = n_main * 512
    consts = _consts()
    w = {k: np.ascontiguousarray(f(k)[0]) for k in ("w_ada", "w_in", "w_o", "w_gate", "w_up", "w_down")}
    wf = np.ascontiguousarray(f("w_f")[0])
    bfv = np.ascontiguousarray(f("b_f")[0].reshape(1, 512))
    in_maps = []
    for core in range(8):
        b, hf = core // 2, core % 2
        pv = np.zeros((128, PV_N), np.float32)
        pv[:, PV_BADA:PV_BADA + 96] = _col(f("b_ada")[0])
        pv[:, PV_LN1G:PV_LN1G + 16] = _col(f("ln1_g")[0])
        pv[:, PV_LN1B:PV_LN1B + 16] = _col(f("ln1_b")[0])
        pv[:, PV_LN2G:PV_LN2G + 16] = _col(f("ln2_g")[0])
        pv[:, PV_LN2B:PV_LN2B + 16] = _col(f("ln2_b")[0])
        wc = f("w_conv")[0]
        for c in range(8):
            for i in range(3):
                pv[:, PV_WC + c * 3 + i] = wc[i, c * 128:(c + 1) * 128]
        pv[:, PV_GAIN:PV_GAIN + 2] = _col(f("gla_gain")[0])
        pv[:, PV_FLAG] = float(hf)
        cv = np.zeros((128, 48), np.float32)
        vecs = [c_prompt[b], c_sample[2 * core], c_sample[2 * core + 1]]
        for s, v in enumerate(vecs):
            cv[:, s::3] = _col(v)
        cc = np.zeros((128, 32), np.float32)
        for i in range(2):
            cci = cache_conv[2 * core + i]
            for c in range(8):
                for r in range(2):
                    cc[:, i * 16 + c * 2 + r] = cci[r, c * 128:(c + 1) * 128]
        m = {
            "xm": np.ascontiguousarray(x_prompt[b, hf * half:(hf + 1) * half]) if half > 0 else np.zeros((1, D), np.float32),
            "xp": np.ascontiguousarray(x_prompt[b, 0:max(n_pre * 512, 1)]),
            "xs": np.ascontiguousarray(x_sample[2 * core:2 * core + 2].reshape(64, D)),
            "cvec": cv, "pvec": pv, "consts": consts, "wf": wf, "bf": bfv, "cc": cc,
            "sgla": np.ascontiguousarray(state_gla[2 * core:2 * core + 2]),
        }
        m.update(w)
        in_maps.append(m)
    res = run_bass_kernel_spmd(nc, in_maps, core_ids=list(range(8)), trace=trace)
    rs = res.results
    B = x_prompt.shape[0]
    yp = np.zeros((B, 2 * half, D), np.float32)
    ys = np.zeros((16, 32, D), np.float32)
    conv_p = np.zeros((1, B, 2, 1024), np.float32)
    gla_p = np.zeros((1, B, 4, 128, 256), np.float32)
    conv_s = np.zeros((1, 16, 2, 1024), np.float32)
    gla_s = np.zeros((1, 16, 4, 128, 256), np.float32)

    def unconv(a):
        return np.ascontiguousarray(a.reshape(128, 8, 2).transpose(2, 1, 0).reshape(2, 1024))

    for core in range(8):
        b, hf = core // 2, core % 2
        r = rs[core]
        if half > 0:
            yp[b, hf * half:(hf + 1) * half] = r["yp"]
        ys[2 * core:2 * core + 2] = r["ys"].reshape(2, 32, D)
        if hf == 1:
            conv_p[0, b] = unconv(r["cvp"])
            gla_p[0, b] = r["glp"]
        for i in range(2):
            conv_s[0, 2 * core + i] = unconv(r["cvs"][:, i * 16:(i + 1) * 16])
            gla_s[0, 2 * core + i] = r["gls"][i]
    out = (yp, ys, conv_p, gla_p, conv_s, gla_s)
    if trace:
        return out, res
    return out


def kernel(**inputs):
    return run(inputs)
```

```python
from contextlib import ExitStack
import numpy as np
import concourse.bass as bass
import concourse.mybir as mybir
from concourse.bass_utils import run_bass_kernel_spmd

F32 = mybir.dt.float32
BF16 = mybir.dt.bfloat16
AF = mybir.ActivationFunctionType
ALU = mybir.AluOpType

D = 2048
KC = 16
DFF = 5632
FC = 44
ALPHA = 2.0 ** 0.25
LN_EPS = 1e-5
RMS_EPS = 1e-6
QSCALE_LN = float(np.log(128.0 ** -0.5))
C_HB, C_HC, C_HIN, C_Q, C_K, C_V, C_GO, C_FLR = 0, 1024, 2048, 3072, 3584, 4096, 5120, 6144
NWB = 4
WB_ELEMS = 4096

PV_BADA = 0
PV_LN1G = 96
PV_LN1B = 112
PV_LN2G = 128
PV_LN2B = 144
PV_WC = 160
PV_GAIN = 184
PV_FLAG = 186
PV_N = 187
CS_ID, CS_MAP, CS_MCP, CS_MAS, CS_MCS, CS_N = 0, 128, 256, 384, 448, 512

ENGS = ("pe", "act", "dve", "pool", "sp")
HP_ORDER = (0, 1)
FULL_BARRIER = False
EXP_SKIP = False
FORCE_SIGNAL = False
USE_WCACHE = True


class Buf:
    __slots__ = ("t", "lw", "rd", "name", "psum")

    def __init__(self, t, name="", psum=False):
        self.psum = psum
        self.t = t
        self.lw = None
        self.rd = {}
        self.name = name


class Sched:
    def __init__(self, nc, es):
        self.nc = nc
        self.es = es
        self.q = {e: [] for e in ENGS}
        self.cnt = {e: 0 for e in ENGS}
        self.pending = {e: False for e in ENGS}
        self.sems = {}
        self.waited = {e: {} for e in ENGS}
        self.dma_cnt = {}
        self.out_tokens = []

    def sem(self, key):
        if key not in self.sems:
            self.sems[key] = self.es.enter_context(self.nc.semaphore(key))
        return self.sems[key]

    def _deps(self, eng, reads, writes, extra):
        deps = {}

        def add(tok):
            if tok is None:
                return
            k, v = tok
            if deps.get(k, 0) < v:
                deps[k] = v

        for b in reads:
            add(b.lw)
            if b.psum:
                for k, v in b.rd.items():
                    if k != "e_" + eng:
                        add((k, v))
        for b in writes:
            add(b.lw)
            for k, v in b.rd.items():
                add((k, v))
        for t in extra:
            add(t)
        waits = []
        for k, v in deps.items():
            if k == "e_pe" and eng == "pe":
                continue
            if self.waited[eng].get(k, 0) >= v:
                continue
            self.waited[eng][k] = v
            waits.append((k, v))
        return waits

    def _register(self, tok, reads, writes):
        k, v = tok
        for b in reads:
            if b.rd.get(k, 0) < v:
                b.rd[k] = v
        for b in writes:
            b.lw = tok
            b.rd = {}

    def op(self, eng, fn, reads=(), writes=(), signal=True, extra=()):
        if FORCE_SIGNAL:
            signal = True
        waits = self._deps(eng, reads, writes, extra)
        k = "e_" + eng
        if signal:
            self.cnt[eng] += 1
            self.pending[eng] = False
            tok = (k, self.cnt[eng])
        else:
            self.pending[eng] = True
            tok = (k, self.cnt[eng] + 1)
        self.q[eng].append((waits, fn, (k, 1) if signal else None))
        self._register(tok, reads, writes)
        return tok

    def dma(self, eng, semkey, fn, reads=(), writes=(), extra=()):
        waits = self._deps(eng, reads, writes, extra)
        n = self.dma_cnt.get(semkey, 0) + 1
        self.dma_cnt[semkey] = n
        tok = (semkey, 16 * n)
        self.q[eng].append((waits, fn, (semkey, 16)))
        self._register(tok, reads, writes)
        return tok

    def wait_all(self, eng, toks):
        waits = self._deps(eng, (), (), toks)
        self.q[eng].append((waits, None, None))

    def replay(self, eng, h):
        for waits, fn, inc in self.q[eng]:
            for k, v in waits:
                h.wait_ge(self.sem(k), v)
            if fn is None:
                continue
            ins = fn(h)
            if inc is not None:
                ins.then_inc(self.sem(inc[0]), inc[1])


class TileDesc:
    pass


def build(n_main=8, n_pre=8, do_sample=True, dbg=False):
    TWP = 512
    nc = bass.Bass("TRN2", target_bir_lowering=False)
    NM = n_main * TWP
    NP = n_pre * TWP

    def din(name, shape, dt=F32):
        return nc.dram_tensor(name, list(shape), dt, kind="ExternalInput").ap()

    def dout(name, shape):
        return nc.dram_tensor(name, list(shape), F32, kind="ExternalOutput").ap()

    xm = din("xm", [max(NM, 1), D])
    xp = din("xp", [max(NP, 1), D])
    xs = din("xs", [64, D])
    cvec = din("cvec", [128, 48])
    pvec = din("pvec", [128, PV_N])
    consts = din("consts", [128, CS_N])
    wf_d = din("wf", [16, 512])
    bf_d = din("bf", [1, 512])
    cc_d = din("cc", [128, 32])
    sg_d = din("sgla", [2, 4, 128, 256])
    w_ada = din("w_ada", [D, 6 * D])
    w_in = din("w_in", [D, 6160])
    w_o = din("w_o", [D, D])
    w_gate = din("w_gate", [D, DFF])
    w_up = din("w_up", [D, DFF])
    w_down = din("w_down", [DFF, D])

    yp = dout("yp", [max(NM, 1), D])
    ys = dout("ys", [64, D])
    cvp = dout("cvp", [128, 16])
    glp = dout("glp", [4, 128, 256])
    cvs = dout("cvs", [128, 32])
    gls = dout("gls", [2, 4, 128, 256])
    if dbg:
        dbgmix = nc.dram_tensor("dbgmix", [16, 128, 512], BF16, kind="ExternalOutput").ap()
        dbgh = nc.dram_tensor("dbgh", [16, 128, 512], BF16, kind="ExternalOutput").ap()

    with ExitStack() as es:
        S = Sched(nc, es)

        def sbt(name, shape, dt):
            return es.enter_context(nc.sbuf_tensor(name, list(shape), dt))

        def sb(name, shape, dt):
            t = sbt(name, shape, dt)
            return Buf(t[tuple(slice(None) for _ in shape)], name)

        WB = [sb(f"wb{i}", [128, WB_ELEMS], BF16) for i in range(NWB)]
        XIO = [sb(f"xio{i}", [128, D], F32) for i in range(2)]
        XL = XIO
        OS = XIO
        XPF = sb("xpf", [128, D], F32)
        xpref = {}
        H = [sb(f"h{k}", [128, 512], BF16) for k in range(KC)]
        R = [sb(f"r{k}", [128, 512], F32) for k in range(KC)]
        MIX = [sb(f"mix{k}", [128, 512], BF16) for k in range(KC)]
        ARENA = sbt("arena", [128, FC * 512], BF16)

        def carve(name, off, n, dt=BF16, pat=None, **kw):
            ap = ARENA[:, off:off + n]
            if dt == F32:
                ap = ap.bitcast(F32)
            if pat is not None:
                ap = ap.rearrange(pat, **kw)
            return Buf(ap, name)

        ACTT = [carve(f"actt{j}", j * 512, 512) for j in range(FC)]
        EQ = carve("eq", 0, 4096, F32, "p (h n) -> p h n", h=4)
        EK = carve("ek", 4096, 4096, F32, "p (h n) -> p h n", h=4)
        QT = carve("qt", 8192, 2048, BF16, "p (h n) -> p h n", h=4)
        KT = carve("kt", 10240, 2048, BF16, "p (h n) -> p h n", h=4)
        VT = carve("vt", 12288, 4096, BF16, "p (s n) -> p s n", s=4)
        SG = carve("sg", 16384, 4096, BF16, "p (m n) -> p m n", m=8)
        FLRB = carve("flrb", 20480, 512)
        AM = [carve(f"am{i}", 20992 + i * 256, 256, BF16, "p (h n) -> p h n", h=2) for i in range(2)]
        KTOK = [carve(f"ktok{i}", 21504 + i * 256, 256) for i in range(2)]
        GLA_TMPS = [EQ, EK, QT, KT, VT, SG, FLRB] + AM + KTOK
        TMP32 = [sb(f"tmp32_{i}", [128, 512], F32) for i in range(4)]
        TMP16 = [sb(f"tmp16_{i}", [128, 512], BF16) for i in range(4)]
        E1 = LL = TN = SGT = HCS = CV = TMP32
        SQ = SQB = XB = TMP16
        DEC = sb("dec", [128, 8, 4], F32)
        RINV = [sb(f"rinv{i}", [128, 256], F32) for i in range(2)]
        UB = [sb(f"ub{i}", [128, 520], F32) for i in range(2)]
        RSTD = sb("rstd", [128, 512], F32)
        NMR = sb("nmr", [128, 512], F32)
        MSQ = NMR
        S32 = [sb(f"s32_{i}", [128, 1024], F32) for i in range(2)]
        SBF = [[sb(f"sbf_{i}_{hp}", [128, 512], BF16) for hp in range(2)] for i in range(2)]
        UH = [sb(f"uh{i}", [128, 8, 2], F32) for i in range(2)]
        CST = sb("cst", [128, CS_N], F32)
        IDB = sb("idb", [128, 128], BF16)
        ONESB = sb("onesb", [128, 128], BF16)
        PV = sb("pv", [128, PV_N], F32)
        CVEC = sb("cvecs", [128, 48], F32)
        SCB = sb("scb", [128, 48], BF16)
        WFB = sb("wfb", [16, 512], BF16)
        BFB = sb("bfb", [1, 512], BF16)
        MOD = sb("mod", [128, 3, 96], F32)
        SC1P = sb("sc1p", [128, 3, 16], F32)
        SC2P = sb("sc2p", [128, 3, 16], F32)
        GA2 = sb("ga2", [128, 3, 16], F32)
        BA2 = sb("ba2", [128, 3, 16], F32)
        A1 = sb("a1", [128, 16], F32)
        B1 = sb("b1", [128, 16], F32)

        def alias_barrier(src, dst):
            toks = {}
            for b in src:
                for tk in ([b.lw] if b.lw else []) + list(b.rd.items()):
                    if toks.get(tk[0], 0) < tk[1]:
                        toks[tk[0]] = tk[1]
            for b in dst:
                for k, v in toks.items():
                    if b.rd.get(k, 0) < v:
                        b.rd[k] = v

        def ps(name):
            t = es.enter_context(nc.psum_tensor(name, [128, 512], F32))
            return Buf(t[:, :], name, psum=True)

        P = [ps(f"p{i}") for i in range(8)]
        rr = {"PR": 0, "PG": 0, "ALL": 0}

        def bank(pool):
            live = modst["live"]
            if pool == "PR":
                b = P[rr["PR"] % 4]
            elif pool == "PG":
                b = P[4 + rr["PG"] % (3 if live else 4)]
            else:
                b = P[rr["ALL"] % (7 if live else 8)]
            rr[pool] += 1
            return b

        ident = lambda n: CST.t[:n, CS_ID:CS_ID + n]

        S.dma("sp", "d_cst", lambda e: e.dma_start(out=CST.t[:, :], in_=consts[:, :]), writes=[CST])
        S.dma("sp", "d_pv", lambda e: e.dma_start(out=PV.t[:, :], in_=pvec[:, :]), writes=[PV])
        S.dma("sp", "d_cv", lambda e: e.dma_start(out=CVEC.t[:, :], in_=cvec[:, :]), writes=[CVEC])
        S.dma("pool", "d_wf", lambda e: e.dma_start(out=WFB.t[:, :], in_=wf_d[:, :]), writes=[WFB])
        S.dma("pool", "d_bf", lambda e: e.dma_start(out=BFB.t[:, :], in_=bf_d[:, :]), writes=[BFB])
        S.op("act", lambda e: e.activation(out=IDB.t[:, :], in_=CST.t[:, CS_ID:CS_ID + 128], func=AF.Copy),
             reads=[CST], writes=[IDB])
        S.op("dve", lambda e: e.memset(ONESB.t[:, :], 1.0), writes=[ONESB])
        S.op("act", lambda e: e.activation(out=SCB.t[:, :], in_=CVEC.t[:, :], func=AF.Silu), reads=[CVEC], writes=[SCB])

        wstate = {"i": 0}
        NCACHE = 128
        wcache = nc.dram_tensor("wcache", [NCACHE, 128, WB_ELEMS], BF16, kind="Internal").ap()
        cache_ix = {}
        WNAMES = {id(w_in): "in", id(w_o): "o", id(w_gate): "g", id(w_up): "u", id(w_down): "d", id(w_ada): "ada"}

        def wblock(W, k0, nk, c0, ncols):
            si = wstate["i"] % NWB
            slot = WB[si]
            key = f"d_wb{si}"
            wstate["i"] += 1
            assert nk * ncols <= WB_ELEMS
            n = nk * ncols
            ck = (WNAMES[id(W)], k0, nk, c0, ncols)
            if USE_WCACHE and ck in cache_ix:
                ci, cbuf = cache_ix[ck]
                S.dma("pool", key, lambda e: e.dma_start(out=slot.t[:, 0:n], in_=wcache[ci, :, 0:n]),
                      reads=[cbuf], writes=[slot])
                return slot
            src = W[k0 * 128:(k0 + nk) * 128, c0:c0 + ncols].rearrange("(k p) c -> p k c", p=128)
            dst = slot.t[:, 0:n].rearrange("p (k c) -> p k c", k=nk)
            S.dma("pool", key, lambda e: e.dma_start(out=dst, in_=src), writes=[slot])
            wstate["new"] = wstate.get("new", 0) + 1
            if (USE_WCACHE and WNAMES[id(W)] != "ada" and len(cache_ix) < NCACHE
                    and wstate["new"] % wstate.get("every", 1) == 0):
                ci = len(cache_ix)
                cbuf = Buf(None, f"wc{ci}")
                cache_ix[ck] = (ci, cbuf)
                S.dma("sp", f"d_ws{si}", lambda e: e.dma_start(out=wcache[ci, :, 0:n], in_=slot.t[:, 0:n]),
                      reads=[slot], writes=[cbuf])
            return slot

        def mm_group(out_ap, outbuf, pairs, reads, tail_signal=True):
            n = len(pairs)
            tok = None
            for i, (l, r) in enumerate(pairs):
                tok = S.op("pe", (lambda l=l, r=r, i=i: (lambda e: e.matmul(out_ap, lhsT=l, rhs=r, start=(i == 0),
                                                                           stop=(i == n - 1))))(),
                           reads=reads if i == 0 else (), writes=[outbuf] if i == 0 else (),
                           signal=(i == n - 1) and tail_signal)
            S._register(tok, reads, [outbuf])
            return tok

        PM = P[7]
        modst = {"live": True}

        def mod_block(blk):
            slot = wblock(w_ada, 0, KC, blk * 256, 256)
            for mmi in range(2):
                m = blk * 2 + mmi
                pairs = [(slot.t[:, kc * 256 + mmi * 128: kc * 256 + (mmi + 1) * 128], SCB.t[:, kc * 3:kc * 3 + 3])
                         for kc in range(KC)]
                mm_group(PM.t[:, m * 3:m * 3 + 3], PM, pairs, reads=[slot, SCB])

        def mod_finish(lo, hi):
            for s in range(3):
                S.op("dve", (lambda s=s: (lambda e: e.tensor_tensor(
                    out=MOD.t[:, s, lo:hi], in0=PM.t[:, 0:288].rearrange("p (m s) -> p m s", s=3)[:, lo:hi, s],
                    in1=PV.t[:, PV_BADA + lo:PV_BADA + hi], op=ALU.add)))(), reads=[PM, PV], writes=[MOD])

        def mod_derive():
            S.op("dve", lambda e: e.tensor_scalar(out=SC2P.t[:, :, :], in0=MOD.t[:, :, 64:80], scalar1=1.0, scalar2=None,
                                                  op0=ALU.add), reads=[MOD], writes=[SC2P])
            for s in range(3):
                S.op("dve", (lambda s=s: (lambda e: e.tensor_tensor(out=GA2.t[:, s, :], in0=SC2P.t[:, s, :],
                                                                   in1=PV.t[:, PV_LN1G:PV_LN1G + 16], op=ALU.mult)))(),
                     reads=[SC2P, PV], writes=[GA2])
                S.op("dve", (lambda s=s: (lambda e: e.tensor_tensor(out=BA2.t[:, s, :], in0=SC2P.t[:, s, :],
                                                                   in1=PV.t[:, PV_LN1B:PV_LN1B + 16], op=ALU.mult)))(),
                     reads=[SC2P, PV], writes=[BA2])
                S.op("dve", (lambda s=s: (lambda e: e.tensor_tensor(out=BA2.t[:, s, :], in0=BA2.t[:, s, :],
                                                                   in1=MOD.t[:, s, 48:64], op=ALU.add)))(),
                     reads=[BA2, MOD], writes=[BA2])
            S.op("dve", lambda e: e.tensor_scalar(out=A1.t[:, :], in0=PV.t[:, PV_LN1G:PV_LN1G + 16], scalar1=ALPHA,
                                                  scalar2=None, op0=ALU.mult), reads=[PV], writes=[A1])
            S.op("dve", lambda e: e.tensor_scalar(out=B1.t[:, :], in0=PV.t[:, PV_LN1B:PV_LN1B + 16], scalar1=ALPHA,
                                                  scalar2=None, op0=ALU.mult), reads=[PV], writes=[B1])

        for blk in range(16):
            mod_block(blk)
        mod_finish(0, 32)
        S.op("dve", lambda e: e.tensor_scalar(out=SC1P.t[:, :, :], in0=MOD.t[:, :, 16:32], scalar1=1.0, scalar2=None,
                                              op0=ALU.add), reads=[MOD], writes=[SC1P])
        mod_next = {"blk": 16}

        def mod_some(n):
            for _ in range(n):
                if mod_next["blk"] < 48:
                    mod_block(mod_next["blk"])
                    mod_next["blk"] += 1
            if mod_next["blk"] == 48 and modst["live"]:
                mod_finish(32, 96)
                mod_derive()
                modst["live"] = False

        SH1 = lambda s, kc: MOD.t[:, s, kc:kc + 1]
        G1 = lambda s, m: MOD.t[:, s, 32 + m:33 + m]
        G2M = lambda s, m: MOD.t[:, s, 80 + m:81 + m]

        for i in range(2):
            S.op("dve", (lambda i=i: (lambda e: e.memset(S32[i].t[:, :], 0.0)))(), writes=[S32[i]])
            S.op("dve", (lambda i=i: (lambda e: e.memset(UH[i].t[:, :, :], 0.0)))(), writes=[UH[i]])
            for hp in range(2):
                S.op("dve", (lambda i=i, hp=hp: (lambda e: e.memset(SBF[i][hp].t[:, :], 0.0)))(), writes=[SBF[i][hp]])

        xl_i = {"i": 0}
        os_i = {"i": 0}
        rot = {}

        def nxt(lst, key):
            key = id(lst)
            i = rot.get(key, 0)
            rot[key] = i + 1
            return lst[i % len(lst)]

        def prefetch_x(T):
            SW = T.SW
            src = T.x[T.r0:T.r0 + SW, :]
            S.dma("sp", "d_xpf", lambda e: e.dma_start(out=XPF.t[:SW, :], in_=src), writes=[XPF])
            xpref[T.tid] = True

        def stage0_groups(T):
            TW, SW = T.TW, T.SW
            out = []
            for s in range(T.NSUB):
                holder = {}
                for g in range(4):
                    def grp(s=s, g=g, holder=holder):
                        if g == 0:
                            if s == 0 and xpref.get(T.tid):
                                holder["xl"] = XPF
                            else:
                                xl = XL[xl_i["i"] % 2]
                                key = f"d_xl{xl_i['i'] % 2}"
                                xl_i["i"] += 1
                                src = T.x[T.r0 + s * SW:T.r0 + (s + 1) * SW, :]
                                S.dma("sp", key, lambda e: e.dma_start(out=xl.t[:SW, :], in_=src), writes=[xl])
                                holder["xl"] = xl
                        xl = holder["xl"]
                        bk = bank("ALL")
                        for j in range(4):
                            kc = g * 4 + j
                            S.op("pe", (lambda j=j, kc=kc: (lambda e: e.transpose(
                                out=bk.t[:, j * SW:(j + 1) * SW], in_=xl.t[:SW, kc * 128:(kc + 1) * 128],
                                identity=ident(SW))))(), reads=[xl, CST], writes=[bk], signal=(j == 3))
                        for j in range(4):
                            kc = g * 4 + j
                            for (c0, n, sidx) in T.segs:
                                lo, hi = max(c0, s * SW), min(c0 + n, (s + 1) * SW)
                                if lo >= hi:
                                    continue
                                src_ap = bk.t[:, j * SW + lo - s * SW: j * SW + hi - s * SW]
                                if g % 2 == 0:
                                    S.op("act", (lambda kc=kc, lo=lo, hi=hi, sidx=sidx, src_ap=src_ap: (
                                        lambda e: e.activation(
                                            out=H[kc].t[:, lo:hi], in_=src_ap, func=AF.Identity,
                                            bias=SH1(sidx, kc), scale=SC1P.t[:, sidx, kc:kc + 1])))(),
                                         reads=[bk, MOD, SC1P], writes=[H[kc]])
                                    if T.kind != "pre":
                                        S.op("act", (lambda kc=kc, lo=lo, hi=hi, src_ap=src_ap: (
                                            lambda e: e.activation(out=R[kc].t[:, lo:hi], in_=src_ap, func=AF.Copy,
                                                                   scale=ALPHA)))(), reads=[bk], writes=[R[kc]])
                                else:
                                    S.op("dve", (lambda kc=kc, lo=lo, hi=hi, sidx=sidx, src_ap=src_ap: (
                                        lambda e: e.tensor_scalar(
                                            out=H[kc].t[:, lo:hi], in0=src_ap, scalar1=SC1P.t[:, sidx, kc:kc + 1],
                                            scalar2=SH1(sidx, kc), op0=ALU.mult, op1=ALU.add)))(),
                                         reads=[bk, MOD, SC1P], writes=[H[kc]])
                                    if T.kind != "pre":
                                        S.op("dve", (lambda kc=kc, lo=lo, hi=hi, src_ap=src_ap: (
                                            lambda e: e.tensor_scalar(out=R[kc].t[:, lo:hi], in0=src_ap, scalar1=ALPHA,
                                                                      scalar2=None, op0=ALU.mult)))(),
                                             reads=[bk], writes=[R[kc]])
                    out.append(grp)
            return out

        def stage0(T):
            if getattr(T, "s0done", False):
                return
            for grp in stage0_groups(T):
                grp()
            T.s0done = True

        def proj_fm(W, c0, ncols, T, consume, mrows_last=None):
            TW = T.TW
            nb = (ncols + 255) // 256
            mi = 0
            for b in range(nb):
                bc = min(256, ncols - b * 256)
                slot = wblock(W, 0, KC, c0 + b * 256, bc)
                for mmi in range((bc + 127) // 128):
                    mr = min(128, bc - mmi * 128)
                    bk = bank("PR")
                    pairs = [(slot.t[:, kc * bc + mmi * 128: kc * bc + mmi * 128 + mr], T.rhs(kc)) for kc in range(KC)]
                    mm_group(bk.t[:mr, :TW], bk, pairs, reads=[slot] + T.rhs_bufs)
                    consume(mi, bk)
                    mi += 1

        def proj_groups(W, c0, ncols, T, consume):
            TW = T.TW
            out = []
            nb = (ncols + 255) // 256
            mi = 0
            for b in range(nb):
                bc = min(256, ncols - b * 256)
                holder = {}
                for mmi in range((bc + 127) // 128):
                    mr = min(128, bc - mmi * 128)

                    def g(b=b, bc=bc, mmi=mmi, mr=mr, mi=mi, holder=holder):
                        if mmi == 0:
                            holder["slot"] = wblock(W, 0, KC, c0 + b * 256, bc)
                        slot = holder["slot"]
                        bk = bank("PR")
                        pairs = [(slot.t[:, kc * bc + mmi * 128: kc * bc + mmi * 128 + mr], T.rhs(kc))
                                 for kc in range(KC)]
                        mm_group(bk.t[:mr, :TW], bk, pairs, reads=[slot] + T.rhs_bufs)
                        consume(mi, bk)
                    out.append(g)
                    mi += 1
            return out

        def stage1(T, extra=None):
            TW, SW = T.TW, T.SW
            main = T.kind != "pre"
            nch = len(T.chunks)
            NSUB = T.NSUB
            alias_barrier(ACTT, GLA_TMPS)
            st = {}

            def c_flr(mi, bk):
                S.op("act", lambda e: e.activation(out=FLRB.t[:16, :TW], in_=bk.t[:16, :TW], func=AF.Copy),
                     reads=[bk], writes=[FLRB])
            p_flr = proj_groups(w_in, C_FLR, 16, T, c_flr)[0]

            def p_z(s):
                pz = bank("PG")
                S.op("pe", lambda e: e.matmul(pz.t[:SW, :512], lhsT=FLRB.t[:16, s * SW:(s + 1) * SW],
                                              rhs=WFB.t[:16, :512], start=True, stop=False),
                     reads=[FLRB, WFB], writes=[pz], signal=False)
                S.op("pe", lambda e: e.matmul(pz.t[:SW, :512], lhsT=ONESB.t[0:1, :SW],
                                              rhs=BFB.t[0:1, :512], start=False, stop=True),
                     reads=[ONESB, BFB], writes=[pz])
                e1 = nxt(E1, "e1")
                ll = nxt(LL, "ll")
                S.op("act", lambda e: e.activation(out=e1.t[:SW, :], in_=pz.t[:SW, :512], func=AF.Exp, scale=-1.0),
                     reads=[pz], writes=[e1])
                S.op("act", lambda e: e.activation(out=ll.t[:SW, :], in_=e1.t[:SW, :], func=AF.Ln, bias=1.0),
                     reads=[e1], writes=[ll])
                st[("ll", s)] = ll

            def p_b(s):
                ll = st[("ll", s)]
                pb = bank("PG")
                for h in range(4):
                    S.op("pe", (lambda h=h: (lambda e: e.matmul(
                        pb.t[:, h * SW:(h + 1) * SW], lhsT=ll.t[:SW, h * 128:(h + 1) * 128],
                        rhs=T.maskC, start=True, stop=True)))(), reads=[ll, CST], writes=[pb], signal=(h == 3))
                pb3 = pb.t[:, 0:4 * SW].rearrange("p (h n) -> p h n", h=4)
                if main:
                    S.op("act", lambda e: e.activation(out=EQ.t[:, :, s * SW:(s + 1) * SW], in_=pb3, func=AF.Exp,
                                                       bias=QSCALE_LN), reads=[pb], writes=[EQ])
                S.op("act", lambda e: e.activation(out=EK.t[:, :, s * SW:(s + 1) * SW], in_=pb3, func=AF.Exp,
                                                   scale=-1.0), reads=[pb], writes=[EK])
                for ic, (c0, CL, sid) in enumerate(T.chunks):
                    ci = s * nch + ic
                    S.op("act", (lambda c0=c0, CL=CL, ci=ci: (lambda e: e.activation(
                        out=DEC.t[:, ci, :], in_=pb3[:, :, c0 + CL - 1], func=AF.Exp)))(), reads=[pb], writes=[DEC])

            def c_q(mi, bk):
                S.op("dve", lambda e: e.tensor_tensor(out=QT.t[:, mi, :TW], in0=bk.t[:, :TW], in1=EQ.t[:, mi, :TW],
                                                      op=ALU.mult), reads=[bk, EQ], writes=[QT])

            def c_k(mi, bk):
                S.op("dve", lambda e: e.tensor_tensor(out=KT.t[:, mi, :TW], in0=bk.t[:, :TW], in1=EK.t[:, mi, :TW],
                                                      op=ALU.mult), reads=[bk, EK], writes=[KT])

            def c_go(mi, bk):
                st[("go", mi)] = True
                S.op("act", lambda e: e.activation(out=SG.t[:, mi, :TW], in_=bk.t[:, :TW], func=AF.Silu),
                     reads=[bk], writes=[SG])

            f_k = proj_groups(w_in, C_K, 512, T, c_k)
            f_q = proj_groups(w_in, C_Q, 512, T, c_q) if main else []
            f_go = proj_groups(w_in, C_GO, 1024, T, c_go) if main else []

            f_v = []
            for b in range(4):
                holder = {}
                for s in range(NSUB):
                    def g(b=b, s=s, holder=holder):
                        if s == 0:
                            holder["slot"] = wblock(w_in, 0, KC, C_V + b * 256, 256)
                        slot = holder["slot"]
                        bk = bank("PR")
                        pairs = [(H[kc].t[:, s * SW:(s + 1) * SW], slot.t[:, kc * 256:(kc + 1) * 256])
                                 for kc in range(KC)]
                        mm_group(bk.t[:SW, :256], bk, pairs, reads=[slot] + H)
                        if s % 2 == 0:
                            S.op("act", lambda e: e.activation(out=VT.t[:SW, s, b * 256:(b + 1) * 256],
                                                               in_=bk.t[:SW, :256], func=AF.Copy),
                                 reads=[bk], writes=[VT])
                        else:
                            S.op("dve", lambda e: e.tensor_copy(out=VT.t[:SW, s, b * 256:(b + 1) * 256],
                                                                in_=bk.t[:SW, :256]), reads=[bk], writes=[VT])
                    f_v.append(g)

            f_conv = []
            if main:
                for c in range(8):
                    cst = {}
                    for nm, cb in (("hc", C_HC), ("hin", C_HIN), ("hb", C_HB)):
                        def g(c=c, nm=nm, cb=cb, cst=cst):
                            slot = wblock(w_in, 0, KC, cb + c * 128, 128)
                            bk = bank("PR")
                            pairs = [(slot.t[:, kc * 128:(kc + 1) * 128], T.rhs(kc)) for kc in range(KC)]
                            mm_group(bk.t[:, :TW], bk, pairs, reads=[slot] + T.rhs_bufs)
                            cst[nm] = bk
                            if nm == "hc":
                                hcs = nxt(HCS, "hcs")
                                cst["hcs"] = hcs
                                S.op("act", lambda e: e.activation(out=hcs.t[:, :TW], in_=bk.t[:, :TW], func=AF.Copy),
                                     reads=[bk], writes=[hcs])
                            elif nm == "hin":
                                hcs = cst["hcs"]
                                ub = nxt(UB, "ub")
                                cst["ub"] = ub
                                off = 0
                                for (c0, n, sidx) in T.segs:
                                    uv = ub.t[:, off:off + n + 2]
                                    off += n + 2
                                    uh = UH[T.slot[sidx]]
                                    S.op("dve", (lambda uv=uv, uh=uh: (lambda e: e.tensor_copy(
                                        out=uv[:, 0:2], in_=uh.t[:, c, :])))(), reads=[uh], writes=[ub])
                                    S.op("dve", (lambda uv=uv, c0=c0, n=n: (lambda e: e.tensor_tensor(
                                        out=uv[:, 2:2 + n], in0=bk.t[:, c0:c0 + n], in1=hcs.t[:, c0:c0 + n],
                                        op=ALU.mult)))(), reads=[bk, hcs], writes=[ub])
                                    S.op("dve", (lambda uv=uv, n=n, uh=uh: (lambda e: e.tensor_copy(
                                        out=uh.t[:, c, :], in_=uv[:, n:n + 2])))(), reads=[ub], writes=[uh])
                            else:
                                ub = cst["ub"]
                                cv = nxt(CV, "cv")
                                off = 0
                                for (c0, n, sidx) in T.segs:
                                    uv = ub.t[:, off:off + n + 2]
                                    off += n + 2
                                    S.op("act", (lambda uv=uv, c0=c0, n=n: (lambda e: e.activation(
                                        out=cv.t[:, c0:c0 + n], in_=uv[:, 0:n], func=AF.Identity,
                                        scale=PV.t[:, PV_WC + c * 3:PV_WC + c * 3 + 1])))(), reads=[ub, PV], writes=[cv])
                                    for i in (1, 2):
                                        S.op("dve", (lambda uv=uv, c0=c0, n=n, i=i: (lambda e: e.scalar_tensor_tensor(
                                            out=cv.t[:, c0:c0 + n], in0=uv[:, i:i + n],
                                            scalar=PV.t[:, PV_WC + c * 3 + i:PV_WC + c * 3 + i + 1],
                                            in1=cv.t[:, c0:c0 + n], op0=ALU.mult, op1=ALU.add)))(),
                                             reads=[ub, PV, cv], writes=[cv])
                                    S.op("dve", (lambda c0=c0, n=n: (lambda e: e.tensor_tensor(
                                        out=MIX[c].t[:, c0:c0 + n], in0=bk.t[:, c0:c0 + n], in1=cv.t[:, c0:c0 + n],
                                        op=ALU.mult)))(), reads=[bk, cv], writes=[MIX[c]])
                        f_conv.append(g)

            def s1(s, hp):
                g0 = P[4]
                g0bf = g0.t[:, :].bitcast(BF16)
                if main:
                    for hh in range(2):
                        h = 2 * hp + hh
                        S.op("pe", (lambda hh=hh, h=h: (lambda e: e.matmul(
                            g0.t[:SW, hh * SW:(hh + 1) * SW], lhsT=KT.t[:, h, s * SW:(s + 1) * SW],
                            rhs=QT.t[:, h, s * SW:(s + 1) * SW], start=True, stop=True)))(),
                             reads=[KT, QT], writes=[g0], signal=False)
                for hh in range(2):
                    h = 2 * hp + hh
                    S.op("pe", (lambda hh=hh, h=h: (lambda e: e.transpose(
                        out=g0bf[:SW, 512 + hh * 128: 512 + (hh + 1) * 128], in_=KT.t[:, h, s * SW:(s + 1) * SW],
                        identity=IDB.t[:, :])))(), reads=[KT, IDB], writes=[g0], signal=(hh == 1))
                ktok = nxt(KTOK, "ktok")
                st[("ktok", s, hp)] = ktok
                S.op("dve", lambda e: e.tensor_copy(out=ktok.t[:SW, :], in_=g0bf[:SW, 512:768]),
                     reads=[g0], writes=[ktok])
                if main:
                    am = nxt(AM, "am")
                    st[("am", s, hp)] = am
                    for hh in range(2):
                        S.op("dve", (lambda hh=hh: (lambda e: e.tensor_tensor(
                            out=am.t[:SW, hh, :SW], in0=g0.t[:SW, hh * SW:(hh + 1) * SW], in1=T.maskA,
                            op=ALU.mult)))(), reads=[g0, CST], writes=[am])

            def s2(s, hp):
                ktok = st[("ktok", s, hp)]
                if main:
                    am = st[("am", s, hp)]
                    g1 = P[5]
                    st[("g1", s, hp)] = g1
                    for hh in range(2):
                        h = 2 * hp + hh
                        for ee in range(2):
                            blk = (hh * 2 + ee) * SW
                            S.op("pe", (lambda blk=blk, h=h, ee=ee, hh=hh: (lambda e: e.matmul(
                                g1.t[:, blk:blk + SW], lhsT=VT.t[:SW, s, h * 256 + ee * 128: h * 256 + (ee + 1) * 128],
                                rhs=am.t[:SW, hh, :SW], start=True, stop=False, skip_group_check=True)))(),
                                 reads=[VT, am], writes=[g1], signal=False)
                            for ic, (c0, CL, sid) in enumerate(T.chunks):
                                last = (ic == nch - 1) and hh == 1 and ee == 1
                                S.op("pe", (lambda blk=blk, c0=c0, CL=CL, sid=sid, h=h, hh=hh, ee=ee, ic=ic:
                                            (lambda e: e.matmul(
                                                g1.t[:, blk + c0: blk + c0 + CL],
                                                lhsT=SBF[sid][hp].t[:, hh * 256 + ee * 128: hh * 256 + (ee + 1) * 128],
                                                rhs=QT.t[:, h, s * SW + c0: s * SW + c0 + CL], start=False,
                                                stop=(ic == nch - 1), skip_group_check=True)))(),
                                     reads=[SBF[sid][hp], QT], writes=[g1], signal=last)
                for ic, (c0, CL, sid) in enumerate(T.chunks):
                    ci = s * nch + ic
                    g2 = P[6]
                    for hh in range(2):
                        h = 2 * hp + hh
                        S.op("pe", (lambda hh=hh, h=h, c0=c0, CL=CL, g2=g2: (lambda e: e.matmul(
                            g2.t[:, hh * 256:(hh + 1) * 256], lhsT=ktok.t[c0:c0 + CL, hh * 128:(hh + 1) * 128],
                            rhs=VT.t[c0:c0 + CL, s, h * 256:(h + 1) * 256], start=True, stop=True)))(),
                             reads=[ktok, VT], writes=[g2], signal=(hh == 1))
                    for hh in range(2):
                        h = 2 * hp + hh
                        sl = S32[sid].t[:, h * 256:(h + 1) * 256]
                        S.op("dve", (lambda sl=sl, ci=ci, h=h: (lambda e: e.tensor_scalar(
                            out=sl, in0=sl, scalar1=DEC.t[:, ci, h:h + 1], scalar2=None, op0=ALU.mult)))(),
                             reads=[S32[sid], DEC], writes=[S32[sid]])
                        S.op("dve", (lambda sl=sl, ci=ci, h=h, g2=g2, hh=hh: (lambda e: e.scalar_tensor_tensor(
                            out=sl, in0=g2.t[:, hh * 256:(hh + 1) * 256], scalar=DEC.t[:, ci, h:h + 1], in1=sl,
                            op0=ALU.mult, op1=ALU.add)))(), reads=[g2, S32[sid], DEC], writes=[S32[sid]])
                    if main:
                        S.op("act", (lambda sid=sid: (lambda e: e.activation(
                            out=SBF[sid][hp].t[:, :], in_=S32[sid].t[:, hp * 512:(hp + 1) * 512], func=AF.Copy)))(),
                             reads=[S32[sid]], writes=[SBF[sid][hp]])
                if main:
                    sq = nxt(SQ, "sq")
                    st[("sq", s, hp)] = sq
                    S.op("act", lambda e: e.activation(out=sq.t[:, :4 * SW], in_=g1.t[:, :4 * SW], func=AF.Square),
                         reads=[g1], writes=[sq])

            def s3(s, hp):
                if not main:
                    return
                while not all(st.get(("go", (2 * hp + hh) * 2 + ee)) for hh in range(2) for ee in range(2)):
                    F(1)
                g1 = st[("g1", s, hp)]
                sq = st[("sq", s, hp)]
                g3 = P[7]
                for hh in range(2):
                    for ee in range(2):
                        S.op("pe", (lambda hh=hh, ee=ee: (lambda e: e.matmul(
                            g3.t[:, hh * SW:(hh + 1) * SW], lhsT=ONESB.t[:, :],
                            rhs=sq.t[:, (hh * 2 + ee) * SW:(hh * 2 + ee + 1) * SW], start=(ee == 0),
                            stop=(ee == 1))))(), reads=[sq, ONESB], writes=[g3], signal=(hh == 1 and ee == 1))
                rinv = nxt(RINV, "rinv")
                S.op("act", lambda e: e.activation(out=rinv.t[:, :2 * SW], in_=g3.t[:, :2 * SW], func=AF.Ln,
                                                   scale=1.0 / 256.0, bias=RMS_EPS), reads=[g3], writes=[rinv])
                S.op("act", lambda e: e.activation(out=rinv.t[:, :2 * SW], in_=rinv.t[:, :2 * SW], func=AF.Exp,
                                                   scale=-0.5), reads=[rinv], writes=[rinv])
                tn = nxt(TN, "tn")
                for hh in range(2):
                    h = 2 * hp + hh
                    for ee in range(2):
                        blk = (hh * 2 + ee) * SW
                        S.op("dve", (lambda blk=blk, ee=ee, hh=hh: (lambda e: e.scalar_tensor_tensor(
                            out=tn.t[:, blk:blk + SW], in0=g1.t[:, blk:blk + SW],
                            scalar=PV.t[:, PV_GAIN + ee:PV_GAIN + ee + 1],
                            in1=rinv.t[:, hh * SW:(hh + 1) * SW], op0=ALU.mult, op1=ALU.mult)))(),
                             reads=[g1, PV, rinv], writes=[tn])
                        mchunk = 8 + h * 2 + ee
                        S.op("dve", (lambda blk=blk, mchunk=mchunk, h=h, ee=ee: (lambda e: e.tensor_tensor(
                            out=MIX[mchunk].t[:, s * SW:(s + 1) * SW], in0=tn.t[:, blk:blk + SW],
                            in1=SG.t[:, h * 2 + ee, s * SW:(s + 1) * SW], op=ALU.mult)))(),
                             reads=[tn, SG], writes=[MIX[mchunk]])

            fill = []

            def F(n):
                for _ in range(n):
                    if fill:
                        fill.pop(0)()

            p_flr()
            fill.extend(f_v)
            F(2)
            p_z(0)
            F(2)
            for s in range(NSUB):
                if s + 1 < NSUB:
                    p_z(s + 1)
                p_b(s)
                F(2)
            F(len(fill))
            for g in f_k:
                g()
            for g in f_q:
                g()
            for g in f_go[:4]:
                g()
            fill.extend(f_go[4:])
            fill.extend(f_conv)
            if extra:
                fill.extend(extra)
            its = [(s, hp) for s in range(NSUB) for hp in HP_ORDER]
            nf = max(1, len(fill) // (2 * len(its) + 1)) if fill else 0
            F(nf)
            s1(*its[0])
            for i, it in enumerate(its):
                F(nf)
                s2(*it)
                if i + 1 < len(its):
                    s1(*its[i + 1])
                F(nf)
                s3(*it)
            F(len(fill))

        def ln_stats_accum(m, TW, pst1, pst2):
            sqb = nxt(SQB, "sqb")
            xb = nxt(XB, "xb")
            S.op("act", lambda e: e.activation(out=sqb.t[:, :TW], in_=R[m].t[:, :TW], func=AF.Square),
                 reads=[R[m]], writes=[sqb])
            S.op("act", lambda e: e.activation(out=xb.t[:, :TW], in_=R[m].t[:, :TW], func=AF.Copy),
                 reads=[R[m]], writes=[xb])

            def pe_part():
                S.op("pe", lambda e: e.matmul(pst1.t[:, :TW], lhsT=ONESB.t[:, :], rhs=xb.t[:, :TW],
                                              start=(m == 0), stop=(m == KC - 1)),
                     reads=[xb, ONESB], writes=[pst1], signal=False)
                S.op("pe", lambda e: e.matmul(pst2.t[:, :TW], lhsT=ONESB.t[:, :], rhs=sqb.t[:, :TW],
                                              start=(m == 0), stop=(m == KC - 1)),
                     reads=[sqb, ONESB], writes=[pst2])
            return pe_part

        def ln_finish(TW, pst1, pst2):
            S.op("act", lambda e: e.activation(out=MSQ.t[:, :TW], in_=pst1.t[:, :TW], func=AF.Square, scale=1.0 / D),
                 reads=[pst1], writes=[MSQ])
            S.op("dve", lambda e: e.scalar_tensor_tensor(out=RSTD.t[:, :TW], in0=pst2.t[:, :TW], scalar=1.0 / D,
                                                         in1=MSQ.t[:, :TW], op0=ALU.mult, op1=ALU.subtract),
                 reads=[pst2, MSQ], writes=[RSTD])
            S.op("act", lambda e: e.activation(out=RSTD.t[:, :TW], in_=RSTD.t[:, :TW], func=AF.Ln, bias=LN_EPS),
                 reads=[RSTD], writes=[RSTD])
            S.op("act", lambda e: e.activation(out=RSTD.t[:, :TW], in_=RSTD.t[:, :TW], func=AF.Exp, scale=-0.5),
                 reads=[RSTD], writes=[RSTD])
            S.op("dve", lambda e: e.scalar_tensor_tensor(out=NMR.t[:, :TW], in0=pst1.t[:, :TW], scalar=-1.0 / D,
                                                         in1=RSTD.t[:, :TW], op0=ALU.mult, op1=ALU.mult),
                 reads=[pst1, RSTD], writes=[NMR])

        def norm_mul(m, TW):
            S.op("dve", lambda e: e.tensor_tensor(out=R[m].t[:, :TW], in0=R[m].t[:, :TW], in1=RSTD.t[:, :TW],
                                                  op=ALU.mult), reads=[R[m], RSTD], writes=[R[m]])

        def norm_add(m, TW):
            S.op("dve", lambda e: e.tensor_tensor(out=R[m].t[:, :TW], in0=R[m].t[:, :TW], in1=NMR.t[:, :TW],
                                                  op=ALU.add), reads=[R[m], NMR], writes=[R[m]])

        def stage2(T):
            TW = T.TW
            pst1, pst2 = P[4], P[5]
            pend = None
            for b in range(8):
                slot = wblock(w_o, 0, KC, b * 256, 256)
                for mmi in range(2):
                    m = b * 2 + mmi
                    bk = bank("PR")
                    pairs = [(slot.t[:, kc * 256 + mmi * 128: kc * 256 + (mmi + 1) * 128], MIX[kc].t[:, :TW])
                             for kc in range(KC)]
                    mm_group(bk.t[:, :TW], bk, pairs, reads=[slot] + MIX)
                    if pend is not None:
                        pend()
                        pend = None
                    for (c0, n, sidx) in T.segs:
                        S.op("dve", (lambda bk=bk, c0=c0, n=n, sidx=sidx, m=m: (lambda e: e.scalar_tensor_tensor(
                            out=R[m].t[:, c0:c0 + n], in0=bk.t[:, c0:c0 + n], scalar=G1(sidx, m),
                            in1=R[m].t[:, c0:c0 + n], op0=ALU.mult, op1=ALU.add)))(),
                             reads=[bk, MOD, R[m]], writes=[R[m]])
                    if pend is not None:
                        pend()
                    pend = ln_stats_accum(m, TW, pst1, pst2)
            pend()
            ln_finish(TW, pst1, pst2)
            def ax1(m):
                S.op("act", lambda e: e.activation(
                    out=R[m].t[:, :TW], in_=R[m].t[:, :TW], func=AF.Identity, scale=A1.t[:, m:m + 1],
                    bias=B1.t[:, m:m + 1]), reads=[R[m], A1, B1], writes=[R[m]])

            norm_mul(0, TW)
            for m in range(KC):
                if m + 1 < KC:
                    norm_mul(m + 1, TW)
                norm_add(m, TW)
                for (c0, n, sidx) in T.segs:
                    S.op("act", (lambda m=m, c0=c0, n=n, sidx=sidx: (lambda e: e.activation(
                        out=MIX[m].t[:, c0:c0 + n], in_=R[m].t[:, c0:c0 + n], func=AF.Identity,
                        bias=BA2.t[:, sidx, m:m + 1], scale=GA2.t[:, sidx, m:m + 1])))(),
                         reads=[R[m], BA2, GA2], writes=[MIX[m]])
                if m >= 1:
                    ax1(m - 1)
            ax1(KC - 1)

        def stage3(T):
            TW, SW = T.TW, T.SW
            alias_barrier(GLA_TMPS, ACTT)
            pst1, pst2 = P[4], P[5]
            for b in range(22):
                sg_ = wblock(w_gate, 0, KC, b * 256, 256)
                su_ = wblock(w_up, 0, KC, b * 256, 256)
                for mmi in range(2):
                    j = b * 2 + mmi
                    bg = bank("PR")
                    bu = bank("PR")
                    mm_group(bg.t[:, :TW], bg, [(sg_.t[:, kc * 256 + mmi * 128: kc * 256 + (mmi + 1) * 128],
                                                 MIX[kc].t[:, :TW]) for kc in range(KC)], reads=[sg_] + MIX)
                    mm_group(bu.t[:, :TW], bu, [(su_.t[:, kc * 256 + mmi * 128: kc * 256 + (mmi + 1) * 128],
                                                 MIX[kc].t[:, :TW]) for kc in range(KC)], reads=[su_] + MIX)
                    sgt = nxt(SGT, "sgt")
                    S.op("act", (lambda bg=bg, sgt=sgt: (lambda e: e.activation(out=sgt.t[:, :TW], in_=bg.t[:, :TW],
                                                                                func=AF.Silu)))(),
                         reads=[bg], writes=[sgt])
                    S.op("dve", (lambda bu=bu, sgt=sgt, j=j: (lambda e: e.tensor_tensor(
                        out=ACTT[j].t[:, :TW], in0=bu.t[:, :TW], in1=sgt.t[:, :TW], op=ALU.mult)))(),
                         reads=[bu, sgt], writes=[ACTT[j]])
            pend = None
            for m in range(KC):
                bk = bank("PR")
                pairs = []
                slots = []
                for half in range(2):
                    slot = wblock(w_down, half * 22, 22, m * 128, 128)
                    slots.append(slot)
                    pairs += [(slot.t[:, kk * 128:(kk + 1) * 128], ACTT[half * 22 + kk].t[:, :TW]) for kk in range(22)]
                mm_group(bk.t[:, :TW], bk, pairs, reads=slots + ACTT)
                if pend is not None:
                    pend()
                    pend = None
                for (c0, n, sidx) in T.segs:
                    S.op("dve", (lambda bk=bk, c0=c0, n=n, sidx=sidx, m=m: (lambda e: e.scalar_tensor_tensor(
                        out=R[m].t[:, c0:c0 + n], in0=bk.t[:, c0:c0 + n], scalar=G2M(sidx, m),
                        in1=R[m].t[:, c0:c0 + n], op0=ALU.mult, op1=ALU.add)))(),
                         reads=[bk, MOD, R[m]], writes=[R[m]])
                pend = ln_stats_accum(m, TW, pst1, pst2)
            pend()
            ln_finish(TW, pst1, pst2)
            norm_mul(0, TW)
            for m in range(KC):
                if m + 1 < KC:
                    norm_mul(m + 1, TW)
                norm_add(m, TW)
                S.op("act", (lambda m=m: (lambda e: e.activation(
                    out=R[m].t[:, :TW], in_=R[m].t[:, :TW], func=AF.Identity,
                    bias=PV.t[:, PV_LN2B + m:PV_LN2B + m + 1], scale=PV.t[:, PV_LN2G + m:PV_LN2G + m + 1])))(),
                     reads=[R[m], PV], writes=[R[m]])
            for s in range(T.NSUB):
                osb = OS[os_i["i"] % 2]
                key = f"d_os{os_i['i'] % 2}"
                os_i["i"] += 1
                for g in range(4):
                    bk = bank("ALL")
                    for j in range(4):
                        kc = g * 4 + j
                        S.op("pe", (lambda bk=bk, j=j, kc=kc, s=s: (lambda e: e.transpose(
                            out=bk.t[:SW, j * 128:(j + 1) * 128], in_=R[kc].t[:, s * SW:(s + 1) * SW],
                            identity=ident(128))))(), reads=[R[kc], CST], writes=[bk], signal=(j == 3))
                    if g % 2 == 0:
                        S.op("act", (lambda bk=bk, g=g, osb=osb: (lambda e: e.activation(
                            out=osb.t[:SW, g * 512:(g + 1) * 512], in_=bk.t[:SW, :512], func=AF.Copy)))(),
                             reads=[bk], writes=[osb])
                    else:
                        S.op("dve", (lambda bk=bk, g=g, osb=osb: (lambda e: e.tensor_copy(
                            out=osb.t[:SW, g * 512:(g + 1) * 512], in_=bk.t[:SW, :512])))(),
                             reads=[bk], writes=[osb])
                dst = T.y[T.r0 + s * SW:T.r0 + (s + 1) * SW, :]
                tok = S.dma("sp", key, (lambda osb=osb, dst=dst: (lambda e: e.dma_start(out=dst, in_=osb.t[:SW, :])))(),
                            reads=[osb])
                S.out_tokens.append(tok)

        def make_tile(kind, x, y, r0, TW, SW, segs, chunks, maskA, maskC):
            T = TileDesc()
            T.tid = (kind, r0)
            T.kind, T.x, T.y, T.r0, T.TW, T.SW = kind, x, y, r0, TW, SW
            T.NSUB = TW // SW
            T.segs, T.chunks, T.maskA, T.maskC = segs, chunks, maskA, maskC
            T.slot = {0: 0, 1: 0, 2: 1}
            T.rhs = lambda kc: H[kc].t[:, :TW]
            T.rhs_bufs = list(H)
            return T

        mAp = CST.t[:, CS_MAP:CS_MAP + 128]
        mCp = CST.t[:, CS_MCP:CS_MCP + 128]
        mAs = CST.t[:64, CS_MAS:CS_MAS + 64]
        mCs = CST.t[:64, CS_MCS:CS_MCS + 64]

        main_tiles = [make_tile("main", xm, yp, t * TWP, TWP, 128, [(0, TWP, 0)], [(0, 128, 0)], mAp, mCp)
                      for t in range(n_main)]
        pre_tiles = [make_tile("pre", xp, None, t * TWP, TWP, 128, [(0, TWP, 0)], [(0, 128, 0)], mAp, mCp)
                     for t in range(n_pre)]
        for t in range(n_pre):
            T = pre_tiles[t]
            stage0(T)
            mod_some(4)
            Tn = pre_tiles[t + 1] if t + 1 < n_pre else (main_tiles[0] if n_main > 0 else None)
            if t == n_pre - 1:
                for c in range(8):
                    bks = {}
                    for nm, cb in (("hc", C_HC), ("hin", C_HIN)):
                        slot = wblock(w_in, 0, KC, cb + c * 128, 128)
                        bk = bank("PR")
                        pairs = [(slot.t[:, kc * 128:(kc + 1) * 128], H[kc].t[:, TWP - 2:TWP]) for kc in range(KC)]
                        mm_group(bk.t[:, :2], bk, pairs, reads=[slot] + H)
                        bks[nm] = bk
                    hcs = nxt(HCS, "hcs")
                    S.op("act", (lambda hcs=hcs, bk=bks["hc"]: (lambda e: e.activation(
                        out=hcs.t[:, :2], in_=bk.t[:, :2], func=AF.Copy)))(), reads=[bks["hc"]], writes=[hcs])
                    S.op("dve", (lambda hcs=hcs, bk=bks["hin"], c=c: (lambda e: e.tensor_tensor(
                        out=UH[0].t[:, c, :], in0=bk.t[:, :2], in1=hcs.t[:, :2], op=ALU.mult)))(),
                         reads=[bks["hin"], hcs], writes=[UH[0]])
            if Tn is not None and dbg == 0:
                stage1(T, extra=stage0_groups(Tn))
                Tn.s0done = True
            else:
                stage1(T)
        mod_some(48)
        if n_pre > 0:
            fl = PV.t[:, PV_FLAG:PV_FLAG + 1]
            S.op("dve", lambda e: e.tensor_scalar(out=S32[0].t[:, :], in0=S32[0].t[:, :], scalar1=fl, scalar2=None,
                                                  op0=ALU.mult), reads=[S32[0], PV], writes=[S32[0]])
            S.op("dve", lambda e: e.tensor_scalar(out=UH[0].t[:, :, :], in0=UH[0].t[:, :, :], scalar1=fl, scalar2=None,
                                                  op0=ALU.mult), reads=[UH[0], PV], writes=[UH[0]])
            for hp in range(2):
                S.op("act", (lambda hp=hp: (lambda e: e.activation(
                    out=SBF[0][hp].t[:, :], in_=S32[0].t[:, hp * 512:(hp + 1) * 512], func=AF.Copy)))(),
                     reads=[S32[0]], writes=[SBF[0][hp]])

        samp_tile = make_tile("samp", xs, ys, 0, 64, 64, [(0, 32, 1), (32, 32, 2)], [(0, 32, 0), (32, 32, 1)],
                              mAs, mCs) if (do_sample and dbg != 2) else None
        for t in range(n_main):
            wstate["every"] = {0: 4, 1: 3, 2: 2}.get(t, 1) if n_main >= 4 else 1
            T = main_tiles[t]
            T.next = main_tiles[t + 1] if t + 1 < n_main else samp_tile
            stage0(T)
            if dbg == 5:
                continue
            stage1(T)
            if dbg == 1 and t == n_main - 1:
                for k in range(16):
                    S.out_tokens.append(S.dma("sp", f"d_dbg{k}", (lambda k=k: (lambda e: e.dma_start(
                        out=dbgmix[k], in_=MIX[k].t[:, :])))(), reads=[MIX[k]]))
                    S.out_tokens.append(S.dma("sp", f"d_dbh{k}", (lambda k=k: (lambda e: e.dma_start(
                        out=dbgh[k], in_=H[k].t[:, :])))(), reads=[H[k]]))
            if dbg == 2:
                continue
            stage2(T)
            if T.next is not None and dbg == 0:
                prefetch_x(T.next)
            stage3(T)
        tok = S.dma("sp", "d_o1", lambda e: e.dma_start(out=glp.rearrange("h d v -> d h v"),
                                                       in_=S32[0].t[:, :].rearrange("p (h v) -> p h v", h=4)),
                    reads=[S32[0]])
        S.out_tokens.append(tok)
        tok = S.dma("sp", "d_o2", lambda e: e.dma_start(out=cvp[:, :], in_=UH[0].t[:, :, :].rearrange("p c r -> p (c r)")),
                    reads=[UH[0]])
        S.out_tokens.append(tok)

        if do_sample and dbg != 2:
            for i in range(2):
                S.dma("sp", f"d_s{i}", (lambda i=i: (lambda e: e.dma_start(
                    out=S32[i].t[:, :].rearrange("p (h v) -> p h v", h=4),
                    in_=sg_d[i].rearrange("h d v -> d h v"))))(), writes=[S32[i]])
                S.dma("sp", f"d_u{i}", (lambda i=i: (lambda e: e.dma_start(
                    out=UH[i].t[:, :, :].rearrange("p c r -> p (c r)"), in_=cc_d[:, i * 16:(i + 1) * 16])))(),
                      writes=[UH[i]])
                for hp in range(2):
                    S.op("act", (lambda i=i, hp=hp: (lambda e: e.activation(
                        out=SBF[i][hp].t[:, :], in_=S32[i].t[:, hp * 512:(hp + 1) * 512], func=AF.Copy)))(),
                         reads=[S32[i]], writes=[SBF[i][hp]])
            T = samp_tile
            stage0(T)
            stage1(T)
            stage2(T)
            stage3(T)
            for i in range(2):
                tok = S.dma("sp", f"d_o3{i}", (lambda i=i: (lambda e: e.dma_start(
                    out=gls[i].rearrange("h d v -> d h v"),
                    in_=S32[i].t[:, :].rearrange("p (h v) -> p h v", h=4))))(), reads=[S32[i]])
                S.out_tokens.append(tok)
                tok = S.dma("sp", f"d_o4{i}", (lambda i=i: (lambda e: e.dma_start(
                    out=cvs[:, i * 16:(i + 1) * 16], in_=UH[i].t[:, :, :].rearrange("p c r -> p (c r)"))))(),
                            reads=[UH[i]])
                S.out_tokens.append(tok)

        S.wait_all("sp", S.out_tokens)

        for e in ENGS:
            for waits, fn, inc in S.q[e]:
                for k, v in waits:
                    S.sem(k)
                if inc is not None:
                    S.sem(inc[0])

        with nc.Block() as block:
            @block.tensor
            def _(h):
                S.replay("pe", h)

            @block.scalar
            def _(h):
                S.replay("act", h)

            @block.vector
            def _(h):
                S.replay("dve", h)

            @block.gpsimd
            def _(h):
                S.replay("pool", h)

            @block.sync
            def _(h):
                S.replay("sp", h)
    return nc


def _consts():
    c = np.zeros((128, CS_N), np.float32)
    c[:, CS_ID:CS_ID + 128] = np.eye(128, dtype=np.float32)
    j = np.arange(128)[:, None]
    i = np.arange(128)[None, :]
    ma = (j <= i).astype(np.float32)
    c[:, CS_MAP:CS_MAP + 128] = ma
    c[:, CS_MCP:CS_MCP + 128] = -ma / 16.0
    js = np.arange(64)[:, None]
    is_ = np.arange(64)[None, :]
    ms = ((js <= is_) & ((js // 32) == (is_ // 32))).astype(np.float32)
    c[:64, CS_MAS:CS_MAS + 64] = ms
    c[:64, CS_MCS:CS_MCS + 64] = -ms / 16.0
    return c


def _col(v):
    return np.ascontiguousarray(np.asarray(v, np.float32).reshape(-1, 128).T)


_NC_CACHE = {}


def run(inputs, n_main=8, n_pre=8, do_sample=True, seq_len=8192, trace=False):
    f = lambda k: np.asarray(inputs[k], np.float32)
    x_prompt, x_sample = f("x_prompt"), f("x_sample")
    c_prompt, c_sample = f("c_prompt"), f("c_sample")
    cache_conv, state_gla = f("cache_conv")[0], f("state_gla")[0]
    key = (n_main, n_pre, do_sample)
    if key not in _NC_CACHE:
        _NC_CACHE[key] = build(n_main, n_pre, do_sample)
    nc = _NC_CACHE[key]
    half = n_main * 512
    consts = _consts()
    w = {k: np.ascontiguousarray(f(k)[0]) for k in ("w_ada", "w_in", "w_o", "w_gate", "w_up", "w_down")}
    wf = np.ascontiguousarray(f("w_f")[0])
    bfv = np.ascontiguousarray(f("b_f")[0].reshape(1, 512))
    in_maps = []
    for core in range(8):
        b, hf = core // 2, core % 2
        pv = np.zeros((128, PV_N), np.float32)
        pv[:, PV_BADA:PV_BADA + 96] = _col(f("b_ada")[0])
        pv[:, PV_LN1G:PV_LN1G + 16] = _col(f("ln1_g")[0])
        pv[:, PV_LN1B:PV_LN1B + 16] = _col(f("ln1_b")[0])
        pv[:, PV_LN2G:PV_LN2G + 16] = _col(f("ln2_g")[0])
        pv[:, PV_LN2B:PV_LN2B + 16] = _col(f("ln2_b")[0])
        wc = f("w_conv")[0]
        for c in range(8):
            for i in range(3):
                pv[:, PV_WC + c * 3 + i] = wc[i, c * 128:(c + 1) * 128]
        pv[:, PV_GAIN:PV_GAIN + 2] = _col(f("gla_gain")[0])
        pv[:, PV_FLAG] = float(hf)
        cv = np.zeros((128, 48), np.float32)
        vecs = [c_prompt[b], c_sample[2 * core], c_sample[2 * core + 1]]
        for s, v in enumerate(vecs):
            cv[:, s::3] = _col(v)
        cc = np.zeros((128, 32), np.float32)
        for i in range(2):
            cci = cache_conv[2 * core + i]
            for c in range(8):
                for r in range(2):
                    cc[:, i * 16 + c * 2 + r] = cci[r, c * 128:(c + 1) * 128]
        m = {
            "xm": np.ascontiguousarray(x_prompt[b, hf * half:(hf + 1) * half]) if half > 0 else np.zeros((1, D), np.float32),
            "xp": np.ascontiguousarray(x_prompt[b, 0:max(n_pre * 512, 1)]),
            "xs": np.ascontiguousarray(x_sample[2 * core:2 * core + 2].reshape(64, D)),
            "cvec": cv, "pvec": pv, "consts": consts, "wf": wf, "bf": bfv, "cc": cc,
            "sgla": np.ascontiguousarray(state_gla[2 * core:2 * core + 2]),
        }
        m.update(w)
        in_maps.append(m)
    res = run_bass_kernel_spmd(nc, in_maps, core_ids=list(range(8)), trace=trace)
    rs = res.results
    B = x_prompt.shape[0]
    yp = np.zeros((B, 2 * half, D), np.float32)
    ys = np.zeros((16, 32, D), np.float32)
    conv_p = np.zeros((1, B, 2, 1024), np.float32)
    gla_p = np.zeros((1, B, 4, 128, 256), np.float32)
    conv_s = np.zeros((1, 16, 2, 1024), np.float32)
    gla_s = np.zeros((1, 16, 4, 128, 256), np.float32)

    def unconv(a):
        return np.ascontiguousarray(a.reshape(128, 8, 2).transpose(2, 1, 0).reshape(2, 1024))

    for core in range(8):
        b, hf = core // 2, core % 2
        r = rs[core]
        if half > 0:
            yp[b, hf * half:(hf + 1) * half] = r["yp"]
        ys[2 * core:2 * core + 2] = r["ys"].reshape(2, 32, D)
        if hf == 1:
            conv_p[0, b] = unconv(r["cvp"])
            gla_p[0, b] = r["glp"]
        for i in range(2):
            conv_s[0, 2 * core + i] = unconv(r["cvs"][:, i * 16:(i + 1) * 16])
            gla_s[0, 2 * core + i] = r["gls"][i]
    out = (yp, ys, conv_p, gla_p, conv_s, gla_s)
    if trace:
        return out, res
    return out


def kernel(**inputs):
    return run(inputs)
```

```python
from contextlib import ExitStack
import numpy as np
import concourse.bass as bass
import concourse.mybir as mybir
from concourse.bass_utils import run_bass_kernel_spmd

F32 = mybir.dt.float32
BF16 = mybir.dt.bfloat16
AF = mybir.ActivationFunctionType
ALU = mybir.AluOpType

D = 2048
KC = 16
DFF = 5632
FC = 44
ALPHA = 2.0 ** 0.25
LN_EPS = 1e-5
RMS_EPS = 1e-6
QSCALE_LN = float(np.log(128.0 ** -0.5))
C_HB, C_HC, C_HIN, C_Q, C_K, C_V, C_GO, C_FLR = 0, 1024, 2048, 3072, 3584, 4096, 5120, 6144
NWB = 4
WB_ELEMS = 4096

PV_BADA = 0
PV_LN1G = 96
PV_LN1B = 112
PV_LN2G = 128
PV_LN2B = 144
PV_WC = 160
PV_GAIN = 184
PV_FLAG = 186
PV_N = 187
CS_ID, CS_MAP, CS_MCP, CS_MAS, CS_MCS, CS_N = 0, 128, 256, 384, 448, 512

ENGS = ("pe", "act", "dve", "pool", "sp")
HP_ORDER = (0, 1)
FULL_BARRIER = False
EXP_SKIP = False
FORCE_SIGNAL = False
USE_WCACHE = True


class Buf:
    __slots__ = ("t", "lw", "rd", "name", "psum")

    def __init__(self, t, name="", psum=False):
        self.psum = psum
        self.t = t
        self.lw = None
        self.rd = {}
        self.name = name


class Sched:
    def __init__(self, nc, es):
        self.nc = nc
        self.es = es
        self.q = {e: [] for e in ENGS}
        self.cnt = {e: 0 for e in ENGS}
        self.pending = {e: False for e in ENGS}
        self.sems = {}
        self.waited = {e: {} for e in ENGS}
        self.dma_cnt = {}
        self.out_tokens = []

    def sem(self, key):
        if key not in self.sems:
            self.sems[key] = self.es.enter_context(self.nc.semaphore(key))
        return self.sems[key]

    def _deps(self, eng, reads, writes, extra):
        deps = {}

        def add(tok):
            if tok is None:
                return
            k, v = tok
            if deps.get(k, 0) < v:
                deps[k] = v

        for b in reads:
            add(b.lw)
            if b.psum:
                for k, v in b.rd.items():
                    if k != "e_" + eng:
                        add((k, v))
        for b in writes:
            add(b.lw)
            for k, v in b.rd.items():
                add((k, v))
        for t in extra:
            add(t)
        waits = []
        for k, v in deps.items():
            if k == "e_pe" and eng == "pe":
                continue
            if self.waited[eng].get(k, 0) >= v:
                continue
            self.waited[eng][k] = v
            waits.append((k, v))
        return waits

    def _register(self, tok, reads, writes):
        k, v = tok
        for b in reads:
            if b.rd.get(k, 0) < v:
                b.rd[k] = v
        for b in writes:
            b.lw = tok
            b.rd = {}

    def op(self, eng, fn, reads=(), writes=(), signal=True, extra=()):
        if FORCE_SIGNAL:
            signal = True
        waits = self._deps(eng, reads, writes, extra)
        k = "e_" + eng
        if signal:
            self.cnt[eng] += 1
            self.pending[eng] = False
            tok = (k, self.cnt[eng])
        else:
            self.pending[eng] = True
            tok = (k, self.cnt[eng] + 1)
        self.q[eng].append((waits, fn, (k, 1) if signal else None))
        self._register(tok, reads, writes)
        return tok

    def dma(self, eng, semkey, fn, reads=(), writes=(), extra=()):
        waits = self._deps(eng, reads, writes, extra)
        n = self.dma_cnt.get(semkey, 0) + 1
        self.dma_cnt[semkey] = n
        tok = (semkey, 16 * n)
        self.q[eng].append((waits, fn, (semkey, 16)))
        self._register(tok, reads, writes)
        return tok

    def wait_all(self, eng, toks):
        waits = self._deps(eng, (), (), toks)
        self.q[eng].append((waits, None, None))

    def replay(self, eng, h):
        for waits, fn, inc in self.q[eng]:
            for k, v in waits:
                h.wait_ge(self.sem(k), v)
            if fn is None:
                continue
            ins = fn(h)
            if inc is not None:
                ins.then_inc(self.sem(inc[0]), inc[1])


class TileDesc:
    pass


def build(n_main=8, n_pre=8, do_sample=True, dbg=False):
    TWP = 512
    nc = bass.Bass("TRN2", target_bir_lowering=False)
    NM = n_main * TWP
    NP = n_pre * TWP

    def din(name, shape, dt=F32):
        return nc.dram_tensor(name, list(shape), dt, kind="ExternalInput").ap()

    def dout(name, shape):
        return nc.dram_tensor(name, list(shape), F32, kind="ExternalOutput").ap()

    xm = din("xm", [max(NM, 1), D])
    xp = din("xp", [max(NP, 1), D])
    xs = din("xs", [64, D])
    cvec = din("cvec", [128, 48])
    pvec = din("pvec", [128, PV_N])
    consts = din("consts", [128, CS_N])
    wf_d = din("wf", [16, 512])
    bf_d = din("bf", [1, 512])
    cc_d = din("cc", [128, 32])
    sg_d = din("sgla", [2, 4, 128, 256])
    w_ada = din("w_ada", [D, 6 * D])
    w_in = din("w_in", [D, 6160])
    w_o = din("w_o", [D, D])
    w_gate = din("w_gate", [D, DFF])
    w_up = din("w_up", [D, DFF])
    w_down = din("w_down", [DFF, D])

    yp = dout("yp", [max(NM, 1), D])
    ys = dout("ys", [64, D])
    cvp = dout("cvp", [128, 16])
    glp = dout("glp", [4, 128, 256])
    cvs = dout("cvs", [128, 32])
    gls = dout("gls", [2, 4, 128, 256])
    if dbg:
        dbgmix = nc.dram_tensor("dbgmix", [16, 128, 512], BF16, kind="ExternalOutput").ap()
        dbgh = nc.dram_tensor("dbgh", [16, 128, 512], BF16, kind="ExternalOutput").ap()

    with ExitStack() as es:
        S = Sched(nc, es)

        def sbt(name, shape, dt):
            return es.enter_context(nc.sbuf_tensor(name, list(shape), dt))

        def sb(name, shape, dt):
            t = sbt(name, shape, dt)
            return Buf(t[tuple(slice(None) for _ in shape)], name)

        WB = [sb(f"wb{i}", [128, WB_ELEMS], BF16) for i in range(NWB)]
        XIO = [sb(f"xio{i}", [128, D], F32) for i in range(2)]
        XL = XIO
        OS = XIO
        XPF = sb("xpf", [128, D], F32)
        xpref = {}
        H = [sb(f"h{k}", [128, 512], BF16) for k in range(KC)]
        R = [sb(f"r{k}", [128, 512], F32) for k in range(KC)]
        MIX = [sb(f"mix{k}", [128, 512], BF16) for k in range(KC)]
        ARENA = sbt("arena", [128, FC * 512], BF16)

        def carve(name, off, n, dt=BF16, pat=None, **kw):
            ap = ARENA[:, off:off + n]
            if dt == F32:
                ap = ap.bitcast(F32)
            if pat is not None:
                ap = ap.rearrange(pat, **kw)
            return Buf(ap, name)

        ACTT = [carve(f"actt{j}", j * 512, 512) for j in range(FC)]
        EQ = carve("eq", 0, 4096, F32, "p (h n) -> p h n", h=4)
        EK = carve("ek", 4096, 4096, F32, "p (h n) -> p h n", h=4)
        QT = carve("qt", 8192, 2048, BF16, "p (h n) -> p h n", h=4)
        KT = carve("kt", 10240, 2048, BF16, "p (h n) -> p h n", h=4)
        VT = carve("vt", 12288, 4096, BF16, "p (s n) -> p s n", s=4)
        SG = carve("sg", 16384, 4096, BF16, "p (m n) -> p m n", m=8)
        FLRB = carve("flrb", 20480, 512)
        AM = [carve(f"am{i}", 20992 + i * 256, 256, BF16, "p (h n) -> p h n", h=2) for i in range(2)]
        KTOK = [carve(f"ktok{i}", 21504 + i * 256, 256) for i in range(2)]
        GLA_TMPS = [EQ, EK, QT, KT, VT, SG, FLRB] + AM + KTOK
        TMP32 = [sb(f"tmp32_{i}", [128, 512], F32) for i in range(4)]
        TMP16 = [sb(f"tmp16_{i}", [128, 512], BF16) for i in range(4)]
        E1 = LL = TN = SGT = HCS = CV = TMP32
        SQ = SQB = XB = TMP16
        DEC = sb("dec", [128, 8, 4], F32)
        RINV = [sb(f"rinv{i}", [128, 256], F32) for i in range(2)]
        UB = [sb(f"ub{i}", [128, 520], F32) for i in range(2)]
        RSTD = sb("rstd", [128, 512], F32)
        NMR = sb("nmr", [128, 512], F32)
        MSQ = NMR
        S32 = [sb(f"s32_{i}", [128, 1024], F32) for i in range(2)]
        SBF = [[sb(f"sbf_{i}_{hp}", [128, 512], BF16) for hp in range(2)] for i in range(2)]
        UH = [sb(f"uh{i}", [128, 8, 2], F32) for i in range(2)]
        CST = sb("cst", [128, CS_N], F32)
        IDB = sb("idb", [128, 128], BF16)
        ONESB = sb("onesb", [128, 128], BF16)
        PV = sb("pv", [128, PV_N], F32)
        CVEC = sb("cvecs", [128, 48], F32)
        SCB = sb("scb", [128, 48], BF16)
        WFB = sb("wfb", [16, 512], BF16)
        BFB = sb("bfb", [1, 512], BF16)
        MOD = sb("mod", [128, 3, 96], F32)
        SC1P = sb("sc1p", [128, 3, 16], F32)
        SC2P = sb("sc2p", [128, 3, 16], F32)
        GA2 = sb("ga2", [128, 3, 16], F32)
        BA2 = sb("ba2", [128, 3, 16], F32)
        A1 = sb("a1", [128, 16], F32)
        B1 = sb("b1", [128, 16], F32)

        def alias_barrier(src, dst):
            toks = {}
            for b in src:
                for tk in ([b.lw] if b.lw else []) + list(b.rd.items()):
                    if toks.get(tk[0], 0) < tk[1]:
                        toks[tk[0]] = tk[1]
            for b in dst:
                for k, v in toks.items():
                    if b.rd.get(k, 0) < v:
                        b.rd[k] = v

        def ps(name):
            t = es.enter_context(nc.psum_tensor(name, [128, 512], F32))
            return Buf(t[:, :], name, psum=True)

        P = [ps(f"p{i}") for i in range(8)]
        rr = {"PR": 0, "PG": 0, "ALL": 0}

        def bank(pool):
            live = modst["live"]
            if pool == "PR":
                b = P[rr["PR"] % 4]
            elif pool == "PG":
                b = P[4 + rr["PG"] % (3 if live else 4)]
            else:
                b = P[rr["ALL"] % (7 if live else 8)]
            rr[pool] += 1
            return b

        ident = lambda n: CST.t[:n, CS_ID:CS_ID + n]

        S.dma("sp", "d_cst", lambda e: e.dma_start(out=CST.t[:, :], in_=consts[:, :]), writes=[CST])
        S.dma("sp", "d_pv", lambda e: e.dma_start(out=PV.t[:, :], in_=pvec[:, :]), writes=[PV])
        S.dma("sp", "d_cv", lambda e: e.dma_start(out=CVEC.t[:, :], in_=cvec[:, :]), writes=[CVEC])
        S.dma("pool", "d_wf", lambda e: e.dma_start(out=WFB.t[:, :], in_=wf_d[:, :]), writes=[WFB])
        S.dma("pool", "d_bf", lambda e: e.dma_start(out=BFB.t[:, :], in_=bf_d[:, :]), writes=[BFB])
        S.op("act", lambda e: e.activation(out=IDB.t[:, :], in_=CST.t[:, CS_ID:CS_ID + 128], func=AF.Copy),
             reads=[CST], writes=[IDB])
        S.op("dve", lambda e: e.memset(ONESB.t[:, :], 1.0), writes=[ONESB])
        S.op("act", lambda e: e.activation(out=SCB.t[:, :], in_=CVEC.t[:, :], func=AF.Silu), reads=[CVEC], writes=[SCB])

        wstate = {"i": 0}
        NCACHE = 128
        wcache = nc.dram_tensor("wcache", [NCACHE, 128, WB_ELEMS], BF16, kind="Internal").ap()
        cache_ix = {}
        WNAMES = {id(w_in): "in", id(w_o): "o", id(w_gate): "g", id(w_up): "u", id(w_down): "d", id(w_ada): "ada"}

        def wblock(W, k0, nk, c0, ncols):
            si = wstate["i"] % NWB
            slot = WB[si]
            key = f"d_wb{si}"
            wstate["i"] += 1
            assert nk * ncols <= WB_ELEMS
            n = nk * ncols
            ck = (WNAMES[id(W)], k0, nk, c0, ncols)
            if USE_WCACHE and ck in cache_ix:
                ci, cbuf = cache_ix[ck]
                S.dma("pool", key, lambda e: e.dma_start(out=slot.t[:, 0:n], in_=wcache[ci, :, 0:n]),
                      reads=[cbuf], writes=[slot])
                return slot
            src = W[k0 * 128:(k0 + nk) * 128, c0:c0 + ncols].rearrange("(k p) c -> p k c", p=128)
            dst = slot.t[:, 0:n].rearrange("p (k c) -> p k c", k=nk)
            S.dma("pool", key, lambda e: e.dma_start(out=dst, in_=src), writes=[slot])
            wstate["new"] = wstate.get("new", 0) + 1
            if (USE_WCACHE and WNAMES[id(W)] != "ada" and len(cache_ix) < NCACHE
                    and wstate["new"] % wstate.get("every", 1) == 0):
                ci = len(cache_ix)
                cbuf = Buf(None, f"wc{ci}")
                cache_ix[ck] = (ci, cbuf)
                S.dma("sp", f"d_ws{si}", lambda e: e.dma_start(out=wcache[ci, :, 0:n], in_=slot.t[:, 0:n]),
                      reads=[slot], writes=[cbuf])
            return slot

        def mm_group(out_ap, outbuf, pairs, reads, tail_signal=True):
            n = len(pairs)
            tok = None
            for i, (l, r) in enumerate(pairs):
                tok = S.op("pe", (lambda l=l, r=r, i=i: (lambda e: e.matmul(out_ap, lhsT=l, rhs=r, start=(i == 0),
                                                                           stop=(i == n - 1))))(),
                           reads=reads if i == 0 else (), writes=[outbuf] if i == 0 else (),
                           signal=(i == n - 1) and tail_signal)
            S._register(tok, reads, [outbuf])
            return tok

        PM = P[7]
        modst = {"live": True}

        def mod_block(blk):
            slot = wblock(w_ada, 0, KC, blk * 256, 256)
            for mmi in range(2):
                m = blk * 2 + mmi
                pairs = [(slot.t[:, kc * 256 + mmi * 128: kc * 256 + (mmi + 1) * 128], SCB.t[:, kc * 3:kc * 3 + 3])
                         for kc in range(KC)]
                mm_group(PM.t[:, m * 3:m * 3 + 3], PM, pairs, reads=[slot, SCB])

        def mod_finish(lo, hi):
            for s in range(3):
                S.op("dve", (lambda s=s: (lambda e: e.tensor_tensor(
                    out=MOD.t[:, s, lo:hi], in0=PM.t[:, 0:288].rearrange("p (m s) -> p m s", s=3)[:, lo:hi, s],
                    in1=PV.t[:, PV_BADA + lo:PV_BADA + hi], op=ALU.add)))(), reads=[PM, PV], writes=[MOD])

        def mod_derive():
            S.op("dve", lambda e: e.tensor_scalar(out=SC2P.t[:, :, :], in0=MOD.t[:, :, 64:80], scalar1=1.0, scalar2=None,
                                                  op0=ALU.add), reads=[MOD], writes=[SC2P])
            for s in range(3):
                S.op("dve", (lambda s=s: (lambda e: e.tensor_tensor(out=GA2.t[:, s, :], in0=SC2P.t[:, s, :],
                                                                   in1=PV.t[:, PV_LN1G:PV_LN1G + 16], op=ALU.mult)))(),
                     reads=[SC2P, PV], writes=[GA2])
                S.op("dve", (lambda s=s: (lambda e: e.tensor_tensor(out=BA2.t[:, s, :], in0=SC2P.t[:, s, :],
                                                                   in1=PV.t[:, PV_LN1B:PV_LN1B + 16], op=ALU.mult)))(),
                     reads=[SC2P, PV], writes=[BA2])
                S.op("dve", (lambda s=s: (lambda e: e.tensor_tensor(out=BA2.t[:, s, :], in0=BA2.t[:, s, :],
                                                                   in1=MOD.t[:, s, 48:64], op=ALU.add)))(),
                     reads=[BA2, MOD], writes=[BA2])
            S.op("dve", lambda e: e.tensor_scalar(out=A1.t[:, :], in0=PV.t[:, PV_LN1G:PV_LN1G + 16], scalar1=ALPHA,
                                                  scalar2=None, op0=ALU.mult), reads=[PV], writes=[A1])
            S.op("dve", lambda e: e.tensor_scalar(out=B1.t[:, :], in0=PV.t[:, PV_LN1B:PV_LN1B + 16], scalar1=ALPHA,
                                                  scalar2=None, op0=ALU.mult), reads=[PV], writes=[B1])

        for blk in range(16):
            mod_block(blk)
        mod_finish(0, 32)
        S.op("dve", lambda e: e.tensor_scalar(out=SC1P.t[:, :, :], in0=MOD.t[:, :, 16:32], scalar1=1.0, scalar2=None,
                                              op0=ALU.add), reads=[MOD], writes=[SC1P])
        mod_next = {"blk": 16}

        def mod_some(n):
            for _ in range(n):
                if mod_next["blk"] < 48:
                    mod_block(mod_next["blk"])
                    mod_next["blk"] += 1
            if mod_next["blk"] == 48 and modst["live"]:
                mod_finish(32, 96)
                mod_derive()
                modst["live"] = False

        SH1 = lambda s, kc: MOD.t[:, s, kc:kc + 1]
        G1 = lambda s, m: MOD.t[:, s, 32 + m:33 + m]
        G2M = lambda s, m: MOD.t[:, s, 80 + m:81 + m]

        for i in range(2):
            S.op("dve", (lambda i=i: (lambda e: e.memset(S32[i].t[:, :], 0.0)))(), writes=[S32[i]])
            S.op("dve", (lambda i=i: (lambda e: e.memset(UH[i].t[:, :, :], 0.0)))(), writes=[UH[i]])
            for hp in range(2):
                S.op("dve", (lambda i=i, hp=hp: (lambda e: e.memset(SBF[i][hp].t[:, :], 0.0)))(), writes=[SBF[i][hp]])

        xl_i = {"i": 0}
        os_i = {"i": 0}
        rot = {}

        def nxt(lst, key):
            key = id(lst)
            i = rot.get(key, 0)
            rot[key] = i + 1
            return lst[i % len(lst)]

        def prefetch_x(T):
            SW = T.SW
            src = T.x[T.r0:T.r0 + SW, :]
            S.dma("sp", "d_xpf", lambda e: e.dma_start(out=XPF.t[:SW, :], in_=src), writes=[XPF])
            xpref[T.tid] = True

        def stage0_groups(T):
            TW, SW = T.TW, T.SW
            out = []
            for s in range(T.NSUB):
                holder = {}
                for g in range(4):
                    def grp(s=s, g=g, holder=holder):
                        if g == 0:
                            if s == 0 and xpref.get(T.tid):
                                holder["xl"] = XPF
                            else:
                                xl = XL[xl_i["i"] % 2]
                                key = f"d_xl{xl_i['i'] % 2}"
                                xl_i["i"] += 1
                                src = T.x[T.r0 + s * SW:T.r0 + (s + 1) * SW, :]
                                S.dma("sp", key, lambda e: e.dma_start(out=xl.t[:SW, :], in_=src), writes=[xl])
                                holder["xl"] = xl
                        xl = holder["xl"]
                        bk = bank("ALL")
                        for j in range(4):
                            kc = g * 4 + j
                            S.op("pe", (lambda j=j, kc=kc: (lambda e: e.transpose(
                                out=bk.t[:, j * SW:(j + 1) * SW], in_=xl.t[:SW, kc * 128:(kc + 1) * 128],
                                identity=ident(SW))))(), reads=[xl, CST], writes=[bk], signal=(j == 3))
                        for j in range(4):
                            kc = g * 4 + j
                            for (c0, n, sidx) in T.segs:
                                lo, hi = max(c0, s * SW), min(c0 + n, (s + 1) * SW)
                                if lo >= hi:
                                    continue
                                src_ap = bk.t[:, j * SW + lo - s * SW: j * SW + hi - s * SW]
                                if g % 2 == 0:
                                    S.op("act", (lambda kc=kc, lo=lo, hi=hi, sidx=sidx, src_ap=src_ap: (
                                        lambda e: e.activation(
                                            out=H[kc].t[:, lo:hi], in_=src_ap, func=AF.Identity,
                                            bias=SH1(sidx, kc), scale=SC1P.t[:, sidx, kc:kc + 1])))(),
                                         reads=[bk, MOD, SC1P], writes=[H[kc]])
                                    if T.kind != "pre":
                                        S.op("act", (lambda kc=kc, lo=lo, hi=hi, src_ap=src_ap: (
                                            lambda e: e.activation(out=R[kc].t[:, lo:hi], in_=src_ap, func=AF.Copy,
                                                                   scale=ALPHA)))(), reads=[bk], writes=[R[kc]])
                                else:
                                    S.op("dve", (lambda kc=kc, lo=lo, hi=hi, sidx=sidx, src_ap=src_ap: (
                                        lambda e: e.tensor_scalar(
                                            out=H[kc].t[:, lo:hi], in0=src_ap, scalar1=SC1P.t[:, sidx, kc:kc + 1],
                                            scalar2=SH1(sidx, kc), op0=ALU.mult, op1=ALU.add)))(),
                                         reads=[bk, MOD, SC1P], writes=[H[kc]])
                                    if T.kind != "pre":
                                        S.op("dve", (lambda kc=kc, lo=lo, hi=hi, src_ap=src_ap: (
                                            lambda e: e.tensor_scalar(out=R[kc].t[:, lo:hi], in0=src_ap, scalar1=ALPHA,
                                                                      scalar2=None, op0=ALU.mult)))(),
                                             reads=[bk], writes=[R[kc]])
                    out.append(grp)
            return out

        def stage0(T):
            if getattr(T, "s0done", False):
                return
            for grp in stage0_groups(T):
                grp()
            T.s0done = True

        def proj_fm(W, c0, ncols, T, consume, mrows_last=None):
            TW = T.TW
            nb = (ncols + 255) // 256
            mi = 0
            for b in range(nb):
                bc = min(256, ncols - b * 256)
                slot = wblock(W, 0, KC, c0 + b * 256, bc)
                for mmi in range((bc + 127) // 128):
                    mr = min(128, bc - mmi * 128)
                    bk = bank("PR")
                    pairs = [(slot.t[:, kc * bc + mmi * 128: kc * bc + mmi * 128 + mr], T.rhs(kc)) for kc in range(KC)]
                    mm_group(bk.t[:mr, :TW], bk, pairs, reads=[slot] + T.rhs_bufs)
                    consume(mi, bk)
                    mi += 1

        def proj_groups(W, c0, ncols, T, consume):
            TW = T.TW
            out = []
            nb = (ncols + 255) // 256
            mi = 0
            for b in range(nb):
                bc = min(256, ncols - b * 256)
                holder = {}
                for mmi in range((bc + 127) // 128):
                    mr = min(128, bc - mmi * 128)

                    def g(b=b, bc=bc, mmi=mmi, mr=mr, mi=mi, holder=holder):
                        if mmi == 0:
                            holder["slot"] = wblock(W, 0, KC, c0 + b * 256, bc)
                        slot = holder["slot"]
                        bk = bank("PR")
                        pairs = [(slot.t[:, kc * bc + mmi * 128: kc * bc + mmi * 128 + mr], T.rhs(kc))
                                 for kc in range(KC)]
                        mm_group(bk.t[:mr, :TW], bk, pairs, reads=[slot] + T.rhs_bufs)
                        consume(mi, bk)
                    out.append(g)
                    mi += 1
            return out

        def stage1(T, extra=None):
            TW, SW = T.TW, T.SW
            main = T.kind != "pre"
            nch = len(T.chunks)
            NSUB = T.NSUB
            alias_barrier(ACTT, GLA_TMPS)
            st = {}

            def c_flr(mi, bk):
                S.op("act", lambda e: e.activation(out=FLRB.t[:16, :TW], in_=bk.t[:16, :TW], func=AF.Copy),
                     reads=[bk], writes=[FLRB])
            p_flr = proj_groups(w_in, C_FLR, 16, T, c_flr)[0]

            def p_z(s):
                pz = bank("PG")
                S.op("pe", lambda e: e.matmul(pz.t[:SW, :512], lhsT=FLRB.t[:16, s * SW:(s + 1) * SW],
                                              rhs=WFB.t[:16, :512], start=True, stop=False),
                     reads=[FLRB, WFB], writes=[pz], signal=False)
                S.op("pe", lambda e: e.matmul(pz.t[:SW, :512], lhsT=ONESB.t[0:1, :SW],
                                              rhs=BFB.t[0:1, :512], start=False, stop=True),
                     reads=[ONESB, BFB], writes=[pz])
                e1 = nxt(E1, "e1")
                ll = nxt(LL, "ll")
                S.op("act", lambda e: e.activation(out=e1.t[:SW, :], in_=pz.t[:SW, :512], func=AF.Exp, scale=-1.0),
                     reads=[pz], writes=[e1])
                S.op("act", lambda e: e.activation(out=ll.t[:SW, :], in_=e1.t[:SW, :], func=AF.Ln, bias=1.0),
                     reads=[e1], writes=[ll])
                st[("ll", s)] = ll

            def p_b(s):
                ll = st[("ll", s)]
                pb = bank("PG")
                for h in range(4):
                    S.op("pe", (lambda h=h: (lambda e: e.matmul(
                        pb.t[:, h * SW:(h + 1) * SW], lhsT=ll.t[:SW, h * 128:(h + 1) * 128],
                        rhs=T.maskC, start=True, stop=True)))(), reads=[ll, CST], writes=[pb], signal=(h == 3))
                pb3 = pb.t[:, 0:4 * SW].rearrange("p (h n) -> p h n", h=4)
                if main:
                    S.op("act", lambda e: e.activation(out=EQ.t[:, :, s * SW:(s + 1) * SW], in_=pb3, func=AF.Exp,
                                                       bias=QSCALE_LN), reads=[pb], writes=[EQ])
                S.op("act", lambda e: e.activation(out=EK.t[:, :, s * SW:(s + 1) * SW], in_=pb3, func=AF.Exp,
                                                   scale=-1.0), reads=[pb], writes=[EK])
                for ic, (c0, CL, sid) in enumerate(T.chunks):
                    ci = s * nch + ic
                    S.op("act", (lambda c0=c0, CL=CL, ci=ci: (lambda e: e.activation(
                        out=DEC.t[:, ci, :], in_=pb3[:, :, c0 + CL - 1], func=AF.Exp)))(), reads=[pb], writes=[DEC])

            def c_q(mi, bk):
                S.op("dve", lambda e: e.tensor_tensor(out=QT.t[:, mi, :TW], in0=bk.t[:, :TW], in1=EQ.t[:, mi, :TW],
                                                      op=ALU.mult), reads=[bk, EQ], writes=[QT])

            def c_k(mi, bk):
                S.op("dve", lambda e: e.tensor_tensor(out=KT.t[:, mi, :TW], in0=bk.t[:, :TW], in1=EK.t[:, mi, :TW],
                                                      op=ALU.mult), reads=[bk, EK], writes=[KT])

            def c_go(mi, bk):
                st[("go", mi)] = True
                S.op("act", lambda e: e.activation(out=SG.t[:, mi, :TW], in_=bk.t[:, :TW], func=AF.Silu),
                     reads=[bk], writes=[SG])

            f_k = proj_groups(w_in, C_K, 512, T, c_k)
            f_q = proj_groups(w_in, C_Q, 512, T, c_q) if main else []
            f_go = proj_groups(w_in, C_GO, 1024, T, c_go) if main else []

            f_v = []
            for b in range(4):
                holder = {}
                for s in range(NSUB):
                    def g(b=b, s=s, holder=holder):
                        if s == 0:
                            holder["slot"] = wblock(w_in, 0, KC, C_V + b * 256, 256)
                        slot = holder["slot"]
                        bk = bank("PR")
                        pairs = [(H[kc].t[:, s * SW:(s + 1) * SW], slot.t[:, kc * 256:(kc + 1) * 256])
                                 for kc in range(KC)]
                        mm_group(bk.t[:SW, :256], bk, pairs, reads=[slot] + H)
                        if s % 2 == 0:
                            S.op("act", lambda e: e.activation(out=VT.t[:SW, s, b * 256:(b + 1) * 256],
                                                               in_=bk.t[:SW, :256], func=AF.Copy),
                                 reads=[bk], writes=[VT])
                        else:
                            S.op("dve", lambda e: e.tensor_copy(out=VT.t[:SW, s, b * 256:(b + 1) * 256],
                                                                in_=bk.t[:SW, :256]), reads=[bk], writes=[VT])
                    f_v.append(g)

            f_conv = []
            if main:
                for c in range(8):
                    cst = {}
                    for nm, cb in (("hc", C_HC), ("hin", C_HIN), ("hb", C_HB)):
                        def g(c=c, nm=nm, cb=cb, cst=cst):
                            slot = wblock(w_in, 0, KC, cb + c * 128, 128)
                            bk = bank("PR")
                            pairs = [(slot.t[:, kc * 128:(kc + 1) * 128], T.rhs(kc)) for kc in range(KC)]
                            mm_group(bk.t[:, :TW], bk, pairs, reads=[slot] + T.rhs_bufs)
                            cst[nm] = bk
                            if nm == "hc":
                                hcs = nxt(HCS, "hcs")
                                cst["hcs"] = hcs
                                S.op("act", lambda e: e.activation(out=hcs.t[:, :TW], in_=bk.t[:, :TW], func=AF.Copy),
                                     reads=[bk], writes=[hcs])
                            elif nm == "hin":
                                hcs = cst["hcs"]
                                ub = nxt(UB, "ub")
                                cst["ub"] = ub
                                off = 0
                                for (c0, n, sidx) in T.segs:
                                    uv = ub.t[:, off:off + n + 2]
                                    off += n + 2
                                    uh = UH[T.slot[sidx]]
                                    S.op("dve", (lambda uv=uv, uh=uh: (lambda e: e.tensor_copy(
                                        out=uv[:, 0:2], in_=uh.t[:, c, :])))(), reads=[uh], writes=[ub])
                                    S.op("dve", (lambda uv=uv, c0=c0, n=n: (lambda e: e.tensor_tensor(
                                        out=uv[:, 2:2 + n], in0=bk.t[:, c0:c0 + n], in1=hcs.t[:, c0:c0 + n],
                                        op=ALU.mult)))(), reads=[bk, hcs], writes=[ub])
                                    S.op("dve", (lambda uv=uv, n=n, uh=uh: (lambda e: e.tensor_copy(
                                        out=uh.t[:, c, :], in_=uv[:, n:n + 2])))(), reads=[ub], writes=[uh])
                            else:
                                ub = cst["ub"]
                                cv = nxt(CV, "cv")
                                off = 0
                                for (c0, n, sidx) in T.segs:
                                    uv = ub.t[:, off:off + n + 2]
                                    off += n + 2
                                    S.op("act", (lambda uv=uv, c0=c0, n=n: (lambda e: e.activation(
                                        out=cv.t[:, c0:c0 + n], in_=uv[:, 0:n], func=AF.Identity,
                                        scale=PV.t[:, PV_WC + c * 3:PV_WC + c * 3 + 1])))(), reads=[ub, PV], writes=[cv])
                                    for i in (1, 2):
                                        S.op("dve", (lambda uv=uv, c0=c0, n=n, i=i: (lambda e: e.scalar_tensor_tensor(
                                            out=cv.t[:, c0:c0 + n], in0=uv[:, i:i + n],
                                            scalar=PV.t[:, PV_WC + c * 3 + i:PV_WC + c * 3 + i + 1],
                                            in1=cv.t[:, c0:c0 + n], op0=ALU.mult, op1=ALU.add)))(),
                                             reads=[ub, PV, cv], writes=[cv])
                                    S.op("dve", (lambda c0=c0, n=n: (lambda e: e.tensor_tensor(
                                        out=MIX[c].t[:, c0:c0 + n], in0=bk.t[:, c0:c0 + n], in1=cv.t[:, c0:c0 + n],
                                        op=ALU.mult)))(), reads=[bk, cv], writes=[MIX[c]])
                        f_conv.append(g)

            def s1(s, hp):
                g0 = P[4]
                g0bf = g0.t[:, :].bitcast(BF16)
                if main:
                    for hh in range(2):
                        h = 2 * hp + hh
                        S.op("pe", (lambda hh=hh, h=h: (lambda e: e.matmul(
                            g0.t[:SW, hh * SW:(hh + 1) * SW], lhsT=KT.t[:, h, s * SW:(s + 1) * SW],
                            rhs=QT.t[:, h, s * SW:(s + 1) * SW], start=True, stop=True)))(),
                             reads=[KT, QT], writes=[g0], signal=False)
                for hh in range(2):
                    h = 2 * hp + hh
                    S.op("pe", (lambda hh=hh, h=h: (lambda e: e.transpose(
                        out=g0bf[:SW, 512 + hh * 128: 512 + (hh + 1) * 128], in_=KT.t[:, h, s * SW:(s + 1) * SW],
                        identity=IDB.t[:, :])))(), reads=[KT, IDB], writes=[g0], signal=(hh == 1))
                ktok = nxt(KTOK, "ktok")
                st[("ktok", s, hp)] = ktok
                S.op("dve", lambda e: e.tensor_copy(out=ktok.t[:SW, :], in_=g0bf[:SW, 512:768]),
                     reads=[g0], writes=[ktok])
                if main:
                    am = nxt(AM, "am")
                    st[("am", s, hp)] = am
                    for hh in range(2):
                        S.op("dve", (lambda hh=hh: (lambda e: e.tensor_tensor(
                            out=am.t[:SW, hh, :SW], in0=g0.t[:SW, hh * SW:(hh + 1) * SW], in1=T.maskA,
                            op=ALU.mult)))(), reads=[g0, CST], writes=[am])

            def s2(s, hp):
                ktok = st[("ktok", s, hp)]
                if main:
                    am = st[("am", s, hp)]
                    g1 = P[5]
                    st[("g1", s, hp)] = g1
                    for hh in range(2):
                        h = 2 * hp + hh
                        for ee in range(2):
                            blk = (hh * 2 + ee) * SW
                            S.op("pe", (lambda blk=blk, h=h, ee=ee, hh=hh: (lambda e: e.matmul(
                                g1.t[:, blk:blk + SW], lhsT=VT.t[:SW, s, h * 256 + ee * 128: h * 256 + (ee + 1) * 128],
                                rhs=am.t[:SW, hh, :SW], start=True, stop=False, skip_group_check=True)))(),
                                 reads=[VT, am], writes=[g1], signal=False)
                            for ic, (c0, CL, sid) in enumerate(T.chunks):
                                last = (ic == nch - 1) and hh == 1 and ee == 1
                                S.op("pe", (lambda blk=blk, c0=c0, CL=CL, sid=sid, h=h, hh=hh, ee=ee, ic=ic:
                                            (lambda e: e.matmul(
                                                g1.t[:, blk + c0: blk + c0 + CL],
                                                lhsT=SBF[sid][hp].t[:, hh * 256 + ee * 128: hh * 256 + (ee + 1) * 128],
                                                rhs=QT.t[:, h, s * SW + c0: s * SW + c0 + CL], start=False,
                                                stop=(ic == nch - 1), skip_group_check=True)))(),
                                     reads=[SBF[sid][hp], QT], writes=[g1], signal=last)
                for ic, (c0, CL, sid) in enumerate(T.chunks):
                    ci = s * nch + ic
                    g2 = P[6]
                    for hh in range(2):
                        h = 2 * hp + hh
                        S.op("pe", (lambda hh=hh, h=h, c0=c0, CL=CL, g2=g2: (lambda e: e.matmul(
                            g2.t[:, hh * 256:(hh + 1) * 256], lhsT=ktok.t[c0:c0 + CL, hh * 128:(hh + 1) * 128],
                            rhs=VT.t[c0:c0 + CL, s, h * 256:(h + 1) * 256], start=True, stop=True)))(),
                             reads=[ktok, VT], writes=[g2], signal=(hh == 1))
                    for hh in range(2):
                        h = 2 * hp + hh
                        sl = S32[sid].t[:, h * 256:(h + 1) * 256]
                        S.op("dve", (lambda sl=sl, ci=ci, h=h: (lambda e: e.tensor_scalar(
                            out=sl, in0=sl, scalar1=DEC.t[:, ci, h:h + 1], scalar2=None, op0=ALU.mult)))(),
                             reads=[S32[sid], DEC], writes=[S32[sid]])
                        S.op("dve", (lambda sl=sl, ci=ci, h=h, g2=g2, hh=hh: (lambda e: e.scalar_tensor_tensor(
                            out=sl, in0=g2.t[:, hh * 256:(hh + 1) * 256], scalar=DEC.t[:, ci, h:h + 1], in1=sl,
                            op0=ALU.mult, op1=ALU.add)))(), reads=[g2, S32[sid], DEC], writes=[S32[sid]])
                    if main:
                        S.op("act", (lambda sid=sid: (lambda e: e.activation(
                            out=SBF[sid][hp].t[:, :], in_=S32[sid].t[:, hp * 512:(hp + 1) * 512], func=AF.Copy)))(),
                             reads=[S32[sid]], writes=[SBF[sid][hp]])
                if main:
                    sq = nxt(SQ, "sq")
                    st[("sq", s, hp)] = sq
                    S.op("act", lambda e: e.activation(out=sq.t[:, :4 * SW], in_=g1.t[:, :4 * SW], func=AF.Square),
                         reads=[g1], writes=[sq])

            def s3(s, hp):
                if not main:
                    return
                while not all(st.get(("go", (2 * hp + hh) * 2 + ee)) for hh in range(2) for ee in range(2)):
                    F(1)
                g1 = st[("g1", s, hp)]
                sq = st[("sq", s, hp)]
                g3 = P[7]
                for hh in range(2):
                    for ee in range(2):
                        S.op("pe", (lambda hh=hh, ee=ee: (lambda e: e.matmul(
                            g3.t[:, hh * SW:(hh + 1) * SW], lhsT=ONESB.t[:, :],
                            rhs=sq.t[:, (hh * 2 + ee) * SW:(hh * 2 + ee + 1) * SW], start=(ee == 0),
                            stop=(ee == 1))))(), reads=[sq, ONESB], writes=[g3], signal=(hh == 1 and ee == 1))
                rinv = nxt(RINV, "rinv")
                S.op("act", lambda e: e.activation(out=rinv.t[:, :2 * SW], in_=g3.t[:, :2 * SW], func=AF.Ln,
                                                   scale=1.0 / 256.0, bias=RMS_EPS), reads=[g3], writes=[rinv])
                S.op("act", lambda e: e.activation(out=rinv.t[:, :2 * SW], in_=rinv.t[:, :2 * SW], func=AF.Exp,
                                                   scale=-0.5), reads=[rinv], writes=[rinv])
                tn = nxt(TN, "tn")
                for hh in range(2):
                    h = 2 * hp + hh
                    for ee in range(2):
                        blk = (hh * 2 + ee) * SW
                        S.op("dve", (lambda blk=blk, ee=ee, hh=hh: (lambda e: e.scalar_tensor_tensor(
                            out=tn.t[:, blk:blk + SW], in0=g1.t[:, blk:blk + SW],
                            scalar=PV.t[:, PV_GAIN + ee:PV_GAIN + ee + 1],
                            in1=rinv.t[:, hh * SW:(hh + 1) * SW], op0=ALU.mult, op1=ALU.mult)))(),
                             reads=[g1, PV, rinv], writes=[tn])
                        mchunk = 8 + h * 2 + ee
                        S.op("dve", (lambda blk=blk, mchunk=mchunk, h=h, ee=ee: (lambda e: e.tensor_tensor(
                            out=MIX[mchunk].t[:, s * SW:(s + 1) * SW], in0=tn.t[:, blk:blk + SW],
                            in1=SG.t[:, h * 2 + ee, s * SW:(s + 1) * SW], op=ALU.mult)))(),
                             reads=[tn, SG], writes=[MIX[mchunk]])

            fill = []

            def F(n):
                for _ in range(n):
                    if fill:
                        fill.pop(0)()

            p_flr()
            fill.extend(f_v)
            F(2)
            p_z(0)
            F(2)
            for s in range(NSUB):
                if s + 1 < NSUB:
                    p_z(s + 1)
                p_b(s)
                F(2)
            F(len(fill))
            for g in f_k:
                g()
            for g in f_q:
                g()
            for g in f_go[:4]:
                g()
            fill.extend(f_go[4:])
            fill.extend(f_conv)
            if extra:
                fill.extend(extra)
            its = [(s, hp) for s in range(NSUB) for hp in HP_ORDER]
            nf = max(1, len(fill) // (2 * len(its) + 1)) if fill else 0
            F(nf)
            s1(*its[0])
            for i, it in enumerate(its):
                F(nf)
                s2(*it)
                if i + 1 < len(its):
                    s1(*its[i + 1])
                F(nf)
                s3(*it)
            F(len(fill))

        def ln_stats_accum(m, TW, pst1, pst2):
            sqb = nxt(SQB, "sqb")
            xb = nxt(XB, "xb")
            S.op("act", lambda e: e.activation(out=sqb.t[:, :TW], in_=R[m].t[:, :TW], func=AF.Square),
                 reads=[R[m]], writes=[sqb])
            S.op("act", lambda e: e.activation(out=xb.t[:, :TW], in_=R[m].t[:, :TW], func=AF.Copy),
                 reads=[R[m]], writes=[xb])

            def pe_part():
                S.op("pe", lambda e: e.matmul(pst1.t[:, :TW], lhsT=ONESB.t[:, :], rhs=xb.t[:, :TW],
                                              start=(m == 0), stop=(m == KC - 1)),
                     reads=[xb, ONESB], writes=[pst1], signal=False)
                S.op("pe", lambda e: e.matmul(pst2.t[:, :TW], lhsT=ONESB.t[:, :], rhs=sqb.t[:, :TW],
                                              start=(m == 0), stop=(m == KC - 1)),
                     reads=[sqb, ONESB], writes=[pst2])
            return pe_part

        def ln_finish(TW, pst1, pst2):
            S.op("act", lambda e: e.activation(out=MSQ.t[:, :TW], in_=pst1.t[:, :TW], func=AF.Square, scale=1.0 / D),
                 reads=[pst1], writes=[MSQ])
            S.op("dve", lambda e: e.scalar_tensor_tensor(out=RSTD.t[:, :TW], in0=pst2.t[:, :TW], scalar=1.0 / D,
                                                         in1=MSQ.t[:, :TW], op0=ALU.mult, op1=ALU.subtract),
                 reads=[pst2, MSQ], writes=[RSTD])
            S.op("act", lambda e: e.activation(out=RSTD.t[:, :TW], in_=RSTD.t[:, :TW], func=AF.Ln, bias=LN_EPS),
                 reads=[RSTD], writes=[RSTD])
            S.op("act", lambda e: e.activation(out=RSTD.t[:, :TW], in_=RSTD.t[:, :TW], func=AF.Exp, scale=-0.5),
                 reads=[RSTD], writes=[RSTD])

        def ln_nmr(TW, pst1):
            S.op("dve", lambda e: e.scalar_tensor_tensor(out=NMR.t[:, :TW], in0=pst1.t[:, :TW], scalar=-1.0 / D,
                                                         in1=RSTD.t[:, :TW], op0=ALU.mult, op1=ALU.mult),
                 reads=[pst1, RSTD], writes=[NMR])

        def norm_mul(m, TW):
            S.op("dve", lambda e: e.tensor_tensor(out=R[m].t[:, :TW], in0=R[m].t[:, :TW], in1=RSTD.t[:, :TW],
                                                  op=ALU.mult), reads=[R[m], RSTD], writes=[R[m]])

        def norm_add(m, TW):
            S.op("dve", lambda e: e.tensor_tensor(out=R[m].t[:, :TW], in0=R[m].t[:, :TW], in1=NMR.t[:, :TW],
                                                  op=ALU.add), reads=[R[m], NMR], writes=[R[m]])

        def stage2(T):
            TW = T.TW
            pst1, pst2 = P[4], P[5]
            pend = None
            for b in range(8):
                slot = wblock(w_o, 0, KC, b * 256, 256)
                for mmi in range(2):
                    m = b * 2 + mmi
                    bk = bank("PR")
                    pairs = [(slot.t[:, kc * 256 + mmi * 128: kc * 256 + (mmi + 1) * 128], MIX[kc].t[:, :TW])
                             for kc in range(KC)]
                    mm_group(bk.t[:, :TW], bk, pairs, reads=[slot] + MIX)
                    if pend is not None:
                        pend()
                        pend = None
                    for (c0, n, sidx) in T.segs:
                        S.op("dve", (lambda bk=bk, c0=c0, n=n, sidx=sidx, m=m: (lambda e: e.scalar_tensor_tensor(
                            out=R[m].t[:, c0:c0 + n], in0=bk.t[:, c0:c0 + n], scalar=G1(sidx, m),
                            in1=R[m].t[:, c0:c0 + n], op0=ALU.mult, op1=ALU.add)))(),
                             reads=[bk, MOD, R[m]], writes=[R[m]])
                    if pend is not None:
                        pend()
                    pend = ln_stats_accum(m, TW, pst1, pst2)
            pend()
            ln_finish(TW, pst1, pst2)
            def ax1(m):
                S.op("act", lambda e: e.activation(
                    out=R[m].t[:, :TW], in_=R[m].t[:, :TW], func=AF.Identity, scale=A1.t[:, m:m + 1],
                    bias=B1.t[:, m:m + 1]), reads=[R[m], A1, B1], writes=[R[m]])

            norm_mul(0, TW)
            norm_mul(1, TW)
            ln_nmr(TW, pst1)
            for m in range(KC):
                if 1 <= m and m + 1 < KC:
                    norm_mul(m + 1, TW)
                norm_add(m, TW)
                for (c0, n, sidx) in T.segs:
                    S.op("act", (lambda m=m, c0=c0, n=n, sidx=sidx: (lambda e: e.activation(
                        out=MIX[m].t[:, c0:c0 + n], in_=R[m].t[:, c0:c0 + n], func=AF.Identity,
                        bias=BA2.t[:, sidx, m:m + 1], scale=GA2.t[:, sidx, m:m + 1])))(),
                         reads=[R[m], BA2, GA2], writes=[MIX[m]])
            for m in range(KC):
                ax1(m)

        def stage3(T):
            TW, SW = T.TW, T.SW
            alias_barrier(GLA_TMPS, ACTT)
            pst1, pst2 = P[4], P[5]
            for b in range(22):
                sg_ = wblock(w_gate, 0, KC, b * 256, 256)
                su_ = wblock(w_up, 0, KC, b * 256, 256)
                for mmi in range(2):
                    j = b * 2 + mmi
                    bg = bank("PR")
                    bu = bank("PR")
                    mm_group(bg.t[:, :TW], bg, [(sg_.t[:, kc * 256 + mmi * 128: kc * 256 + (mmi + 1) * 128],
                                                 MIX[kc].t[:, :TW]) for kc in range(KC)], reads=[sg_] + MIX)
                    mm_group(bu.t[:, :TW], bu, [(su_.t[:, kc * 256 + mmi * 128: kc * 256 + (mmi + 1) * 128],
                                                 MIX[kc].t[:, :TW]) for kc in range(KC)], reads=[su_] + MIX)
                    sgt = nxt(SGT, "sgt")
                    S.op("act", (lambda bg=bg, sgt=sgt: (lambda e: e.activation(out=sgt.t[:, :TW], in_=bg.t[:, :TW],
                                                                                func=AF.Silu)))(),
                         reads=[bg], writes=[sgt])
                    S.op("dve", (lambda bu=bu, sgt=sgt, j=j: (lambda e: e.tensor_tensor(
                        out=ACTT[j].t[:, :TW], in0=bu.t[:, :TW], in1=sgt.t[:, :TW], op=ALU.mult)))(),
                         reads=[bu, sgt], writes=[ACTT[j]])
            pend = None
            for m in range(KC):
                bk = bank("PR")
                pairs = []
                slots = []
                for half in range(2):
                    slot = wblock(w_down, half * 22, 22, m * 128, 128)
                    slots.append(slot)
                    pairs += [(slot.t[:, kk * 128:(kk + 1) * 128], ACTT[half * 22 + kk].t[:, :TW]) for kk in range(22)]
                mm_group(bk.t[:, :TW], bk, pairs, reads=slots + ACTT)
                if pend is not None:
                    pend()
                    pend = None
                for (c0, n, sidx) in T.segs:
                    S.op("dve", (lambda bk=bk, c0=c0, n=n, sidx=sidx, m=m: (lambda e: e.scalar_tensor_tensor(
                        out=R[m].t[:, c0:c0 + n], in0=bk.t[:, c0:c0 + n], scalar=G2M(sidx, m),
                        in1=R[m].t[:, c0:c0 + n], op0=ALU.mult, op1=ALU.add)))(),
                         reads=[bk, MOD, R[m]], writes=[R[m]])
                pend = ln_stats_accum(m, TW, pst1, pst2)
            pend()
            ln_finish(TW, pst1, pst2)
            norm_mul(0, TW)
            norm_mul(1, TW)
            ln_nmr(TW, pst1)
            for m in range(KC):
                if 1 <= m and m + 1 < KC:
                    norm_mul(m + 1, TW)
                norm_add(m, TW)
                S.op("act", (lambda m=m: (lambda e: e.activation(
                    out=R[m].t[:, :TW], in_=R[m].t[:, :TW], func=AF.Identity,
                    bias=PV.t[:, PV_LN2B + m:PV_LN2B + m + 1], scale=PV.t[:, PV_LN2G + m:PV_LN2G + m + 1])))(),
                     reads=[R[m], PV], writes=[R[m]])
            for s in range(T.NSUB):
                osb = OS[os_i["i"] % 2]
                key = f"d_os{os_i['i'] % 2}"
                os_i["i"] += 1
                for g in range(4):
                    bk = bank("ALL")
                    for j in range(4):
                        kc = g * 4 + j
                        S.op("pe", (lambda bk=bk, j=j, kc=kc, s=s: (lambda e: e.transpose(
                            out=bk.t[:SW, j * 128:(j + 1) * 128], in_=R[kc].t[:, s * SW:(s + 1) * SW],
                            identity=ident(128))))(), reads=[R[kc], CST], writes=[bk], signal=(j == 3))
                    if g % 2 == 0:
                        S.op("act", (lambda bk=bk, g=g, osb=osb: (lambda e: e.activation(
                            out=osb.t[:SW, g * 512:(g + 1) * 512], in_=bk.t[:SW, :512], func=AF.Copy)))(),
                             reads=[bk], writes=[osb])
                    else:
                        S.op("dve", (lambda bk=bk, g=g, osb=osb: (lambda e: e.tensor_copy(
                            out=osb.t[:SW, g * 512:(g + 1) * 512], in_=bk.t[:SW, :512])))(),
                             reads=[bk], writes=[osb])
                dst = T.y[T.r0 + s * SW:T.r0 + (s + 1) * SW, :]
                tok = S.dma("sp", key, (lambda osb=osb, dst=dst: (lambda e: e.dma_start(out=dst, in_=osb.t[:SW, :])))(),
                            reads=[osb])
                S.out_tokens.append(tok)

        def make_tile(kind, x, y, r0, TW, SW, segs, chunks, maskA, maskC):
            T = TileDesc()
            T.tid = (kind, r0)
            T.kind, T.x, T.y, T.r0, T.TW, T.SW = kind, x, y, r0, TW, SW
            T.NSUB = TW // SW
            T.segs, T.chunks, T.maskA, T.maskC = segs, chunks, maskA, maskC
            T.slot = {0: 0, 1: 0, 2: 1}
            T.rhs = lambda kc: H[kc].t[:, :TW]
            T.rhs_bufs = list(H)
            return T

        mAp = CST.t[:, CS_MAP:CS_MAP + 128]
        mCp = CST.t[:, CS_MCP:CS_MCP + 128]
        mAs = CST.t[:64, CS_MAS:CS_MAS + 64]
        mCs = CST.t[:64, CS_MCS:CS_MCS + 64]

        main_tiles = [make_tile("main", xm, yp, t * TWP, TWP, 128, [(0, TWP, 0)], [(0, 128, 0)], mAp, mCp)
                      for t in range(n_main)]
        pre_tiles = [make_tile("pre", xp, None, t * TWP, TWP, 128, [(0, TWP, 0)], [(0, 128, 0)], mAp, mCp)
                     for t in range(n_pre)]
        for t in range(n_pre):
            T = pre_tiles[t]
            stage0(T)
            mod_some(4)
            Tn = pre_tiles[t + 1] if t + 1 < n_pre else (main_tiles[0] if n_main > 0 else None)
            if t == n_pre - 1:
                for c in range(8):
                    bks = {}
                    for nm, cb in (("hc", C_HC), ("hin", C_HIN)):
                        slot = wblock(w_in, 0, KC, cb + c * 128, 128)
                        bk = bank("PR")
                        pairs = [(slot.t[:, kc * 128:(kc + 1) * 128], H[kc].t[:, TWP - 2:TWP]) for kc in range(KC)]
                        mm_group(bk.t[:, :2], bk, pairs, reads=[slot] + H)
                        bks[nm] = bk
                    hcs = nxt(HCS, "hcs")
                    S.op("act", (lambda hcs=hcs, bk=bks["hc"]: (lambda e: e.activation(
                        out=hcs.t[:, :2], in_=bk.t[:, :2], func=AF.Copy)))(), reads=[bks["hc"]], writes=[hcs])
                    S.op("dve", (lambda hcs=hcs, bk=bks["hin"], c=c: (lambda e: e.tensor_tensor(
                        out=UH[0].t[:, c, :], in0=bk.t[:, :2], in1=hcs.t[:, :2], op=ALU.mult)))(),
                         reads=[bks["hin"], hcs], writes=[UH[0]])
            if Tn is not None and dbg == 0:
                stage1(T, extra=stage0_groups(Tn))
                Tn.s0done = True
            else:
                stage1(T)
        mod_some(48)
        if n_pre > 0:
            fl = PV.t[:, PV_FLAG:PV_FLAG + 1]
            S.op("dve", lambda e: e.tensor_scalar(out=S32[0].t[:, :], in0=S32[0].t[:, :], scalar1=fl, scalar2=None,
                                                  op0=ALU.mult), reads=[S32[0], PV], writes=[S32[0]])
            S.op("dve", lambda e: e.tensor_scalar(out=UH[0].t[:, :, :], in0=UH[0].t[:, :, :], scalar1=fl, scalar2=None,
                                                  op0=ALU.mult), reads=[UH[0], PV], writes=[UH[0]])
            for hp in range(2):
                S.op("act", (lambda hp=hp: (lambda e: e.activation(
                    out=SBF[0][hp].t[:, :], in_=S32[0].t[:, hp * 512:(hp + 1) * 512], func=AF.Copy)))(),
                     reads=[S32[0]], writes=[SBF[0][hp]])

        samp_tile = make_tile("samp", xs, ys, 0, 64, 64, [(0, 32, 1), (32, 32, 2)], [(0, 32, 0), (32, 32, 1)],
                              mAs, mCs) if (do_sample and dbg != 2) else None
        for t in range(n_main):
            wstate["every"] = {0: 4, 1: 3, 2: 2}.get(t, 1) if n_main >= 4 else 1
            T = main_tiles[t]
            T.next = main_tiles[t + 1] if t + 1 < n_main else samp_tile
            stage0(T)
            if dbg == 5:
                continue
            stage1(T)
            if dbg == 1 and t == n_main - 1:
                for k in range(16):
                    S.out_tokens.append(S.dma("sp", f"d_dbg{k}", (lambda k=k: (lambda e: e.dma_start(
                        out=dbgmix[k], in_=MIX[k].t[:, :])))(), reads=[MIX[k]]))
                    S.out_tokens.append(S.dma("sp", f"d_dbh{k}", (lambda k=k: (lambda e: e.dma_start(
                        out=dbgh[k], in_=H[k].t[:, :])))(), reads=[H[k]]))
            if dbg == 2:
                continue
            stage2(T)
            if T.next is not None and dbg == 0:
                prefetch_x(T.next)
            stage3(T)
        tok = S.dma("sp", "d_o1", lambda e: e.dma_start(out=glp.rearrange("h d v -> d h v"),
                                                       in_=S32[0].t[:, :].rearrange("p (h v) -> p h v", h=4)),
                    reads=[S32[0]])
        S.out_tokens.append(tok)
        tok = S.dma("sp", "d_o2", lambda e: e.dma_start(out=cvp[:, :], in_=UH[0].t[:, :, :].rearrange("p c r -> p (c r)")),
                    reads=[UH[0]])
        S.out_tokens.append(tok)

        if do_sample and dbg != 2:
            for i in range(2):
                S.dma("sp", f"d_s{i}", (lambda i=i: (lambda e: e.dma_start(
                    out=S32[i].t[:, :].rearrange("p (h v) -> p h v", h=4),
                    in_=sg_d[i].rearrange("h d v -> d h v"))))(), writes=[S32[i]])
                S.dma("sp", f"d_u{i}", (lambda i=i: (lambda e: e.dma_start(
                    out=UH[i].t[:, :, :].rearrange("p c r -> p (c r)"), in_=cc_d[:, i * 16:(i + 1) * 16])))(),
                      writes=[UH[i]])
                for hp in range(2):
                    S.op("act", (lambda i=i, hp=hp: (lambda e: e.activation(
                        out=SBF[i][hp].t[:, :], in_=S32[i].t[:, hp * 512:(hp + 1) * 512], func=AF.Copy)))(),
                         reads=[S32[i]], writes=[SBF[i][hp]])
            T = samp_tile
            stage0(T)
            stage1(T)
            stage2(T)
            stage3(T)
            for i in range(2):
                tok = S.dma("sp", f"d_o3{i}", (lambda i=i: (lambda e: e.dma_start(
                    out=gls[i].rearrange("h d v -> d h v"),
                    in_=S32[i].t[:, :].rearrange("p (h v) -> p h v", h=4))))(), reads=[S32[i]])
                S.out_tokens.append(tok)
                tok = S.dma("sp", f"d_o4{i}", (lambda i=i: (lambda e: e.dma_start(
                    out=cvs[:, i * 16:(i + 1) * 16], in_=UH[i].t[:, :, :].rearrange("p c r -> p (c r)"))))(),
                            reads=[UH[i]])
                S.out_tokens.append(tok)

        S.wait_all("sp", S.out_tokens)

        for e in ENGS:
            for waits, fn, inc in S.q[e]:
                for k, v in waits:
                    S.sem(k)
                if inc is not None:
                    S.sem(inc[0])

        with nc.Block() as block:
            @block.tensor
            def _(h):
                S.replay("pe", h)

            @block.scalar
            def _(h):
                S.replay("act", h)

            @block.vector
            def _(h):
                S.replay("dve", h)

            @block.gpsimd
            def _(h):
                S.replay("pool", h)

            @block.sync
            def _(h):
                S.replay("sp", h)
    return nc


def _consts():
    c = np.zeros((128, CS_N), np.float32)
    c[:, CS_ID:CS_ID + 128] = np.eye(128, dtype=np.float32)
    j = np.arange(128)[:, None]
    i = np.arange(128)[None, :]
    ma = (j <= i).astype(np.float32)
    c[:, CS_MAP:CS_MAP + 128] = ma
    c[:, CS_MCP:CS_MCP + 128] = -ma / 16.0
    js = np.arange(64)[:, None]
    is_ = np.arange(64)[None, :]
    ms = ((js <= is_) & ((js // 32) == (is_ // 32))).astype(np.float32)
    c[:64, CS_MAS:CS_MAS + 64] = ms
    c[:64, CS_MCS:CS_MCS + 64] = -ms / 16.0
    return c


def _col(v):
    return np.ascontiguousarray(np.asarray(v, np.float32).reshape(-1, 128).T)


_NC_CACHE = {}


def run(inputs, n_main=8, n_pre=8, do_sample=True, seq_len=8192, trace=False):
    f = lambda k: np.asarray(inputs[k], np.float32)
    x_prompt, x_sample = f("x_prompt"), f("x_sample")
    c_prompt, c_sample = f("c_prompt"), f("c_sample")
    cache_conv, state_gla = f("cache_conv")[0], f("state_gla")[0]
    key = (n_main, n_pre, do_sample)
    if key not in _NC_CACHE:
        _NC_CACHE[key] = build(n_main, n_pre, do_sample)
    nc = _NC_CACHE[key]
    half = n_main * 512
    consts = _consts()
    w = {k: np.ascontiguousarray(f(k)[0]) for k in ("w_ada", "w_in", "w_o", "w_gate", "w_up", "w_down")}
    wf = np.ascontiguousarray(f("w_f")[0])
    bfv = np.ascontiguousarray(f("b_f")[0].reshape(1, 512))
    in_maps = []
    for core in range(8):
        b, hf = core // 2, core % 2
        pv = np.zeros((128, PV_N), np.float32)
        pv[:, PV_BADA:PV_BADA + 96] = _col(f("b_ada")[0])
        pv[:, PV_LN1G:PV_LN1G + 16] = _col(f("ln1_g")[0])
        pv[:, PV_LN1B:PV_LN1B + 16] = _col(f("ln1_b")[0])
        pv[:, PV_LN2G:PV_LN2G + 16] = _col(f("ln2_g")[0])
        pv[:, PV_LN2B:PV_LN2B + 16] = _col(f("ln2_b")[0])
        wc = f("w_conv")[0]
        for c in range(8):
            for i in range(3):
                pv[:, PV_WC + c * 3 + i] = wc[i, c * 128:(c + 1) * 128]
        pv[:, PV_GAIN:PV_GAIN + 2] = _col(f("gla_gain")[0])
        pv[:, PV_FLAG] = float(hf)
        cv = np.zeros((128, 48), np.float32)
        vecs = [c_prompt[b], c_sample[2 * core], c_sample[2 * core + 1]]
        for s, v in enumerate(vecs):
            cv[:, s::3] = _col(v)
        cc = np.zeros((128, 32), np.float32)
        for i in range(2):
            cci = cache_conv[2 * core + i]
            for c in range(8):
                for r in range(2):
                    cc[:, i * 16 + c * 2 + r] = cci[r, c * 128:(c + 1) * 128]
        m = {
            "xm": np.ascontiguousarray(x_prompt[b, hf * half:(hf + 1) * half]) if half > 0 else np.zeros((1, D), np.float32),
            "xp": np.ascontiguousarray(x_prompt[b, 0:max(n_pre * 512, 1)]),
            "xs": np.ascontiguousarray(x_sample[2 * core:2 * core + 2].reshape(64, D)),
            "cvec": cv, "pvec": pv, "consts": consts, "wf": wf, "bf": bfv, "cc": cc,
            "sgla": np.ascontiguousarray(state_gla[2 * core:2 * core + 2]),
        }
        m.update(w)
        in_maps.append(m)
    res = run_bass_kernel_spmd(nc, in_maps, core_ids=list(range(8)), trace=trace)
    rs = res.results
    B = x_prompt.shape[0]
    yp = np.zeros((B, 2 * half, D), np.float32)
    ys = np.zeros((16, 32, D), np.float32)
    conv_p = np.zeros((1, B, 2, 1024), np.float32)
    gla_p = np.zeros((1, B, 4, 128, 256), np.float32)
    conv_s = np.zeros((1, 16, 2, 1024), np.float32)
    gla_s = np.zeros((1, 16, 4, 128, 256), np.float32)

    def unconv(a):
        return np.ascontiguousarray(a.reshape(128, 8, 2).transpose(2, 1, 0).reshape(2, 1024))

    for core in range(8):
        b, hf = core // 2, core % 2
        r = rs[core]
        if half > 0:
            yp[b, hf * half:(hf + 1) * half] = r["yp"]
        ys[2 * core:2 * core + 2] = r["ys"].reshape(2, 32, D)
        if hf == 1:
            conv_p[0, b] = unconv(r["cvp"])
            gla_p[0, b] = r["glp"]
        for i in range(2):
            conv_s[0, 2 * core + i] = unconv(r["cvs"][:, i * 16:(i + 1) * 16])
            gla_s[0, 2 * core + i] = r["gls"][i]
    out = (yp, ys, conv_p, gla_p, conv_s, gla_s)
    if trace:
        return out, res
    return out


def kernel(**inputs):
    return run(inputs)
```
